# Optimizing a Trainium2 kernel written in Bass

```python
import math
import jax, jax.numpy as jnp
from jax import lax
import numpy as np

D_MODEL = 2048
BATCH = 2
SEQ = 8192
DEPTH = 2
DEC_BATCH = 2
DEC_SEQ = 16384
PAST_LEN = 128

N_META = 16
N_MIXERS = 2
N_POOL_GROUPS = 4
POOL_WINDOWS = (2, 4, 8, 16)
POOL_GROUP_DIM = D_MODEL // N_POOL_GROUPS
N_HEADS = 8
HEAD_DIM = D_MODEL // (2 * N_HEADS)
V_HEAD_DIM = 2 * HEAD_DIM
_FF_RAW = -(-8 * D_MODEL // 3)
D_FF = -(-_FF_RAW // 256) * 256
Q_BLOCK = 128
RMS_EPS = 1e-6
N_POOL_LAYERS = (DEPTH + 1) // 2
N_ATTN_LAYERS = DEPTH // 2

kernel_name = 'hybrid_pool_diffattn_encoder'


def rmsnorm(x, g):
    xf = x.astype(jnp.float32)
    y = xf * lax.rsqrt(jnp.mean(xf * xf, axis=-1, keepdims=True) + RMS_EPS) * g.astype(jnp.float32)
    return y.astype(x.dtype)


def centred_mean(xf, w):
    L = xf.shape[1]
    c = jnp.concatenate([jnp.zeros_like(xf[:, :1]), jnp.cumsum(xf, axis=1)], axis=1)
    t = jnp.arange(L)
    lo = jnp.clip(t - w // 2, 0, L)
    hi = jnp.clip(t + w // 2, 0, L)
    s = jnp.take(c, hi, axis=1) - jnp.take(c, lo, axis=1)
    cnt = (hi - lo).astype(jnp.float32)
    return s / cnt[None, :, None]


def pool_mixer(h, pool_w, pool_scale):
    B, L, _ = h.shape
    hf = h.astype(jnp.float32).reshape(B, L, N_POOL_GROUPS, POOL_GROUP_DIM)
    pooled = jnp.stack([centred_mean(hf[:, :, g], POOL_WINDOWS[g]) for g in range(N_POOL_GROUPS)], axis=2) - hf
    out = jnp.einsum('blgc,gcd->blgd', pooled.astype(h.dtype), pool_w).reshape(B, L, D_MODEL)
    return out * pool_scale


def diff_attention(h, w_qkv, lq1, lk1, lq2, lk2, subln_g, w_o, lambda_init):
    B, L, _ = h.shape
    Lp = -(-L // Q_BLOCK) * Q_BLOCK
    nb = Lp // Q_BLOCK
    qkv = h @ w_qkv
    q, k, v = jnp.split(qkv, 3, axis=-1)
    q = q.reshape(B, L, N_HEADS, 2, HEAD_DIM)
    k = k.reshape(B, L, N_HEADS, 2, HEAD_DIM)
    v = v.reshape(B, L, N_HEADS, V_HEAD_DIM)
    q = jnp.pad(q, ((0, 0), (0, Lp - L), (0, 0), (0, 0), (0, 0)))
    qb = q.reshape(B, nb, Q_BLOCK, N_HEADS, 2, HEAD_DIM).transpose(1, 0, 2, 3, 4, 5)
    qpos = jnp.arange(Lp).reshape(nb, Q_BLOCK)
    kpos = jnp.arange(L)
    slopes = 2.0 ** (-8.0 * (jnp.arange(N_HEADS, dtype=jnp.float32) + 1.0) / N_HEADS)
    lam = (jnp.exp(jnp.sum(lq1.astype(jnp.float32) * lk1.astype(jnp.float32)))
           - jnp.exp(jnp.sum(lq2.astype(jnp.float32) * lk2.astype(jnp.float32))) + lambda_init)
    scale = HEAD_DIM ** -0.5

    def block(args):
        qblk, qp = args
        s = jnp.einsum('bqhjd,bkhjd->bhjqk', qblk, k, preferred_element_type=jnp.float32) * scale
        dist = jnp.abs(qp[:, None] - kpos[None, :]).astype(jnp.float32)
        s = s - slopes[None, :, None, None, None] * dist[None, None, None]
        p = jax.nn.softmax(s, axis=-1)
        a = p[:, :, 0] - lam * p[:, :, 1]
        return jnp.einsum('bhqk,bkhe->bqhe', a.astype(v.dtype), v)

    o = lax.map(block, (qb, qpos))
    o = o.transpose(1, 0, 2, 3, 4).reshape(B, Lp, N_HEADS, V_HEAD_DIM)[:, :L]
    o = rmsnorm(o, subln_g) * (1.0 - lambda_init)
    return o.reshape(B, L, D_MODEL) @ w_o


def swiglu(h, w_gate, w_up, w_down):
    return (jax.nn.silu(h @ w_gate) * (h @ w_up)) @ w_down


def setup_inputs(seed: int = 0) -> dict:
    key = jax.random.key(seed)
    ks = jax.random.split(key, 24)
    f32 = jnp.float32
    D, G, GD = D_MODEL, N_POOL_GROUPS, POOL_GROUP_DIM
    nrm = lambda k, shape, s: jax.random.normal(k, shape, f32) * s
    return {
        'x_prompt': nrm(ks[0], (BATCH, SEQ, D), 1.0),
        'x_sample': nrm(ks[1], (DEC_BATCH, DEC_SEQ, D), 1.0),
        'meta_tokens': nrm(ks[2], (N_META, D), 1.0),
        'mixer_norm_g': 1.0 + nrm(ks[3], (DEPTH, D), 0.05),
        'pool_w': nrm(ks[4], (N_POOL_LAYERS, G, GD, GD), GD ** -0.5),
        'pool_scale': 1.0 + nrm(ks[5], (N_POOL_LAYERS, D), 0.1),
        'w_qkv': nrm(ks[6], (N_ATTN_LAYERS, D, 3 * D), D ** -0.5),
        'lambda_q1': nrm(ks[7], (N_ATTN_LAYERS, HEAD_DIM), 0.1),
        'lambda_k1': nrm(ks[8], (N_ATTN_LAYERS, HEAD_DIM), 0.1),
        'lambda_q2': nrm(ks[9], (N_ATTN_LAYERS, HEAD_DIM), 0.1),
        'lambda_k2': nrm(ks[10], (N_ATTN_LAYERS, HEAD_DIM), 0.1),
        'subln_g': 1.0 + nrm(ks[11], (N_ATTN_LAYERS, V_HEAD_DIM), 0.05),
        'w_o': nrm(ks[12], (N_ATTN_LAYERS, D, D), D ** -0.5),
        'ffn_norm_g': 1.0 + nrm(ks[13], (DEPTH, D), 0.05),
        'w_gate': nrm(ks[14], (DEPTH, D, D_FF), D ** -0.5),
        'w_up': nrm(ks[15], (DEPTH, D, D_FF), D ** -0.5),
        'w_down': nrm(ks[16], (DEPTH, D_FF, D), D_FF ** -0.5),
        'final_norm_g': 1.0 + nrm(ks[17], (D,), 0.05),
    }


def reference(x_prompt, x_sample, meta_tokens, mixer_norm_g, pool_w, pool_scale, w_qkv,
              lambda_q1, lambda_k1, lambda_q2, lambda_k2, subln_g, w_o, ffn_norm_g,
              w_gate, w_up, w_down, final_norm_g):
    def encode(x):
        B = x.shape[0]
        meta = jnp.broadcast_to(meta_tokens[None].astype(x.dtype), (B, N_META, D_MODEL))
        h = jnp.concatenate([meta, x], axis=1)
        for i in range(DEPTH):
            j = i // N_MIXERS
            hn = rmsnorm(h, mixer_norm_g[i])
            if i % N_MIXERS == 0:
                h = h + pool_mixer(hn, pool_w[j], pool_scale[j])
            else:
                lambda_init = 0.8 - 0.6 * math.exp(-0.3 * i)
                h = h + diff_attention(hn, w_qkv[j], lambda_q1[j], lambda_k1[j], lambda_q2[j],
                                       lambda_k2[j], subln_g[j], w_o[j], lambda_init)
            h = h + swiglu(rmsnorm(h, ffn_norm_g[i]), w_gate[i], w_up[i], w_down[i])
        h = rmsnorm(h, final_norm_g)
        return h[:, N_META:]

    y_prompt = encode(x_prompt)
    y_sample = encode(x_sample)
    return (y_prompt, y_sample)
```

```python
import math
from contextlib import ExitStack
import numpy as np
import ml_dtypes
import concourse.bass as bass
import concourse.mybir as mybir
from concourse.bass_utils import run_bass_kernel_spmd

F32 = mybir.dt.float32
BF16 = mybir.dt.bfloat16
ALU = mybir.AluOpType
AF = mybir.ActivationFunctionType
AX = mybir.AxisListType
NPBF = ml_dtypes.bfloat16

N_META = 16
WINDOWS = (2, 4, 8, 16)
EPS = 1e-6
NEG = -30000.0


class Cfg:
    def __init__(self, D=2048, DFF=5632, NT=32, NQ=16):
        self.D, self.DFF, self.NT, self.NQ = D, DFF, NT, NQ
        self.KC = D // 128
        self.FC = DFF // 128
        self.H = D // 256
        self.GD = D // 4
        self.CG = self.GD // 128
        self.NS = NT + 1
        self.NKB = 4 * NT + 1
        self.OC = min(D, 1024)
        self.NPASS = D // self.OC
        self.NOG = self.OC // 512 if self.OC >= 512 else 1
        self.OW = min(512, self.OC)


class Buf:
    def __init__(self, name):
        self.name = name
        self.w = None
        self.r = {}


class Tracker:
    ENG = ("pe", "act", "dve", "pool", "sp")

    def __init__(self):
        self.stream = {e: [] for e in self.ENG}
        self.cnt = {e: 0 for e in self.ENG}
        self.waited = {e: {} for e in self.ENG}
        self.dma_cnt = {}
        self.semkeys = set("prog_" + e for e in ("pe", "act", "dve", "pool"))

    def _deps(self, reads, writes):
        deps = {}

        def add(tok):
            if tok is None:
                return
            k, v = tok
            if deps.get(k, 0) < v:
                deps[k] = v
        for b in reads:
            add(b.w)
        for b in writes:
            add(b.w)
            for k, v in b.r.items():
                add((k, v))
        return deps

    def _waits(self, e, deps):
        ws = []
        for k, v in deps.items():
            if e == "pe" and k == "prog_pe":
                continue
            if self.waited[e].get(k, 0) >= v:
                continue
            self.waited[e][k] = v
            ws.append((k, v))
        return ws

    def op(self, e, fn, reads=(), writes=(), inc=True):
        deps = self._deps(reads, writes)
        ws = self._waits(e, deps)
        key = "prog_" + e
        if inc:
            self.cnt[e] += 1
            tok = (key, self.cnt[e])
            incs = [(key, 1)]
        else:
            tok = (key, self.cnt[e] + 1)
            incs = []
        self.stream[e].append((ws, fn, incs))
        for b in reads:
            if b.r.get(key, 0) < tok[1]:
                b.r[key] = tok[1]
        for b in writes:
            b.w = tok
            b.r = {}
        return tok

    def dma(self, q, fn, semname, reads=(), writes=()):
        deps = self._deps(reads, writes)
        ws = self._waits(q, deps)
        self.semkeys.add(semname)
        self.dma_cnt[semname] = self.dma_cnt.get(semname, 0) + 1
        tok = (semname, 16 * self.dma_cnt[semname])
        self.stream[q].append((ws, fn, [(semname, 16)]))
        for b in reads:
            if b.r.get(semname, 0) < tok[1]:
                b.r[semname] = tok[1]
        for b in writes:
            b.w = tok
            b.r = {}
        return tok

    def barrier(self):
        allv = {"prog_" + e: self.cnt[e] for e in ("pe", "act", "dve", "pool") if self.cnt[e] > 0}
        for k, c in self.dma_cnt.items():
            allv[k] = 16 * c
        for e in self.ENG:
            ws = self._waits(e, allv)
            if ws:
                self.stream[e].append((ws, None, []))

    def replay(self, e, eng, sems):
        for ws, fn, incs in self.stream[e]:
            for k, v in ws:
                eng.wait_ge(sems[k], v)
            if fn is None:
                continue
            ins = fn(eng)
            for k, v in incs:
                ins = ins.then_inc(sems[k], v)


def build(cfg):
    D, DFF, NT, NQ = cfg.D, cfg.DFF, cfg.NT, cfg.NQ
    KC, FC, H, GD, CG, NS, NKB = cfg.KC, cfg.FC, cfg.H, cfg.GD, cfg.CG, cfg.NS, cfg.NKB
    OC, NPASS, NOG, OW = cfg.OC, cfg.NPASS, cfg.NOG, cfg.OW
    scale = 128 ** -0.5
    lam_init = 0.8 - 0.6 * math.exp(-0.3 * 1)

    nc = bass.Bass("TRN2", target_bir_lowering=False)

    def din(name, shape, dt=F32):
        return nc.dram_tensor(name, list(shape), dt, kind="ExternalInput").ap()

    def dscr(name, shape, dt):
        return nc.dram_tensor(name, list(shape), dt, kind="Internal").ap()

    xt_d = din("xt", [NS * 512, D])
    halo_d = din("halo", [NS * 128, D])
    bandm_d = din("bandm", [NS * 128, 4 * 4 * 128], BF16)
    bandh_d = din("bandh", [NS * 64, 2 * 4 * 128], BF16)
    tabs_d = din("tabs", [NQ * 128, 3 * NKB])
    base0_d = din("base0", [128, 512])
    w0_d = din("w0", [128, 896])
    ident_d = din("ident", [128, 128], BF16)
    ones_d = din("ones", [128, 128], BF16)
    gcols_d = din("gcols", [128, 4 * KC])
    gfin_d = din("gfin", [128, D])
    pscb_d = din("pscb", [128, D])
    subg_d = din("subg", [128, 2])
    lamb_d = din("lamb", [128, 4 * 128])
    poolw_d = din("pool_w", [4 * GD, GD])
    wqkv_d = din("w_qkv", [D, 3 * D])
    wo_d = din("w_o", [D, D])
    wg_d = din("w_gate", [2 * D, DFF])
    wu_d = din("w_up", [2 * D, DFF])
    wd_d = din("w_down", [2 * DFF, D])
    y_d = nc.dram_tensor("y", [NQ * 512, D], F32, kind="ExternalOutput").ap()

    wqkv_b = dscr("wqkv_b", [D, 3 * D], BF16)
    wo_b = dscr("wo_b", [D, D], BF16)
    wg_b = dscr("wg_b", [2 * D, DFF], BF16)
    wu_b = dscr("wu_b", [2 * D, DFF], BF16)
    wd_b = dscr("wd_b", [2 * DFF, D], BF16)
    kt_s = dscr("kt_s", [H * 2 * 128, NS * 512], BF16)
    v_s = dscr("v_s", [NS * 512, D], BF16)
    qt_s = dscr("qt_s", [H * 2 * 128, NQ * 512], BF16)
    h1_s = dscr("h1_s", [NQ * 512, D], F32)

    tr = Tracker()
    es = ExitStack()

    def sb(name, shape, dt):
        return es.enter_context(nc.sbuf_tensor("sb_" + name, list(shape), dt))

    def ps(name):
        return es.enter_context(nc.psum_tensor(name, [128, 512], F32))

    xt = sb("xt", [128, 4, D], F32)
    hn = sb("hn", [128, 4, D], BF16)
    cT = sb("cT", [128, KC, 512], BF16)
    ARW = max(FC * 512, 22 * 1024)
    arena = sb("arena", [128, ARW], BF16)
    slabs = [sb("slab%d" % i, [128, KC, 256], BF16) for i in range(4)]
    wdc = [sb("wdc%d" % i, [128, 2, OC], BF16) for i in range(2)]
    sg = [sb("sg%d" % i, [128, 512], F32) for i in range(2)]
    PHW = 18 * 1024
    parena = sb("parena", [128, PHW], BF16)
    ident = sb("ident_sb", [128, 128], BF16)
    ones = sb("ones_sb", [128, 128], BF16)
    gcols = sb("gcols_sb", [128, 4 * KC], F32)
    subg = sb("subg_sb", [128, 2], F32)
    small = sb("small", [128, 32], F32)
    lamc = sb("lamc", [128, 8], F32)
    pb = [ps("ps%d" % i) for i in range(8)]

    def carve(base_ap_t, off, n, dt, shape=None):
        a = base_ap_t[:, off:off + n]
        if dt == F32:
            a = a.bitcast(F32)
        return a

    actT = arena[:, 0:FC * 512].rearrange("p (f t) -> p f t", f=FC)
    o = [0]

    def acarve(nbf, dt):
        a = carve(arena, o[0], nbf, dt)
        o[0] += nbf
        return a
    hl = arena[0:64, 0:4 * D].bitcast(F32).rearrange("p (a c) -> p a c", a=2)
    QT = [acarve(1024, BF16).rearrange("p (j t) -> p j t", j=2) for _ in range(2)]
    KTc = [acarve(1024, BF16).rearrange("p (j t) -> p j t", j=2) for _ in range(2)]
    Vc = [acarve(1024, BF16).rearrange("p (b e) -> p b e", b=4) for _ in range(2)]
    Sp = [acarve(1024, F32) for _ in range(2)]
    PT = [[acarve(512, BF16) for _ in range(2)] for _ in range(2)]
    Rr = [acarve(1024, F32) for _ in range(2)]
    of = acarve(2048, F32).rearrange("p (e t) -> p e t", e=2)
    t1 = acarve(1024, F32)
    sq = acarve(1024, BF16).rearrange("p (e t) -> p e t", e=2)
    rs = acarve(1024, F32)
    assert o[0] <= ARW

    po = [0]

    def pcarve(nbf, dt, parts=128):
        a = parena[0:parts, po[0]:po[0] + nbf]
        if dt == F32:
            a = a.bitcast(F32)
        po[0] += nbf
        return a
    hnh = pcarve(2 * D, BF16, 64).rearrange("p (a c) -> p a c", a=2)
    pw = pcarve(4 * CG * GD, BF16).rearrange("p (g k d) -> p g k d", g=4, k=CG)
    bandm = pcarve(4 * 4 * 128, BF16).rearrange("p (s g t) -> p s g t", s=4, g=4)
    bandh = pcarve(2 * 4 * 128, BF16, 64).rearrange("p (a g t) -> p a g t", a=2, g=4)
    ktsb = [pcarve(512, BF16) for _ in range(2)]
    vsb = [pcarve(1024, BF16).rearrange("p (s e) -> p s e", s=4) for _ in range(2)]
    assert po[0] <= PHW, po[0]
    po[0] = 0
    base0 = pcarve(1024, F32)
    w0 = pcarve(1792, F32)
    tabs = pcarve(2 * 3 * NKB, F32).rearrange("p (a k) -> p a k", a=3)
    biasrow = pcarve(2 * NKB, F32)
    sgnrow = pcarve(2 * NKB, F32)
    gfin = pcarve(2 * D, F32)
    assert po[0] <= PHW, po[0]

    B = {}

    def bf(name):
        if name not in B:
            B[name] = Buf(name)
        return B[name]

    PB = [bf("psum%d" % i) for i in range(8)]

    def mm(out, lhsT, rhs, start, stop, reads, wbuf, last):
        tr.op("pe", lambda e: e.matmul(out, lhsT, rhs, start=start, stop=stop),
              reads=reads, writes=[wbuf], inc=last)

    def tp(out, in_, reads, wbuf, last):
        tr.op("pe", lambda e: e.transpose(out, in_, ident[:, :]), reads=reads, writes=[wbuf], inc=last)

    WB = bf("wbf16")
    for src, dst in ((wqkv_d, wqkv_b), (wo_d, wo_b), (wg_d, wg_b), (wu_d, wu_b), (wd_d, wd_b)):
        R = src.shape[0]
        step = 256
        for r0 in range(0, R, step):
            tr.dma("pool", (lambda s_, d_, r0_: (lambda e: e.dma_start(out=d_[r0_:r0_ + step, :], in_=s_[r0_:r0_ + step, :])))(src, dst, r0),
                   "s_wconv", writes=[WB])
    CB = bf("consts")
    for dst, src in ((ident[:, :], ident_d), (ones[:, :], ones_d), (gcols[:, :], gcols_d), (subg[:, :], subg_d)):
        tr.dma("sp", (lambda d_, s_: (lambda e: e.dma_start(out=d_, in_=s_[:, :])))(dst, src), "s_const", writes=[CB])
    pwtmp = arena[:, 0:2 * 4 * CG * GD].bitcast(F32).rearrange("p (g k d) -> p g k d", g=4, k=CG)
    psc_t = xt[:, 0, :]
    AR = bf("arena")
    XT = bf("xt")
    PW = bf("pw")
    tr.dma("sp", lambda e: e.dma_start(out=pwtmp, in_=poolw_d.rearrange("(g k p) d -> p g k d", g=4, k=CG)), "s_arena", writes=[AR])
    tr.dma("sp", lambda e: e.dma_start(out=psc_t, in_=pscb_d[:, :]), "s_xt", writes=[XT])
    for g in range(4):
        for k in range(CG):
            tr.op("dve", (lambda g_, k_: (lambda e: e.tensor_tensor(out=pw[:, g_, k_, :], in0=pwtmp[:, g_, k_, :], in1=psc_t[:, g_ * GD:(g_ + 1) * GD], op=ALU.mult)))(g, k),
                  reads=[AR, XT], writes=[PW])
    lamt = xt[:, 1, 0:512]
    lamp = xt[:, 2, 0:256]
    LM = bf("lamtmp")
    SM = bf("small")
    tr.dma("sp", lambda e: e.dma_start(out=lamt, in_=lamb_d[:, :]), "s_lam", writes=[LM])
    tr.op("dve", lambda e: e.tensor_tensor(out=lamp[:, 0:128], in0=lamt[:, 0:128], in1=lamt[:, 128:256], op=ALU.mult), reads=[LM], writes=[LM])
    tr.op("dve", lambda e: e.tensor_tensor(out=lamp[:, 128:256], in0=lamt[:, 256:384], in1=lamt[:, 384:512], op=ALU.mult), reads=[LM], writes=[LM])
    tr.op("dve", lambda e: e.reduce_sum(out=lamc[:, 0:1], in_=lamp[:, 0:128], axis=AX.X), reads=[LM], writes=[SM])
    tr.op("dve", lambda e: e.reduce_sum(out=lamc[:, 1:2], in_=lamp[:, 128:256], axis=AX.X), reads=[LM], writes=[SM])
    tr.op("act", lambda e: e.activation(out=lamc[:, 2:4], in_=lamc[:, 0:2], func=AF.Exp), reads=[SM], writes=[SM])
    tr.op("dve", lambda e: e.tensor_tensor(out=lamc[:, 4:5], in0=lamc[:, 2:3], in1=lamc[:, 3:4], op=ALU.subtract), reads=[SM], writes=[SM])
    tr.op("dve", lambda e: e.tensor_scalar(out=lamc[:, 5:6], in0=lamc[:, 4:5], scalar1=lam_init, scalar2=None, op0=ALU.add), reads=[SM], writes=[SM])
    lam_col = lamc[:, 5:6]
    tr.op("dve", lambda e: e.tensor_scalar(out=subg[:, :], in0=subg[:, :], scalar1=(1.0 - lam_init), scalar2=None, op0=ALU.mult), reads=[CB], writes=[CB])
    tr.barrier()

    HN = bf("hn")
    CT = bf("cT")
    SL = [bf("slab%d" % i) for i in range(4)]
    WDC = [bf("wdc%d" % i) for i in range(2)]
    SG = [bf("sg%d" % i) for i in range(2)]
    slab_i = [0]
    wdc_i = [0]
    sg_i = [0]
    pbi = [0]

    def next_pb():
        i = pbi[0] % 8
        pbi[0] += 1
        return i

    def rms_stats(col0, nsub=4):
        tr.op("dve", lambda e: e.memset(small[:, col0:col0 + nsub], 0.0), writes=[SM])
        for s in range(nsub):
            tr.op("act", (lambda s_: (lambda e: e.activation(out=hn[:, s_, :], in_=xt[:, s_, :], func=AF.Square, accum_out=small[:, col0 + s_:col0 + s_ + 1])))(s),
                  reads=[XT], writes=[HN, SM])
        tr.op("dve", lambda e: e.tensor_scalar(out=small[:, col0:col0 + nsub], in0=small[:, col0:col0 + nsub], scalar1=1.0 / D, scalar2=EPS, op0=ALU.mult, op1=ALU.add), reads=[SM], writes=[SM])
        tr.op("act", lambda e: e.activation(out=small[:, col0:col0 + nsub], in_=small[:, col0:col0 + nsub], func=AF.Sqrt), reads=[SM], writes=[SM])
        tr.op("dve", lambda e: e.reciprocal(out=small[:, col0:col0 + nsub], in_=small[:, col0:col0 + nsub]), reads=[SM], writes=[SM])

    def make_hn(col0):
        for s in range(4):
            tr.op("dve", (lambda s_: (lambda e: e.tensor_scalar(out=hn[:, s_, :], in0=xt[:, s_, :], scalar1=small[:, col0 + s_:col0 + s_ + 1], scalar2=None, op0=ALU.mult)))(s),
                  reads=[XT, SM], writes=[HN])

    def transpose_hn(gi):
        for k in range(KC):
            b = next_pb()
            pv = pb[b][:, :].bitcast(BF16)
            for s in range(4):
                tp(pv[:, s * 128:(s + 1) * 128], hn[:, s, k * 128:(k + 1) * 128], [HN, CB], PB[b], last=(s == 3))
            eng = "act" if k % 2 == 0 else "dve"
            if eng == "act":
                tr.op("act", (lambda k_, pv_: (lambda e: e.activation(out=cT[:, k_, :], in_=pv_[:, 0:512], func=AF.Copy, scale=gcols[:, gi * KC + k_:gi * KC + k_ + 1])))(k, pv),
                      reads=[PB[b], CB], writes=[CT])
            else:
                tr.op("dve", (lambda k_, pv_: (lambda e: e.tensor_scalar(out=cT[:, k_, :], in0=pv_[:, 0:512], scalar1=gcols[:, gi * KC + k_:gi * KC + k_ + 1], scalar2=None, op0=ALU.mult)))(k, pv),
                      reads=[PB[b], CB], writes=[CT])

    def load_slab(src_ap, c0):
        i = slab_i[0] % 4
        slab_i[0] += 1
        tr.dma("sp", (lambda i_: (lambda e: e.dma_start(out=slabs[i_][:, :, :], in_=src_ap[:, c0:c0 + 256].rearrange("(k p) f -> p k f", p=128))))(i),
               "s_slab%d" % i, reads=[WB], writes=[SL[i]])
        return i

    AT = bf("actT")

    def ffn(layer, gi, col0):
        rms_stats(col0)
        make_hn(col0)
        transpose_hn(gi)
        wg_l = wg_b[layer * D:(layer + 1) * D, :]
        wu_l = wu_b[layer * D:(layer + 1) * D, :]
        wd_l = wd_b[layer * DFF:(layer + 1) * DFF, :]
        nfg = DFF // 256
        pend = None
        nxt = (load_slab(wg_l, 0), load_slab(wu_l, 0))
        for fg in range(nfg):
            cur = nxt
            if fg + 1 < nfg:
                nxt = (load_slab(wg_l, (fg + 1) * 256), load_slab(wu_l, (fg + 1) * 256))
            for fc in range(2):
                f = fg * 2 + fc
                bg, bu = next_pb(), next_pb()
                for (si, b) in ((cur[0], bg), (cur[1], bu)):
                    for k in range(KC):
                        mm(pb[b][:, :], slabs[si][:, k, fc * 128:(fc + 1) * 128], cT[:, k, :], k == 0, k == KC - 1,
                           [SL[si], CT], PB[b], last=(k == KC - 1))
                gi_ = sg_i[0] % 2
                sg_i[0] += 1
                tr.op("act", (lambda b_, g_: (lambda e: e.activation(out=sg[g_][:, :], in_=pb[b_][:, :], func=AF.Silu)))(bg, gi_),
                      reads=[PB[bg]], writes=[SG[gi_]])
                tr.op("dve", (lambda b_, g_, f_: (lambda e: e.tensor_tensor(out=actT[:, f_, :], in0=sg[g_][:, :], in1=pb[b_][:, :], op=ALU.mult)))(bu, gi_, f),
                      reads=[SG[gi_], PB[bu]], writes=[AT])
        for p_ in range(NPASS):
            banks = [[next_pb() for _ in range(NOG)] for _ in range(4)]
            nch = FC // 2
            def ld(c):
                i = wdc_i[0] % 2
                wdc_i[0] += 1
                tr.dma("sp", (lambda i_, c_, pp_: (lambda e: e.dma_start(out=wdc[i_][:, :, :], in_=wd_l[c_ * 256:(c_ + 1) * 256, pp_ * OC:(pp_ + 1) * OC].rearrange("(a p) d -> p a d", p=128))))(i, c, p_),
                       "s_wdc%d" % i, reads=[WB], writes=[WDC[i]])
                return i
            nx = ld(0)
            for c in range(nch):
                cu = nx
                if c + 1 < nch:
                    nx = ld(c + 1)
                for a in range(2):
                    f = c * 2 + a
                    for s in range(4):
                        for og in range(NOG):
                            b = banks[s][og]
                            mm(pb[b][:, 0:OW], actT[:, f, s * 128:(s + 1) * 128], wdc[cu][:, a, og * OW:(og + 1) * OW], f == 0, f == FC - 1,
                               [AT, WDC[cu]], PB[b], last=(f == FC - 1 or (a == 1 and s == 3 and og == NOG - 1)))
            for s in range(4):
                for og in range(NOG):
                    b = banks[s][og]
                    c0 = p_ * OC + og * OW
                    tr.op("dve", (lambda s_, b_, c0_: (lambda e: e.tensor_tensor(out=xt[:, s_, c0_:c0_ + OW], in0=xt[:, s_, c0_:c0_ + OW], in1=pb[b_][:, 0:OW], op=ALU.add)))(s, b, c0),
                          reads=[XT, PB[b]], writes=[XT])

    HL = AR
    BD = bf("band")
    HH = bf("hnh")
    KSB = [bf("ktsb%d" % i) for i in range(2)]
    VSB = [bf("vsb%d" % i) for i in range(2)]
    KTS, VS, QTS, H1S = bf("kt_s"), bf("v_s"), bf("qt_s"), bf("h1_s")
    ksb_i = [0]
    vsb_i = [0]
    for slot in range(NS):
        isq = slot < NQ
        tr.dma("sp", (lambda sl: (lambda e: e.dma_start(out=xt[:, :, :], in_=xt_d[sl * 512:(sl + 1) * 512, :].rearrange("(s p) c -> p s c", p=128))))(slot), "s_xt", writes=[XT])
        tr.dma("sp", (lambda sl: (lambda e: e.dma_start(out=hl, in_=halo_d[sl * 128:(sl + 1) * 128, :].rearrange("(a p) c -> p a c", p=64))))(slot), "s_arena", writes=[AT])
        tr.dma("sp", (lambda sl: (lambda e: e.dma_start(out=bandm, in_=bandm_d[sl * 128:(sl + 1) * 128, :].rearrange("p (s g t) -> p s g t", s=4, g=4))))(slot), "s_band", writes=[BD])
        tr.dma("sp", (lambda sl: (lambda e: e.dma_start(out=bandh, in_=bandh_d[sl * 64:(sl + 1) * 64, :].rearrange("p (a g t) -> p a g t", a=2, g=4))))(slot), "s_band", writes=[BD])
        rms_stats(0)
        tr.op("dve", lambda e: e.memset(small[0:64, 4:6], 0.0), writes=[SM])
        for a in range(2):
            tr.op("act", (lambda a_: (lambda e: e.activation(out=hnh[:, a_, :], in_=hl[:, a_, :], func=AF.Square, accum_out=small[0:64, 4 + a_:5 + a_])))(a),
                  reads=[AT], writes=[HH, SM])
        tr.op("dve", lambda e: e.tensor_scalar(out=small[0:64, 4:6], in0=small[0:64, 4:6], scalar1=1.0 / D, scalar2=EPS, op0=ALU.mult, op1=ALU.add), reads=[SM], writes=[SM])
        tr.op("act", lambda e: e.activation(out=small[0:64, 4:6], in_=small[0:64, 4:6], func=AF.Sqrt), reads=[SM], writes=[SM])
        tr.op("dve", lambda e: e.reciprocal(out=small[0:64, 4:6], in_=small[0:64, 4:6]), reads=[SM], writes=[SM])
        make_hn(0)
        for a in range(2):
            tr.op("dve", (lambda a_: (lambda e: e.tensor_scalar(out=hnh[:, a_, :], in0=hl[:, a_, :], scalar1=small[0:64, 4 + a_:5 + a_], scalar2=None, op0=ALU.mult)))(a),
                  reads=[AT, SM], writes=[HH])
        for k in range(KC):
            g = k // CG
            b = next_pb()
            for s in range(4):
                a, hp = s // 2, 32 * (s % 2)
                mm(pb[b][:, s * 128:(s + 1) * 128], hn[:, s, k * 128:(k + 1) * 128], bandm[:, s, g, :], True, False, [HN, BD], PB[b], last=False)
                mm(pb[b][:, s * 128:(s + 1) * 128], hnh[hp:hp + 32, a, k * 128:(k + 1) * 128], bandh[hp:hp + 32, a, g, :], False, True, [HH, BD], PB[b], last=(s == 3))
            if k % 2 == 0:
                tr.op("act", (lambda k_, b_: (lambda e: e.activation(out=cT[:, k_, :], in_=pb[b_][:, :], func=AF.Copy, scale=gcols[:, k_:k_ + 1])))(k, b),
                      reads=[PB[b], CB], writes=[CT])
            else:
                tr.op("dve", (lambda k_, b_: (lambda e: e.tensor_scalar(out=cT[:, k_, :], in0=pb[b_][:, :], scalar1=gcols[:, k_:k_ + 1], scalar2=None, op0=ALU.mult)))(k, b),
                      reads=[PB[b], CB], writes=[CT])
        for s in range(4):
            for g in range(4):
                b = next_pb()
                for kk in range(CG):
                    mm(pb[b][:, 0:GD], cT[:, g * CG + kk, s * 128:(s + 1) * 128], pw[:, g, kk, :], kk == 0, kk == CG - 1, [CT, PW], PB[b], last=(kk == CG - 1))
                tr.op("dve", (lambda s_, g_, b_: (lambda e: e.tensor_tensor(out=xt[:, s_, g_ * GD:(g_ + 1) * GD], in0=xt[:, s_, g_ * GD:(g_ + 1) * GD], in1=pb[b_][:, 0:GD], op=ALU.add)))(s, g, b),
                      reads=[XT, PB[b]], writes=[XT])
        ffn(0, 1, 8)
        if isq:
            tr.dma("pool", (lambda sl: (lambda e: e.dma_start(out=h1_s[sl * 512:(sl + 1) * 512, :].rearrange("(s p) c -> p s c", p=128), in_=xt[:, :, :])))(slot),
                   "s_h1st", reads=[XT], writes=[H1S])
        rms_stats(12)
        make_hn(12)
        transpose_hn(2)
        jobs = []
        for h in range(H):
            jobs.append(("k", h))
            jobs.append(("v", h))
            if isq:
                jobs.append(("q", h))

        def jcol(job):
            kind, h = job
            return {"q": 0, "k": D, "v": 2 * D}[kind] + h * 256
        nxs = load_slab(wqkv_b, jcol(jobs[0]))
        for ji, job in enumerate(jobs):
            cs = nxs
            if ji + 1 < len(jobs):
                nxs = load_slab(wqkv_b, jcol(jobs[ji + 1]))
            kind, h = job
            if kind in ("k", "q"):
                for j in range(2):
                    b = next_pb()
                    for k in range(KC):
                        mm(pb[b][:, :], slabs[cs][:, k, j * 128:(j + 1) * 128], cT[:, k, :], k == 0, k == KC - 1, [SL[cs], CT], PB[b], last=(k == KC - 1))
                    ki = ksb_i[0] % 2
                    ksb_i[0] += 1
                    tr.op("act", (lambda b_, ki_: (lambda e: e.activation(out=ktsb[ki_], in_=pb[b_][:, :], func=AF.Copy)))(b, ki), reads=[PB[b]], writes=[KSB[ki]])
                    dst, DB, ncol = (kt_s, KTS, NS * 512) if kind == "k" else (qt_s, QTS, NQ * 512)
                    r0 = (h * 2 + j) * 128
                    tr.dma("pool", (lambda dst_, r0_, sl, ki_: (lambda e: e.dma_start(out=dst_[r0_:r0_ + 128, sl * 512:(sl + 1) * 512], in_=ktsb[ki_])))(dst, r0, slot, ki),
                           "s_kst%d" % ki, reads=[KSB[ki]], writes=[DB])
            else:
                vi = vsb_i[0] % 2
                vsb_i[0] += 1
                for s in range(4):
                    b = next_pb()
                    for k in range(KC):
                        mm(pb[b][:, 0:256], cT[:, k, s * 128:(s + 1) * 128], slabs[cs][:, k, :], k == 0, k == KC - 1, [CT, SL[cs]], PB[b], last=(k == KC - 1))
                    tr.op("dve", (lambda b_, vi_, s_: (lambda e: e.tensor_copy(out=vsb[vi_][:, s_, :], in_=pb[b_][:, 0:256])))(b, vi, s), reads=[PB[b]], writes=[VSB[vi]])
                tr.dma("pool", (lambda sl, h_, vi_: (lambda e: e.dma_start(out=v_s[sl * 512:(sl + 1) * 512, h_ * 256:(h_ + 1) * 256].rearrange("(s p) e -> p s e", p=128), in_=vsb[vi_])))(slot, h, vi),
                       "s_vst%d" % vi, reads=[VSB[vi]], writes=[VS])
    tr.barrier()

    QC = bf("qconst")
    for dst, src in ((base0, base0_d), (w0, w0_d), (gfin, gfin_d)):
        tr.dma("sp", (lambda d_, s_: (lambda e: e.dma_start(out=d_, in_=s_[:, :])))(dst, src), "s_const", writes=[QC])
    TB = bf("tabs")
    ROW = bf("rows")
    QTB = [bf("QT%d" % i) for i in range(2)]
    KTB = [bf("KTc%d" % i) for i in range(2)]
    VCB = [bf("Vc%d" % i) for i in range(2)]
    SPB = [bf("Sp%d" % i) for i in range(2)]
    PTB = [[bf("PT%d_%d" % (j, i)) for i in range(2)] for j in range(2)]
    FIN = bf("fin")
    qt_i = [0]
    kv_i = [0]
    pt_i = [0]
    slopes = [2.0 ** (-8.0 * (h + 1) / H) for h in range(H)]
    for qi in range(NQ):
        tr.barrier()
        tr.dma("sp", (lambda q_: (lambda e: e.dma_start(out=xt[:, :, :], in_=h1_s[q_ * 512:(q_ + 1) * 512, :].rearrange("(s p) c -> p s c", p=128))))(qi), "s_xt", reads=[H1S], writes=[XT])
        tr.dma("sp", (lambda q_: (lambda e: e.dma_start(out=tabs, in_=tabs_d[q_ * 128:(q_ + 1) * 128, :].rearrange("p (a k) -> p a k", a=3))))(qi), "s_tabs", writes=[TB])
        for h in range(H):
            fh = -slopes[h] / scale
            tr.op("dve", (lambda h_: (lambda e: e.scalar_tensor_tensor(out=biasrow, in0=tabs[:, 1, :], scalar=-slopes[h_], in1=tabs[:, 2, :], op0=ALU.mult, op1=ALU.add)))(h), reads=[TB], writes=[ROW])
            tr.op("dve", (lambda f_: (lambda e: e.tensor_scalar(out=sgnrow, in0=tabs[:, 0, :], scalar1=f_, scalar2=None, op0=ALU.mult)))(fh), reads=[TB], writes=[ROW])
            qb_ = qt_i[0] % 2
            qt_i[0] += 1
            tr.dma("sp", (lambda h_, q_, b_: (lambda e: e.dma_start(out=QT[b_], in_=qt_s[h_ * 256:(h_ + 1) * 256, q_ * 512:(q_ + 1) * 512].rearrange("(j p) t -> p j t", p=128))))(h, qi, qb_),
                   "s_qt%d" % qb_, reads=[QTS], writes=[QTB[qb_]])
            nchunk = (NKB + 3) // 4

            def ldkv(c):
                i = kv_i[0] % 2
                kv_i[0] += 1
                nb = min(4, NKB - 4 * c)
                tr.dma("sp", (lambda h_, c_, i_, nb_: (lambda e: e.dma_start(out=KTc[i_][:, :, 0:nb_ * 128], in_=kt_s[h_ * 256:(h_ + 1) * 256, c_ * 512:c_ * 512 + nb_ * 128].rearrange("(j p) t -> p j t", p=128))))(h, c, i, nb),
                       "s_kt%d" % i, reads=[KTS], writes=[KTB[i]])
                tr.dma("sp", (lambda h_, c_, i_, nb_: (lambda e: e.dma_start(out=Vc[i_][:, 0:nb_, :], in_=v_s[c_ * 512:c_ * 512 + nb_ * 128, h_ * 256:(h_ + 1) * 256].rearrange("(b p) e -> p b e", p=128))))(h, c, i, nb),
                       "s_vc%d" % i, reads=[VS], writes=[VCB[i]])
                return i
            nxkv = ldkv(0)
            first = True
            pend_pv = [None]
            for c in range(nchunk):
                ckv = nxkv
                nb = min(4, NKB - 4 * c)
                for bi in range(nb):
                    kb = 4 * c + bi
                    lastkb = (kb == NKB - 1)
                    diag = (kb // 4 == qi) and kb < 4 * NT
                    pts = []
                    for j in range(2):
                        mm(pb[j][:, :], KTc[ckv][:, j, bi * 128:(bi + 1) * 128], QT[qb_][:, j, :], True, True, [KTB[ckv], QTB[qb_]], PB[j], last=(j == 1))
                    for j in range(2):
                        if diag:
                            o_ = kb % 4
                            tr.op("dve", (lambda j_, o__, f_: (lambda e: e.scalar_tensor_tensor(out=Sp[j_], in0=w0[:, 384 - 128 * o__:384 - 128 * o__ + 512], scalar=f_, in1=pb[j_][:, :], op0=ALU.mult, op1=ALU.add)))(j, o_, fh),
                                  reads=[QC, PB[j]], writes=[SPB[j]])
                        else:
                            tr.op("dve", (lambda j_, kb_: (lambda e: e.scalar_tensor_tensor(out=Sp[j_], in0=base0, scalar=sgnrow[:, kb_:kb_ + 1], in1=pb[j_][:, :], op0=ALU.mult, op1=ALU.add)))(j, kb),
                                  reads=[QC, ROW, PB[j]], writes=[SPB[j]])
                        pi = pt_i[0] % 2
                        tr.op("act", (lambda j_, kb_, pi_: (lambda e: e.activation(out=PT[j_][pi_], in_=Sp[j_], func=AF.Exp, bias=biasrow[:, kb_:kb_ + 1], scale=scale)))(j, kb, pi),
                              reads=[SPB[j], ROW], writes=[PTB[j][pi]])
                        pts.append(pi)
                    pt_i[0] += 1

                    def pv(pts=pts, ckv=ckv, bi=bi, first=first, lastkb=lastkb):
                        for j in range(2):
                            pi = pts[j]
                            for ec in range(2):
                                b = 2 + 2 * j + ec
                                mm(pb[b][:, :], Vc[ckv][:, bi, ec * 128:(ec + 1) * 128], PT[j][pi], first, lastkb, [VCB[ckv], PTB[j][pi]], PB[b], last=False)
                            mm(pb[6 + j][:, :], ones[:, :], PT[j][pi], first, lastkb, [CB, PTB[j][pi]], PB[6 + j], last=(j == 1))
                    if pend_pv[0] is not None:
                        pend_pv[0]()
                    pend_pv[0] = pv
                    first = False
                    if bi == 0 and c + 1 < nchunk:
                        nxkv = ldkv(c + 1)
            pend_pv[0]()
            pend_pv[0] = None
            for j in range(2):
                tr.op("dve", (lambda j_: (lambda e: e.reciprocal(out=Rr[j_], in_=pb[6 + j_][:, :])))(j), reads=[PB[6 + j]], writes=[FIN])
            tr.op("dve", lambda e: e.tensor_scalar(out=Rr[1], in0=Rr[1], scalar1=lam_col, scalar2=None, op0=ALU.mult), reads=[FIN, SM], writes=[FIN])
            for ec in range(2):
                tr.op("dve", (lambda ec_: (lambda e: e.tensor_tensor(out=of[:, ec_, :], in0=pb[2 + ec_][:, :], in1=Rr[0], op=ALU.mult)))(ec), reads=[PB[2 + ec], FIN], writes=[FIN])
                tr.op("dve", (lambda ec_: (lambda e: e.tensor_tensor(out=t1, in0=pb[4 + ec_][:, :], in1=Rr[1], op=ALU.mult)))(ec), reads=[PB[4 + ec], FIN], writes=[FIN])
                tr.op("dve", (lambda ec_: (lambda e: e.tensor_tensor(out=of[:, ec_, :], in0=of[:, ec_, :], in1=t1, op=ALU.subtract)))(ec), reads=[FIN], writes=[FIN])
                tr.op("act", (lambda ec_: (lambda e: e.activation(out=sq[:, ec_, :], in_=of[:, ec_, :], func=AF.Square)))(ec), reads=[FIN], writes=[FIN])
            for ec in range(2):
                mm(pb[0][:, :], ones[:, :], sq[:, ec, :], ec == 0, ec == 1, [CB, FIN], PB[0], last=(ec == 1))
            tr.op("dve", lambda e: e.tensor_scalar(out=rs, in0=pb[0][:, :], scalar1=1.0 / 256.0, scalar2=EPS, op0=ALU.mult, op1=ALU.add), reads=[PB[0]], writes=[FIN])
            tr.op("act", lambda e: e.activation(out=rs, in_=rs, func=AF.Sqrt), reads=[FIN], writes=[FIN])
            tr.op("dve", lambda e: e.reciprocal(out=rs, in_=rs), reads=[FIN], writes=[FIN])
            for ec in range(2):
                tr.op("dve", (lambda ec_, h_: (lambda e: e.scalar_tensor_tensor(out=cT[:, 2 * h_ + ec_, :], in0=of[:, ec_, :], scalar=subg[:, ec_:ec_ + 1], in1=rs, op0=ALU.mult, op1=ALU.mult)))(ec, h),
                      reads=[FIN, CB], writes=[CT])
        tr.barrier()
        for p_ in range(NPASS):
            banks = [[next_pb() for _ in range(NOG)] for _ in range(4)]
            nch = KC // 2

            def ldo(c):
                i = wdc_i[0] % 2
                wdc_i[0] += 1
                tr.dma("sp", (lambda i_, c_, pp_: (lambda e: e.dma_start(out=wdc[i_][:, :, :], in_=wo_b[c_ * 256:(c_ + 1) * 256, pp_ * OC:(pp_ + 1) * OC].rearrange("(a p) d -> p a d", p=128))))(i, c, p_),
                       "s_wdc%d" % i, reads=[WB], writes=[WDC[i]])
                return i
            nx = ldo(0)
            for c in range(nch):
                cu = nx
                if c + 1 < nch:
                    nx = ldo(c + 1)
                for a in range(2):
                    k = c * 2 + a
                    for s in range(4):
                        for og in range(NOG):
                            b = banks[s][og]
                            mm(pb[b][:, 0:OW], cT[:, k, s * 128:(s + 1) * 128], wdc[cu][:, a, og * OW:(og + 1) * OW], k == 0, k == KC - 1,
                               [CT, WDC[cu]], PB[b], last=(k == KC - 1 or (a == 1 and s == 3 and og == NOG - 1)))
            for s in range(4):
                for og in range(NOG):
                    b = banks[s][og]
                    c0 = p_ * OC + og * OW
                    tr.op("dve", (lambda s_, b_, c0_: (lambda e: e.tensor_tensor(out=xt[:, s_, c0_:c0_ + OW], in0=xt[:, s_, c0_:c0_ + OW], in1=pb[b_][:, 0:OW], op=ALU.add)))(s, b, c0),
                          reads=[XT, PB[b]], writes=[XT])
        ffn(1, 3, 16)
        rms_stats(20)
        for s in range(4):
            tr.op("dve", (lambda s_: (lambda e: e.scalar_tensor_tensor(out=xt[:, s_, :], in0=xt[:, s_, :], scalar=small[:, 20 + s_:21 + s_], in1=gfin, op0=ALU.mult, op1=ALU.mult)))(s),
                  reads=[XT, SM, QC], writes=[XT])
        tr.dma("pool", (lambda q_: (lambda e: e.dma_start(out=y_d[q_ * 512:(q_ + 1) * 512, :].rearrange("(s p) c -> p s c", p=128), in_=xt[:, :, :])))(qi), "s_yst", reads=[XT], writes=[bf("y")])
    tr.barrier()

    sems = {}
    for k in sorted(tr.semkeys):
        sems[k] = es.enter_context(nc.semaphore(k))
    with nc.Block() as block:
        @block.tensor
        def _(e):
            tr.replay("pe", e, sems)

        @block.scalar
        def _(e):
            tr.replay("act", e, sems)

        @block.vector
        def _(e):
            tr.replay("dve", e, sems)

        @block.gpsimd
        def _(e):
            tr.replay("pool", e, sems)

        @block.sync
        def _(e):
            tr.replay("sp", e, sems)
    es.close()
    return nc


_band_cache = {}


def _band_for(pos_main, pos_halo, L):
    key = (tuple(pos_main.tolist()), tuple(pos_halo.tolist()), L)
    p0 = int(pos_main[0])
    rel = (tuple((pos_main - p0).tolist()) if p0 >= 0 else None, tuple(np.where(pos_halo >= 0, pos_halo - p0, -999).tolist()), min(p0, 40), min(L - p0, 300) if p0 >= 0 else -1)
    if rel in _band_cache:
        return _band_cache[rel]
    bm = np.zeros((128, 4, 128), np.float32)
    bh = np.zeros((32, 4, 128), np.float32)
    where = {}
    for i, p in enumerate(pos_main):
        if p >= 0:
            where[int(p)] = ("m", i)
    for i, p in enumerate(pos_halo):
        if p >= 0 and int(p) not in where:
            where[int(p)] = ("h", i)
    for r, p in enumerate(pos_main):
        if p < 0:
            continue
        p = int(p)
        for g, w in enumerate(WINDOWS):
            lo, hi = max(p - w // 2, 0), min(p + w // 2, L)
            inv = 1.0 / float(hi - lo)
            for u in range(lo, hi):
                kind, i = where[u]
                if kind == "m":
                    bm[i, g, r] += inv
                else:
                    bh[i, g, r] += inv
            bm[r, g, r] -= 1.0
    _band_cache[rel] = (bm, bh)
    return bm, bh


def _core_layout(cfg, x_seq, meta, ctype):
    D, NT, NQ, NS, NKB = cfg.D, cfg.NT, cfg.NQ, cfg.NS, cfg.NKB
    S = x_seq.shape[0]
    L = S + N_META
    ntile = S // 512
    half = ntile // 2
    tmap = [-1] * NT
    for i in range(half):
        tmap[i] = (ctype * half + i)
        tmap[NQ + i] = ((1 - ctype) * half + i)
    pos = -np.ones((NS, 512), np.int64)
    xt = np.zeros((NS * 512, D), np.float32)
    for sl in range(NT):
        if tmap[sl] >= 0:
            t = tmap[sl]
            pos[sl] = N_META + 512 * t + np.arange(512)
            xt[sl * 512:(sl + 1) * 512] = x_seq[512 * t:512 * (t + 1)]
    pos[NT, :N_META] = np.arange(N_META)
    xt[NT * 512:NT * 512 + N_META] = meta

    def row_of(p):
        return meta[p] if p < N_META else x_seq[p - N_META]
    halo = np.zeros((NS, 2, 64, D), np.float32)
    bandm = np.zeros((NS, 128, 4, 4, 128), np.float32)
    bandh = np.zeros((NS, 64, 2, 4, 128), np.float32)
    for sl in range(NS):
        for s in range(4):
            pm = pos[sl, s * 128:(s + 1) * 128]
            ph = -np.ones(32, np.int64)
            if pm[0] >= 0:
                nvalid = int((pm >= 0).sum())
                p0, p1 = int(pm[0]), int(pm[0]) + nvalid
                cand = list(range(p0 - 8, p0)) + list(range(p1, p1 + 8))
                for i, p in enumerate(cand):
                    if 0 <= p < L:
                        ph[i] = p
                        halo[sl, s // 2, 32 * (s % 2) + i] = row_of(p)
                bm, bh = _band_for(pm, ph, L)
                bandm[sl, :, s] = bm
                bandh[sl, 32 * (s % 2):32 * (s % 2) + 32, s // 2] = bh
    tabs = np.zeros((NQ, 128, 3, NKB), np.float32)
    tabs[:, :, 0, :] = 1.0
    for qi in range(NQ):
        if pos[qi, 0] < 0:
            continue
        pq0 = int(pos[qi, 0])
        for kb in range(NKB):
            if kb == NKB - 1:
                tabs[qi, :, 0, kb] = 1.0
                tabs[qi, :, 1, kb] = pq0
                tabs[qi, N_META:, 2, kb] = NEG
                continue
            sl, sub = kb // 4, kb % 4
            if sl == qi:
                continue
            pk0 = int(pos[sl, sub * 128])
            if pk0 < 0:
                tabs[qi, :, 2, kb] = NEG
                continue
            dlt = pq0 - pk0
            tabs[qi, :, 0, kb] = 1.0 if dlt > 0 else -1.0
            tabs[qi, :, 1, kb] = abs(dlt)
    valid_q = [tmap[i] for i in range(NQ)]
    return dict(xt=xt, halo=halo.reshape(NS * 128, D),
                bandm=bandm.reshape(NS * 128, 2048).astype(NPBF),
                bandh=bandh.reshape(NS * 64, 1024).astype(NPBF),
                tabs=tabs.reshape(NQ * 128, 3 * NKB)), valid_q


def _shared_inputs(cfg, inp):
    D, KC = cfg.D, cfg.KC
    f = np.float32
    kk = np.arange(128)[:, None]
    base0 = (np.arange(512)[None, :] - kk).astype(f)
    w0 = np.abs(np.arange(896)[None, :] - 384 - kk).astype(f)
    gl = [inp["mixer_norm_g"][0], inp["ffn_norm_g"][0], inp["mixer_norm_g"][1], inp["ffn_norm_g"][1]]
    gcols = np.concatenate([np.asarray(g, f).reshape(KC, 128).T for g in gl], axis=1)
    lamb = np.concatenate([np.asarray(inp[k], f).reshape(1, 128) for k in ("lambda_q1", "lambda_k1", "lambda_q2", "lambda_k2")], axis=1)
    return dict(
        base0=np.ascontiguousarray(base0), w0=np.ascontiguousarray(w0),
        ident=np.eye(128, dtype=f).astype(NPBF), ones=np.ones((128, 128), f).astype(NPBF),
        gcols=np.ascontiguousarray(gcols),
        gfin=np.ascontiguousarray(np.broadcast_to(np.asarray(inp["final_norm_g"], f).reshape(1, D), (128, D))),
        pscb=np.ascontiguousarray(np.broadcast_to(np.asarray(inp["pool_scale"], f).reshape(1, D), (128, D))),
        subg=np.ascontiguousarray(np.asarray(inp["subln_g"], f).reshape(2, 128).T),
        lamb=np.ascontiguousarray(np.broadcast_to(lamb, (128, 512))),
        pool_w=np.ascontiguousarray(np.asarray(inp["pool_w"], f).reshape(4 * cfg.GD, cfg.GD)),
        w_qkv=np.ascontiguousarray(np.asarray(inp["w_qkv"], f).reshape(D, 3 * D)),
        w_o=np.ascontiguousarray(np.asarray(inp["w_o"], f).reshape(D, D)),
        w_gate=np.ascontiguousarray(np.asarray(inp["w_gate"], f).reshape(2 * D, cfg.DFF)),
        w_up=np.ascontiguousarray(np.asarray(inp["w_up"], f).reshape(2 * D, cfg.DFF)),
        w_down=np.ascontiguousarray(np.asarray(inp["w_down"], f).reshape(2 * cfg.DFF, D)),
    )


def run(cfg, inp):
    xp = np.asarray(inp["x_prompt"], np.float32)
    xs = np.asarray(inp["x_sample"], np.float32)
    meta = np.asarray(inp["meta_tokens"], np.float32)
    shared = _shared_inputs(cfg, inp)
    seqs = [xs[0], xs[1], xp[0], xp[1]]
    in_maps, vq = [], []
    for c in range(8):
        lay, valid_q = _core_layout(cfg, seqs[c // 2], meta, c % 2)
        m = dict(shared)
        m.update(lay)
        in_maps.append(m)
        vq.append(valid_q)
    nc = build(cfg)
    res = run_bass_kernel_spmd(nc, in_maps, core_ids=list(range(8)))
    yp = np.zeros_like(xp)
    ys = np.zeros_like(xs)
    outs = [ys[0], ys[1], yp[0], yp[1]]
    for c in range(8):
        y = np.asarray(res.results[c]["y"], dtype=np.float32)
        for qi, t in enumerate(vq[c]):
            if t >= 0:
                outs[c // 2][512 * t:512 * (t + 1)] = y[512 * qi:512 * (qi + 1)]
    return yp, ys


def kernel(**inputs):
    cfg = Cfg()
    return run(cfg, inputs)
```

```python
import math
from contextlib import ExitStack
import numpy as np
import ml_dtypes
import concourse.bass as bass
import concourse.mybir as mybir
from concourse.bass_utils import run_bass_kernel_spmd

F32 = mybir.dt.float32
BF16 = mybir.dt.bfloat16
ALU = mybir.AluOpType
AF = mybir.ActivationFunctionType
AX = mybir.AxisListType
NPBF = ml_dtypes.bfloat16

N_META = 16
WINDOWS = (2, 4, 8, 16)
EPS = 1e-6
NEG = -30000.0
_SKIP_TH = 60.0


class Cfg:
    def __init__(self, D=2048, DFF=5632, NT=32, NQ=16):
        self.D, self.DFF, self.NT, self.NQ = D, DFF, NT, NQ
        self.KC = D // 128
        self.FC = DFF // 128
        self.H = D // 256
        self.GD = D // 4
        self.CG = self.GD // 128
        self.NS = NT + 1
        self.NKB = 4 * NT + 1
        self.OC = min(D, 1024)
        self.NPASS = D // self.OC
        self.NOG = self.OC // 512 if self.OC >= 512 else 1
        self.OW = min(512, self.OC)


class Buf:
    def __init__(self, name):
        self.name = name
        self.w = None
        self.r = {}


class Tracker:
    ENG = ("pe", "act", "dve", "pool", "sp")

    def __init__(self):
        self.stream = {e: [] for e in self.ENG}
        self.cnt = {e: 0 for e in self.ENG}
        self.waited = {e: {} for e in self.ENG}
        self.dma_cnt = {}
        self.semkeys = set("prog_" + e for e in ("pe", "act", "dve", "pool"))

    def _deps(self, reads, writes):
        deps = {}

        def add(tok):
            if tok is None:
                return
            k, v = tok
            if deps.get(k, 0) < v:
                deps[k] = v
        for b in reads:
            add(b.w)
        for b in writes:
            add(b.w)
            for k, v in b.r.items():
                add((k, v))
        return deps

    def _waits(self, e, deps):
        ws = []
        for k, v in deps.items():
            if e == "pe" and k == "prog_pe":
                continue
            if self.waited[e].get(k, 0) >= v:
                continue
            self.waited[e][k] = v
            ws.append((k, v))
        return ws

    def op(self, e, fn, reads=(), writes=(), inc=True):
        deps = self._deps(reads, writes)
        ws = self._waits(e, deps)
        key = "prog_" + e
        if inc:
            self.cnt[e] += 1
            tok = (key, self.cnt[e])
            incs = [(key, 1)]
        else:
            tok = (key, self.cnt[e] + 1)
            incs = []
        self.stream[e].append((ws, fn, incs))
        for b in reads:
            if b.r.get(key, 0) < tok[1]:
                b.r[key] = tok[1]
        for b in writes:
            b.w = tok
            b.r = {}
        return tok

    def dma(self, q, fn, semname, reads=(), writes=()):
        deps = self._deps(reads, writes)
        ws = self._waits(q, deps)
        self.semkeys.add(semname)
        self.dma_cnt[semname] = self.dma_cnt.get(semname, 0) + 1
        tok = (semname, 16 * self.dma_cnt[semname])
        self.stream[q].append((ws, fn, [(semname, 16)]))
        for b in reads:
            if b.r.get(semname, 0) < tok[1]:
                b.r[semname] = tok[1]
        for b in writes:
            b.w = tok
            b.r = {}
        return tok

    def barrier(self):
        allv = {"prog_" + e: self.cnt[e] for e in ("pe", "act", "dve", "pool") if self.cnt[e] > 0}
        for k, c in self.dma_cnt.items():
            allv[k] = 16 * c
        for e in self.ENG:
            ws = self._waits(e, allv)
            if ws:
                self.stream[e].append((ws, None, []))

    def replay(self, e, eng, sems):
        for ws, fn, incs in self.stream[e]:
            for k, v in ws:
                eng.wait_ge(sems[k], v)
            if fn is None:
                continue
            ins = fn(eng)
            for k, v in incs:
                ins = ins.then_inc(sems[k], v)


def build(cfg):
    D, DFF, NT, NQ = cfg.D, cfg.DFF, cfg.NT, cfg.NQ
    KC, FC, H, GD, CG, NS, NKB = cfg.KC, cfg.FC, cfg.H, cfg.GD, cfg.CG, cfg.NS, cfg.NKB
    OC, NPASS, NOG, OW = cfg.OC, cfg.NPASS, cfg.NOG, cfg.OW
    scale = 128 ** -0.5
    lam_init = 0.8 - 0.6 * math.exp(-0.3 * 1)

    nc = bass.Bass("TRN2", target_bir_lowering=False)

    def din(name, shape, dt=F32):
        return nc.dram_tensor(name, list(shape), dt, kind="ExternalInput").ap()

    def dscr(name, shape, dt):
        return nc.dram_tensor(name, list(shape), dt, kind="Internal").ap()

    xt_d = din("xt", [NS * 512, D])
    halo_d = din("halo", [NS * 128, D])
    bandm_d = din("bandm", [NS * 128, 4 * 4 * 128], BF16)
    bandh_d = din("bandh", [NS * 64, 2 * 4 * 128], BF16)
    tabs_d = din("tabs", [NQ * 128, 3 * NKB])
    base0_d = din("base0", [128, 512])
    w0_d = din("w0", [128, 896])
    ident_d = din("ident", [128, 128], BF16)
    ones_d = din("ones", [128, 128], BF16)
    gcols_d = din("gcols", [128, 4 * KC])
    gfin_d = din("gfin", [128, D])
    pscb_d = din("pscb", [128, D])
    subg_d = din("subg", [128, 2])
    lamb_d = din("lamb", [128, 4 * 128])
    poolw_d = din("pool_w", [4 * GD, GD])
    wqkv_d = din("w_qkv", [D, 3 * D])
    wo_d = din("w_o", [D, D])
    wg_d = din("w_gate", [2 * D, DFF])
    wu_d = din("w_up", [2 * D, DFF])
    wd_d = din("w_down", [2 * DFF, D])
    y_d = nc.dram_tensor("y", [NQ * 512, D], F32, kind="ExternalOutput").ap()

    wqkv_b = dscr("wqkv_b", [D, 3 * D], BF16)
    wo_b = dscr("wo_b", [D, D], BF16)
    wg_b = dscr("wg_b", [2 * D, DFF], BF16)
    wu_b = dscr("wu_b", [2 * D, DFF], BF16)
    wd_b = dscr("wd_b", [2 * DFF, D], BF16)
    kt_s = dscr("kt_s", [H * 2 * 128, NS * 512], BF16)
    v_s = dscr("v_s", [NS * 512, D], BF16)
    qt_s = dscr("qt_s", [H * 2 * 128, NQ * 512], BF16)
    h1_s = dscr("h1_s", [NQ * 512, D], F32)

    tr = Tracker()
    es = ExitStack()

    def sb(name, shape, dt):
        return es.enter_context(nc.sbuf_tensor("sb_" + name, list(shape), dt))

    def ps(name):
        return es.enter_context(nc.psum_tensor(name, [128, 512], F32))

    xt = sb("xt", [128, 4, D], F32)
    hn = sb("hn", [128, 4, D], BF16)
    cT = sb("cT", [128, KC, 512], BF16)
    ARW = max(FC * 512, 22 * 1024)
    arena = sb("arena", [128, ARW], BF16)
    slabs = [sb("slab%d" % i, [128, KC, 256], BF16) for i in range(4)]
    wdc = [sb("wdc%d" % i, [128, 2, OC], BF16) for i in range(2)]
    sg = [sb("sg%d" % i, [128, 512], F32) for i in range(2)]
    PHW = 18 * 1024
    parena = sb("parena", [128, PHW], BF16)
    ident = sb("ident_sb", [128, 128], BF16)
    ones = sb("ones_sb", [128, 128], BF16)
    gcols = sb("gcols_sb", [128, 4 * KC], F32)
    subg = sb("subg_sb", [128, 2], F32)
    small = sb("small", [128, 32], F32)
    lamc = sb("lamc", [128, 8], F32)
    pb = [ps("ps%d" % i) for i in range(8)]

    def carve(base_ap_t, off, n, dt, shape=None):
        a = base_ap_t[:, off:off + n]
        if dt == F32:
            a = a.bitcast(F32)
        return a

    actT = arena[:, 0:FC * 512].rearrange("p (f t) -> p f t", f=FC)
    o = [0]

    def acarve(nbf, dt):
        a = carve(arena, o[0], nbf, dt)
        o[0] += nbf
        return a
    hl = arena[0:64, 0:4 * D].bitcast(F32).rearrange("p (a c) -> p a c", a=2)
    QT = [acarve(1024, BF16).rearrange("p (j t) -> p j t", j=2) for _ in range(2)]
    KTc = [acarve(1024, BF16).rearrange("p (j t) -> p j t", j=2) for _ in range(2)]
    Vc = [acarve(1024, BF16).rearrange("p (b e) -> p b e", b=4) for _ in range(2)]
    Sp = [acarve(1024, F32) for _ in range(2)]
    PT = [[acarve(512, BF16) for _ in range(2)] for _ in range(2)]
    Rr = [acarve(1024, F32) for _ in range(2)]
    of = acarve(2048, F32).rearrange("p (e t) -> p e t", e=2)
    t1 = acarve(1024, F32)
    sq = acarve(1024, BF16).rearrange("p (e t) -> p e t", e=2)
    rs = acarve(1024, F32)
    assert o[0] <= ARW

    po = [0]

    def pcarve(nbf, dt, parts=128):
        a = parena[0:parts, po[0]:po[0] + nbf]
        if dt == F32:
            a = a.bitcast(F32)
        po[0] += nbf
        return a
    hnh = pcarve(2 * D, BF16, 64).rearrange("p (a c) -> p a c", a=2)
    pw = pcarve(4 * CG * GD, BF16).rearrange("p (g k d) -> p g k d", g=4, k=CG)
    bandm = pcarve(4 * 4 * 128, BF16).rearrange("p (s g t) -> p s g t", s=4, g=4)
    bandh = pcarve(2 * 4 * 128, BF16, 64).rearrange("p (a g t) -> p a g t", a=2, g=4)
    ktsb = [pcarve(512, BF16) for _ in range(2)]
    vsb = [pcarve(1024, BF16).rearrange("p (s e) -> p s e", s=4) for _ in range(2)]
    assert po[0] <= PHW, po[0]
    po[0] = 0
    base0 = pcarve(1024, F32)
    w0 = pcarve(1792, F32)
    tabs = pcarve(2 * 3 * NKB, F32).rearrange("p (a k) -> p a k", a=3)
    biasrow = pcarve(2 * NKB, F32)
    sgnrow = pcarve(2 * NKB, F32)
    gfin = pcarve(2 * D, F32)
    assert po[0] <= PHW, po[0]

    B = {}

    def bf(name):
        if name not in B:
            B[name] = Buf(name)
        return B[name]

    PB = [bf("psum%d" % i) for i in range(8)]

    def mm(out, lhsT, rhs, start, stop, reads, wbuf, last):
        tr.op("pe", lambda e: e.matmul(out, lhsT, rhs, start=start, stop=stop),
              reads=reads, writes=[wbuf], inc=last)

    def tp(out, in_, reads, wbuf, last):
        tr.op("pe", lambda e: e.transpose(out, in_, ident[:, :]), reads=reads, writes=[wbuf], inc=last)

    WB = bf("wbf16")
    for src, dst in ((wqkv_d, wqkv_b), (wo_d, wo_b), (wg_d, wg_b), (wu_d, wu_b), (wd_d, wd_b)):
        R = src.shape[0]
        step = 256
        for r0 in range(0, R, step):
            tr.dma("pool", (lambda s_, d_, r0_: (lambda e: e.dma_start(out=d_[r0_:r0_ + step, :], in_=s_[r0_:r0_ + step, :])))(src, dst, r0),
                   "s_wconv", writes=[WB])
    CB = bf("consts")
    for dst, src in ((ident[:, :], ident_d), (ones[:, :], ones_d), (gcols[:, :], gcols_d), (subg[:, :], subg_d)):
        tr.dma("sp", (lambda d_, s_: (lambda e: e.dma_start(out=d_, in_=s_[:, :])))(dst, src), "s_const", writes=[CB])
    pwtmp = arena[:, 0:2 * 4 * CG * GD].bitcast(F32).rearrange("p (g k d) -> p g k d", g=4, k=CG)
    psc_t = xt[:, 0, :]
    AR = bf("arena")
    XT = bf("xt")
    PW = bf("pw")
    tr.dma("sp", lambda e: e.dma_start(out=pwtmp, in_=poolw_d.rearrange("(g k p) d -> p g k d", g=4, k=CG)), "s_arena", writes=[AR])
    tr.dma("sp", lambda e: e.dma_start(out=psc_t, in_=pscb_d[:, :]), "s_xt", writes=[XT])
    for g in range(4):
        for k in range(CG):
            tr.op("dve", (lambda g_, k_: (lambda e: e.tensor_tensor(out=pw[:, g_, k_, :], in0=pwtmp[:, g_, k_, :], in1=psc_t[:, g_ * GD:(g_ + 1) * GD], op=ALU.mult)))(g, k),
                  reads=[AR, XT], writes=[PW])
    lamt = xt[:, 1, 0:512]
    lamp = xt[:, 2, 0:256]
    LM = bf("lamtmp")
    SM = bf("small")
    tr.dma("sp", lambda e: e.dma_start(out=lamt, in_=lamb_d[:, :]), "s_lam", writes=[LM])
    tr.op("dve", lambda e: e.tensor_tensor(out=lamp[:, 0:128], in0=lamt[:, 0:128], in1=lamt[:, 128:256], op=ALU.mult), reads=[LM], writes=[LM])
    tr.op("dve", lambda e: e.tensor_tensor(out=lamp[:, 128:256], in0=lamt[:, 256:384], in1=lamt[:, 384:512], op=ALU.mult), reads=[LM], writes=[LM])
    tr.op("dve", lambda e: e.reduce_sum(out=lamc[:, 0:1], in_=lamp[:, 0:128], axis=AX.X), reads=[LM], writes=[SM])
    tr.op("dve", lambda e: e.reduce_sum(out=lamc[:, 1:2], in_=lamp[:, 128:256], axis=AX.X), reads=[LM], writes=[SM])
    tr.op("act", lambda e: e.activation(out=lamc[:, 2:4], in_=lamc[:, 0:2], func=AF.Exp), reads=[SM], writes=[SM])
    tr.op("dve", lambda e: e.tensor_tensor(out=lamc[:, 4:5], in0=lamc[:, 2:3], in1=lamc[:, 3:4], op=ALU.subtract), reads=[SM], writes=[SM])
    tr.op("dve", lambda e: e.tensor_scalar(out=lamc[:, 5:6], in0=lamc[:, 4:5], scalar1=lam_init, scalar2=None, op0=ALU.add), reads=[SM], writes=[SM])
    lam_col = lamc[:, 5:6]
    tr.op("dve", lambda e: e.tensor_scalar(out=subg[:, :], in0=subg[:, :], scalar1=(1.0 - lam_init), scalar2=None, op0=ALU.mult), reads=[CB], writes=[CB])
    tr.barrier()

    HN = bf("hn")
    CT = bf("cT")
    SL = [bf("slab%d" % i) for i in range(4)]
    WDC = [bf("wdc%d" % i) for i in range(2)]
    SG = [bf("sg%d" % i) for i in range(2)]
    slab_i = [0]
    wdc_i = [0]
    sg_i = [0]
    pbi = [0]

    def next_pb():
        i = pbi[0] % 8
        pbi[0] += 1
        return i

    def rms_stats(col0, nsub=4):
        tr.op("dve", lambda e: e.memset(small[:, col0:col0 + nsub], 0.0), writes=[SM])
        for s in range(nsub):
            tr.op("act", (lambda s_: (lambda e: e.activation(out=hn[:, s_, :], in_=xt[:, s_, :], func=AF.Square, accum_out=small[:, col0 + s_:col0 + s_ + 1])))(s),
                  reads=[XT], writes=[HN, SM])
        tr.op("dve", lambda e: e.tensor_scalar(out=small[:, col0:col0 + nsub], in0=small[:, col0:col0 + nsub], scalar1=1.0 / D, scalar2=EPS, op0=ALU.mult, op1=ALU.add), reads=[SM], writes=[SM])
        tr.op("act", lambda e: e.activation(out=small[:, col0:col0 + nsub], in_=small[:, col0:col0 + nsub], func=AF.Sqrt), reads=[SM], writes=[SM])
        tr.op("dve", lambda e: e.reciprocal(out=small[:, col0:col0 + nsub], in_=small[:, col0:col0 + nsub]), reads=[SM], writes=[SM])

    def make_hn(col0):
        for s in range(4):
            tr.op("dve", (lambda s_: (lambda e: e.tensor_scalar(out=hn[:, s_, :], in0=xt[:, s_, :], scalar1=small[:, col0 + s_:col0 + s_ + 1], scalar2=None, op0=ALU.mult)))(s),
                  reads=[XT, SM], writes=[HN])

    def transpose_hn(gi):
        for k in range(KC):
            b = next_pb()
            pv = pb[b][:, :].bitcast(BF16)
            for s in range(4):
                tp(pv[:, s * 128:(s + 1) * 128], hn[:, s, k * 128:(k + 1) * 128], [HN, CB], PB[b], last=(s == 3))
            eng = "act" if k % 2 == 0 else "dve"
            if eng == "act":
                tr.op("act", (lambda k_, pv_: (lambda e: e.activation(out=cT[:, k_, :], in_=pv_[:, 0:512], func=AF.Copy, scale=gcols[:, gi * KC + k_:gi * KC + k_ + 1])))(k, pv),
                      reads=[PB[b], CB], writes=[CT])
            else:
                tr.op("dve", (lambda k_, pv_: (lambda e: e.tensor_scalar(out=cT[:, k_, :], in0=pv_[:, 0:512], scalar1=gcols[:, gi * KC + k_:gi * KC + k_ + 1], scalar2=None, op0=ALU.mult)))(k, pv),
                      reads=[PB[b], CB], writes=[CT])

    def load_slab(src_ap, c0):
        i = slab_i[0] % 4
        slab_i[0] += 1
        tr.dma("sp", (lambda i_: (lambda e: e.dma_start(out=slabs[i_][:, :, :], in_=src_ap[:, c0:c0 + 256].rearrange("(k p) f -> p k f", p=128))))(i),
               "s_slab%d" % i, reads=[WB], writes=[SL[i]])
        return i

    AT = bf("actT")

    def ffn(layer, gi, col0):
        rms_stats(col0)
        make_hn(col0)
        transpose_hn(gi)
        wg_l = wg_b[layer * D:(layer + 1) * D, :]
        wu_l = wu_b[layer * D:(layer + 1) * D, :]
        wd_l = wd_b[layer * DFF:(layer + 1) * DFF, :]
        nfg = DFF // 256
        pend = None
        nxt = (load_slab(wg_l, 0), load_slab(wu_l, 0))
        for fg in range(nfg):
            cur = nxt
            if fg + 1 < nfg:
                nxt = (load_slab(wg_l, (fg + 1) * 256), load_slab(wu_l, (fg + 1) * 256))
            for fc in range(2):
                f = fg * 2 + fc
                bg, bu = next_pb(), next_pb()
                for (si, b) in ((cur[0], bg), (cur[1], bu)):
                    for k in range(KC):
                        mm(pb[b][:, :], slabs[si][:, k, fc * 128:(fc + 1) * 128], cT[:, k, :], k == 0, k == KC - 1,
                           [SL[si], CT], PB[b], last=(k == KC - 1))
                gi_ = sg_i[0] % 2
                sg_i[0] += 1
                tr.op("act", (lambda b_, g_: (lambda e: e.activation(out=sg[g_][:, :], in_=pb[b_][:, :], func=AF.Silu)))(bg, gi_),
                      reads=[PB[bg]], writes=[SG[gi_]])
                tr.op("dve", (lambda b_, g_, f_: (lambda e: e.tensor_tensor(out=actT[:, f_, :], in0=sg[g_][:, :], in1=pb[b_][:, :], op=ALU.mult)))(bu, gi_, f),
                      reads=[SG[gi_], PB[bu]], writes=[AT])
        for p_ in range(NPASS):
            banks = [[next_pb() for _ in range(NOG)] for _ in range(4)]
            nch = FC // 2
            def ld(c):
                i = wdc_i[0] % 2
                wdc_i[0] += 1
                tr.dma("sp", (lambda i_, c_, pp_: (lambda e: e.dma_start(out=wdc[i_][:, :, :], in_=wd_l[c_ * 256:(c_ + 1) * 256, pp_ * OC:(pp_ + 1) * OC].rearrange("(a p) d -> p a d", p=128))))(i, c, p_),
                       "s_wdc%d" % i, reads=[WB], writes=[WDC[i]])
                return i
            nx = ld(0)
            for c in range(nch):
                cu = nx
                if c + 1 < nch:
                    nx = ld(c + 1)
                for a in range(2):
                    f = c * 2 + a
                    for s in range(4):
                        for og in range(NOG):
                            b = banks[s][og]
                            mm(pb[b][:, 0:OW], actT[:, f, s * 128:(s + 1) * 128], wdc[cu][:, a, og * OW:(og + 1) * OW], f == 0, f == FC - 1,
                               [AT, WDC[cu]], PB[b], last=(f == FC - 1 or (a == 1 and s == 3 and og == NOG - 1)))
            for s in range(4):
                for og in range(NOG):
                    b = banks[s][og]
                    c0 = p_ * OC + og * OW
                    tr.op("dve", (lambda s_, b_, c0_: (lambda e: e.tensor_tensor(out=xt[:, s_, c0_:c0_ + OW], in0=xt[:, s_, c0_:c0_ + OW], in1=pb[b_][:, 0:OW], op=ALU.add)))(s, b, c0),
                          reads=[XT, PB[b]], writes=[XT])

    HL = AR
    BD = bf("band")
    HH = bf("hnh")
    KSB = [bf("ktsb%d" % i) for i in range(2)]
    VSB = [bf("vsb%d" % i) for i in range(2)]
    KTS, VS, QTS, H1S = bf("kt_s"), bf("v_s"), bf("qt_s"), bf("h1_s")
    ksb_i = [0]
    vsb_i = [0]
    for slot in range(NS):
        isq = slot < NQ
        tr.dma("sp", (lambda sl: (lambda e: e.dma_start(out=xt[:, :, :], in_=xt_d[sl * 512:(sl + 1) * 512, :].rearrange("(s p) c -> p s c", p=128))))(slot), "s_xt", writes=[XT])
        tr.dma("sp", (lambda sl: (lambda e: e.dma_start(out=hl, in_=halo_d[sl * 128:(sl + 1) * 128, :].rearrange("(a p) c -> p a c", p=64))))(slot), "s_arena", writes=[AT])
        tr.dma("sp", (lambda sl: (lambda e: e.dma_start(out=bandm, in_=bandm_d[sl * 128:(sl + 1) * 128, :].rearrange("p (s g t) -> p s g t", s=4, g=4))))(slot), "s_band", writes=[BD])
        tr.dma("sp", (lambda sl: (lambda e: e.dma_start(out=bandh, in_=bandh_d[sl * 64:(sl + 1) * 64, :].rearrange("p (a g t) -> p a g t", a=2, g=4))))(slot), "s_band", writes=[BD])
        rms_stats(0)
        tr.op("dve", lambda e: e.memset(small[0:64, 4:6], 0.0), writes=[SM])
        for a in range(2):
            tr.op("act", (lambda a_: (lambda e: e.activation(out=hnh[:, a_, :], in_=hl[:, a_, :], func=AF.Square, accum_out=small[0:64, 4 + a_:5 + a_])))(a),
                  reads=[AT], writes=[HH, SM])
        tr.op("dve", lambda e: e.tensor_scalar(out=small[0:64, 4:6], in0=small[0:64, 4:6], scalar1=1.0 / D, scalar2=EPS, op0=ALU.mult, op1=ALU.add), reads=[SM], writes=[SM])
        tr.op("act", lambda e: e.activation(out=small[0:64, 4:6], in_=small[0:64, 4:6], func=AF.Sqrt), reads=[SM], writes=[SM])
        tr.op("dve", lambda e: e.reciprocal(out=small[0:64, 4:6], in_=small[0:64, 4:6]), reads=[SM], writes=[SM])
        make_hn(0)
        for a in range(2):
            tr.op("dve", (lambda a_: (lambda e: e.tensor_scalar(out=hnh[:, a_, :], in0=hl[:, a_, :], scalar1=small[0:64, 4 + a_:5 + a_], scalar2=None, op0=ALU.mult)))(a),
                  reads=[AT, SM], writes=[HH])
        for k in range(KC):
            g = k // CG
            b = next_pb()
            for s in range(4):
                a, hp = s // 2, 32 * (s % 2)
                mm(pb[b][:, s * 128:(s + 1) * 128], hn[:, s, k * 128:(k + 1) * 128], bandm[:, s, g, :], True, False, [HN, BD], PB[b], last=False)
                mm(pb[b][:, s * 128:(s + 1) * 128], hnh[hp:hp + 32, a, k * 128:(k + 1) * 128], bandh[hp:hp + 32, a, g, :], False, True, [HH, BD], PB[b], last=(s == 3))
            if k % 2 == 0:
                tr.op("act", (lambda k_, b_: (lambda e: e.activation(out=cT[:, k_, :], in_=pb[b_][:, :], func=AF.Copy, scale=gcols[:, k_:k_ + 1])))(k, b),
                      reads=[PB[b], CB], writes=[CT])
            else:
                tr.op("dve", (lambda k_, b_: (lambda e: e.tensor_scalar(out=cT[:, k_, :], in0=pb[b_][:, :], scalar1=gcols[:, k_:k_ + 1], scalar2=None, op0=ALU.mult)))(k, b),
                      reads=[PB[b], CB], writes=[CT])
        for s in range(4):
            for g in range(4):
                b = next_pb()
                for kk in range(CG):
                    mm(pb[b][:, 0:GD], cT[:, g * CG + kk, s * 128:(s + 1) * 128], pw[:, g, kk, :], kk == 0, kk == CG - 1, [CT, PW], PB[b], last=(kk == CG - 1))
                tr.op("dve", (lambda s_, g_, b_: (lambda e: e.tensor_tensor(out=xt[:, s_, g_ * GD:(g_ + 1) * GD], in0=xt[:, s_, g_ * GD:(g_ + 1) * GD], in1=pb[b_][:, 0:GD], op=ALU.add)))(s, g, b),
                      reads=[XT, PB[b]], writes=[XT])
        ffn(0, 1, 8)
        if isq:
            tr.dma("pool", (lambda sl: (lambda e: e.dma_start(out=h1_s[sl * 512:(sl + 1) * 512, :].rearrange("(s p) c -> p s c", p=128), in_=xt[:, :, :])))(slot),
                   "s_h1st", reads=[XT], writes=[H1S])
        rms_stats(12)
        make_hn(12)
        transpose_hn(2)
        jobs = []
        for h in range(H):
            jobs.append(("k", h))
            jobs.append(("v", h))
            if isq:
                jobs.append(("q", h))

        def jcol(job):
            kind, h = job
            return {"q": 0, "k": D, "v": 2 * D}[kind] + h * 256
        nxs = load_slab(wqkv_b, jcol(jobs[0]))
        for ji, job in enumerate(jobs):
            cs = nxs
            if ji + 1 < len(jobs):
                nxs = load_slab(wqkv_b, jcol(jobs[ji + 1]))
            kind, h = job
            if kind in ("k", "q"):
                for j in range(2):
                    b = next_pb()
                    for k in range(KC):
                        mm(pb[b][:, :], slabs[cs][:, k, j * 128:(j + 1) * 128], cT[:, k, :], k == 0, k == KC - 1, [SL[cs], CT], PB[b], last=(k == KC - 1))
                    ki = ksb_i[0] % 2
                    ksb_i[0] += 1
                    tr.op("act", (lambda b_, ki_: (lambda e: e.activation(out=ktsb[ki_], in_=pb[b_][:, :], func=AF.Copy)))(b, ki), reads=[PB[b]], writes=[KSB[ki]])
                    dst, DB, ncol = (kt_s, KTS, NS * 512) if kind == "k" else (qt_s, QTS, NQ * 512)
                    r0 = (h * 2 + j) * 128
                    tr.dma("pool", (lambda dst_, r0_, sl, ki_: (lambda e: e.dma_start(out=dst_[r0_:r0_ + 128, sl * 512:(sl + 1) * 512], in_=ktsb[ki_])))(dst, r0, slot, ki),
                           "s_kst%d" % ki, reads=[KSB[ki]], writes=[DB])
            else:
                vi = vsb_i[0] % 2
                vsb_i[0] += 1
                for s in range(4):
                    b = next_pb()
                    for k in range(KC):
                        mm(pb[b][:, 0:256], cT[:, k, s * 128:(s + 1) * 128], slabs[cs][:, k, :], k == 0, k == KC - 1, [CT, SL[cs]], PB[b], last=(k == KC - 1))
                    tr.op("dve", (lambda b_, vi_, s_: (lambda e: e.tensor_copy(out=vsb[vi_][:, s_, :], in_=pb[b_][:, 0:256])))(b, vi, s), reads=[PB[b]], writes=[VSB[vi]])
                tr.dma("pool", (lambda sl, h_, vi_: (lambda e: e.dma_start(out=v_s[sl * 512:(sl + 1) * 512, h_ * 256:(h_ + 1) * 256].rearrange("(s p) e -> p s e", p=128), in_=vsb[vi_])))(slot, h, vi),
                       "s_vst%d" % vi, reads=[VSB[vi]], writes=[VS])
    tr.barrier()

    QC = bf("qconst")
    for dst, src in ((base0, base0_d), (w0, w0_d), (gfin, gfin_d)):
        tr.dma("sp", (lambda d_, s_: (lambda e: e.dma_start(out=d_, in_=s_[:, :])))(dst, src), "s_const", writes=[QC])
    TB = bf("tabs")
    ROW = bf("rows")
    QTB = [bf("QT%d" % i) for i in range(2)]
    KTB = [bf("KTc%d" % i) for i in range(2)]
    VCB = [bf("Vc%d" % i) for i in range(2)]
    SPB = [bf("Sp%d" % i) for i in range(2)]
    PTB = [[bf("PT%d_%d" % (j, i)) for i in range(2)] for j in range(2)]
    FIN = bf("fin")
    qt_i = [0]
    kv_i = [0]
    pt_i = [0]
    slopes = [2.0 ** (-8.0 * (h + 1) / H) for h in range(H)]
    SKIP_TH = _SKIP_TH
    dmin = _min_dist(cfg)
    for qi in range(NQ):
        tr.barrier()
        tr.dma("sp", (lambda q_: (lambda e: e.dma_start(out=xt[:, :, :], in_=h1_s[q_ * 512:(q_ + 1) * 512, :].rearrange("(s p) c -> p s c", p=128))))(qi), "s_xt", reads=[H1S], writes=[XT])
        tr.dma("sp", (lambda q_: (lambda e: e.dma_start(out=tabs, in_=tabs_d[q_ * 128:(q_ + 1) * 128, :].rearrange("p (a k) -> p a k", a=3))))(qi), "s_tabs", writes=[TB])
        for h in range(H):
            fh = -slopes[h] / scale
            tr.op("dve", (lambda h_: (lambda e: e.scalar_tensor_tensor(out=biasrow, in0=tabs[:, 1, :], scalar=-slopes[h_], in1=tabs[:, 2, :], op0=ALU.mult, op1=ALU.add)))(h), reads=[TB], writes=[ROW])
            tr.op("dve", (lambda f_: (lambda e: e.tensor_scalar(out=sgnrow, in0=tabs[:, 0, :], scalar1=f_, scalar2=None, op0=ALU.mult)))(fh), reads=[TB], writes=[ROW])
            qb_ = qt_i[0] % 2
            qt_i[0] += 1
            tr.dma("sp", (lambda h_, q_, b_: (lambda e: e.dma_start(out=QT[b_], in_=qt_s[h_ * 256:(h_ + 1) * 256, q_ * 512:(q_ + 1) * 512].rearrange("(j p) t -> p j t", p=128))))(h, qi, qb_),
                   "s_qt%d" % qb_, reads=[QTS], writes=[QTB[qb_]])
            nchunk = (NKB + 3) // 4
            need = [kb for kb in range(NKB) if slopes[h] * dmin[qi, kb] < SKIP_TH]
            plan = []
            for c in range(nchunk):
                bis = [kb - 4 * c for kb in need if kb // 4 == c]
                if bis:
                    plan.append((c, bis))
            last_need = need[-1]

            def ldkv(c):
                i = kv_i[0] % 2
                kv_i[0] += 1
                nb = min(4, NKB - 4 * c)
                tr.dma("sp", (lambda h_, c_, i_, nb_: (lambda e: e.dma_start(out=KTc[i_][:, :, 0:nb_ * 128], in_=kt_s[h_ * 256:(h_ + 1) * 256, c_ * 512:c_ * 512 + nb_ * 128].rearrange("(j p) t -> p j t", p=128))))(h, c, i, nb),
                       "s_kt%d" % i, reads=[KTS], writes=[KTB[i]])
                tr.dma("sp", (lambda h_, c_, i_, nb_: (lambda e: e.dma_start(out=Vc[i_][:, 0:nb_, :], in_=v_s[c_ * 512:c_ * 512 + nb_ * 128, h_ * 256:(h_ + 1) * 256].rearrange("(b p) e -> p b e", p=128))))(h, c, i, nb),
                       "s_vc%d" % i, reads=[VS], writes=[VCB[i]])
                return i
            nxkv = ldkv(plan[0][0])
            first = True
            pend_pv = [None]
            for pi_, (c, bis) in enumerate(plan):
                ckv = nxkv
                for bn, bi in enumerate(bis):
                    kb = 4 * c + bi
                    lastkb = (kb == last_need)
                    diag = (kb // 4 == qi) and kb < 4 * NT
                    pts = []
                    for j in range(2):
                        mm(pb[j][:, :], KTc[ckv][:, j, bi * 128:(bi + 1) * 128], QT[qb_][:, j, :], True, True, [KTB[ckv], QTB[qb_]], PB[j], last=(j == 1))
                    for j in range(2):
                        if diag:
                            o_ = kb % 4
                            tr.op("dve", (lambda j_, o__, f_: (lambda e: e.scalar_tensor_tensor(out=Sp[j_], in0=w0[:, 384 - 128 * o__:384 - 128 * o__ + 512], scalar=f_, in1=pb[j_][:, :], op0=ALU.mult, op1=ALU.add)))(j, o_, fh),
                                  reads=[QC, PB[j]], writes=[SPB[j]])
                        else:
                            tr.op("dve", (lambda j_, kb_: (lambda e: e.scalar_tensor_tensor(out=Sp[j_], in0=base0, scalar=sgnrow[:, kb_:kb_ + 1], in1=pb[j_][:, :], op0=ALU.mult, op1=ALU.add)))(j, kb),
                                  reads=[QC, ROW, PB[j]], writes=[SPB[j]])
                        pi = pt_i[0] % 2
                        tr.op("act", (lambda j_, kb_, pi_: (lambda e: e.activation(out=PT[j_][pi_], in_=Sp[j_], func=AF.Exp, bias=biasrow[:, kb_:kb_ + 1], scale=scale)))(j, kb, pi),
                              reads=[SPB[j], ROW], writes=[PTB[j][pi]])
                        pts.append(pi)
                    pt_i[0] += 1

                    def pv(pts=pts, ckv=ckv, bi=bi, first=first, lastkb=lastkb):
                        for j in range(2):
                            pi = pts[j]
                            for ec in range(2):
                                b = 2 + 2 * j + ec
                                mm(pb[b][:, :], Vc[ckv][:, bi, ec * 128:(ec + 1) * 128], PT[j][pi], first, lastkb, [VCB[ckv], PTB[j][pi]], PB[b], last=False)
                            mm(pb[6 + j][:, :], ones[:, :], PT[j][pi], first, lastkb, [CB, PTB[j][pi]], PB[6 + j], last=(j == 1))
                    if pend_pv[0] is not None:
                        pend_pv[0]()
                    pend_pv[0] = pv
                    first = False
                    if bn == 0 and pi_ + 1 < len(plan):
                        nxkv = ldkv(plan[pi_ + 1][0])
            pend_pv[0]()
            pend_pv[0] = None
            for j in range(2):
                tr.op("dve", (lambda j_: (lambda e: e.reciprocal(out=Rr[j_], in_=pb[6 + j_][:, :])))(j), reads=[PB[6 + j]], writes=[FIN])
            tr.op("dve", lambda e: e.tensor_scalar(out=Rr[1], in0=Rr[1], scalar1=lam_col, scalar2=None, op0=ALU.mult), reads=[FIN, SM], writes=[FIN])
            for ec in range(2):
                tr.op("dve", (lambda ec_: (lambda e: e.tensor_tensor(out=of[:, ec_, :], in0=pb[2 + ec_][:, :], in1=Rr[0], op=ALU.mult)))(ec), reads=[PB[2 + ec], FIN], writes=[FIN])
                tr.op("dve", (lambda ec_: (lambda e: e.tensor_tensor(out=t1, in0=pb[4 + ec_][:, :], in1=Rr[1], op=ALU.mult)))(ec), reads=[PB[4 + ec], FIN], writes=[FIN])
                tr.op("dve", (lambda ec_: (lambda e: e.tensor_tensor(out=of[:, ec_, :], in0=of[:, ec_, :], in1=t1, op=ALU.subtract)))(ec), reads=[FIN], writes=[FIN])
                tr.op("act", (lambda ec_: (lambda e: e.activation(out=sq[:, ec_, :], in_=of[:, ec_, :], func=AF.Square)))(ec), reads=[FIN], writes=[FIN])
            for ec in range(2):
                mm(pb[0][:, :], ones[:, :], sq[:, ec, :], ec == 0, ec == 1, [CB, FIN], PB[0], last=(ec == 1))
            tr.op("dve", lambda e: e.tensor_scalar(out=rs, in0=pb[0][:, :], scalar1=1.0 / 256.0, scalar2=EPS, op0=ALU.mult, op1=ALU.add), reads=[PB[0]], writes=[FIN])
            tr.op("act", lambda e: e.activation(out=rs, in_=rs, func=AF.Sqrt), reads=[FIN], writes=[FIN])
            tr.op("dve", lambda e: e.reciprocal(out=rs, in_=rs), reads=[FIN], writes=[FIN])
            for ec in range(2):
                tr.op("dve", (lambda ec_, h_: (lambda e: e.scalar_tensor_tensor(out=cT[:, 2 * h_ + ec_, :], in0=of[:, ec_, :], scalar=subg[:, ec_:ec_ + 1], in1=rs, op0=ALU.mult, op1=ALU.mult)))(ec, h),
                      reads=[FIN, CB], writes=[CT])
        tr.barrier()
        for p_ in range(NPASS):
            banks = [[next_pb() for _ in range(NOG)] for _ in range(4)]
            nch = KC // 2

            def ldo(c):
                i = wdc_i[0] % 2
                wdc_i[0] += 1
                tr.dma("sp", (lambda i_, c_, pp_: (lambda e: e.dma_start(out=wdc[i_][:, :, :], in_=wo_b[c_ * 256:(c_ + 1) * 256, pp_ * OC:(pp_ + 1) * OC].rearrange("(a p) d -> p a d", p=128))))(i, c, p_),
                       "s_wdc%d" % i, reads=[WB], writes=[WDC[i]])
                return i
            nx = ldo(0)
            for c in range(nch):
                cu = nx
                if c + 1 < nch:
                    nx = ldo(c + 1)
                for a in range(2):
                    k = c * 2 + a
                    for s in range(4):
                        for og in range(NOG):
                            b = banks[s][og]
                            mm(pb[b][:, 0:OW], cT[:, k, s * 128:(s + 1) * 128], wdc[cu][:, a, og * OW:(og + 1) * OW], k == 0, k == KC - 1,
                               [CT, WDC[cu]], PB[b], last=(k == KC - 1 or (a == 1 and s == 3 and og == NOG - 1)))
            for s in range(4):
                for og in range(NOG):
                    b = banks[s][og]
                    c0 = p_ * OC + og * OW
                    tr.op("dve", (lambda s_, b_, c0_: (lambda e: e.tensor_tensor(out=xt[:, s_, c0_:c0_ + OW], in0=xt[:, s_, c0_:c0_ + OW], in1=pb[b_][:, 0:OW], op=ALU.add)))(s, b, c0),
                          reads=[XT, PB[b]], writes=[XT])
        ffn(1, 3, 16)
        rms_stats(20)
        for s in range(4):
            tr.op("dve", (lambda s_: (lambda e: e.scalar_tensor_tensor(out=xt[:, s_, :], in0=xt[:, s_, :], scalar=small[:, 20 + s_:21 + s_], in1=gfin, op0=ALU.mult, op1=ALU.mult)))(s),
                  reads=[XT, SM, QC], writes=[XT])
        tr.dma("pool", (lambda q_: (lambda e: e.dma_start(out=y_d[q_ * 512:(q_ + 1) * 512, :].rearrange("(s p) c -> p s c", p=128), in_=xt[:, :, :])))(qi), "s_yst", reads=[XT], writes=[bf("y")])
    tr.barrier()

    sems = {}
    for k in sorted(tr.semkeys):
        sems[k] = es.enter_context(nc.semaphore(k))
    with nc.Block() as block:
        @block.tensor
        def _(e):
            tr.replay("pe", e, sems)

        @block.scalar
        def _(e):
            tr.replay("act", e, sems)

        @block.vector
        def _(e):
            tr.replay("dve", e, sems)

        @block.gpsimd
        def _(e):
            tr.replay("pool", e, sems)

        @block.sync
        def _(e):
            tr.replay("sp", e, sems)
    es.close()
    return nc


_band_cache = {}


def _band_for(pos_main, pos_halo, L):
    key = (tuple(pos_main.tolist()), tuple(pos_halo.tolist()), L)
    p0 = int(pos_main[0])
    rel = (tuple((pos_main - p0).tolist()) if p0 >= 0 else None, tuple(np.where(pos_halo >= 0, pos_halo - p0, -999).tolist()), min(p0, 40), min(L - p0, 300) if p0 >= 0 else -1)
    if rel in _band_cache:
        return _band_cache[rel]
    bm = np.zeros((128, 4, 128), np.float32)
    bh = np.zeros((32, 4, 128), np.float32)
    where = {}
    for i, p in enumerate(pos_main):
        if p >= 0:
            where[int(p)] = ("m", i)
    for i, p in enumerate(pos_halo):
        if p >= 0 and int(p) not in where:
            where[int(p)] = ("h", i)
    for r, p in enumerate(pos_main):
        if p < 0:
            continue
        p = int(p)
        for g, w in enumerate(WINDOWS):
            lo, hi = max(p - w // 2, 0), min(p + w // 2, L)
            inv = 1.0 / float(hi - lo)
            for u in range(lo, hi):
                kind, i = where[u]
                if kind == "m":
                    bm[i, g, r] += inv
                else:
                    bh[i, g, r] += inv
            bm[r, g, r] -= 1.0
    _band_cache[rel] = (bm, bh)
    return bm, bh


def _slot_positions(cfg, ntile, ctype):
    NT, NQ, NS = cfg.NT, cfg.NQ, cfg.NS
    half = ntile // 2
    tmap = [-1] * NT
    for i in range(half):
        tmap[i] = (ctype * half + i)
        tmap[NQ + i] = ((1 - ctype) * half + i)
    pos = -np.ones((NS, 512), np.int64)
    for sl in range(NT):
        if tmap[sl] >= 0:
            pos[sl] = N_META + 512 * tmap[sl] + np.arange(512)
    pos[NT, :N_META] = np.arange(N_META)
    return tmap, pos


def _min_dist(cfg):
    NT, NQ, NKB = cfg.NT, cfg.NQ, cfg.NKB
    dmin = np.full((NQ, NKB), np.inf)
    for ntile in (NT, NT // 2):
        for ctype in (0, 1):
            _, pos = _slot_positions(cfg, ntile, ctype)
            for qi in range(NQ):
                if pos[qi, 0] < 0:
                    continue
                q0, q1 = int(pos[qi, 0]), int(pos[qi, 0]) + 511
                for kb in range(NKB):
                    if kb == NKB - 1:
                        k0, k1 = 0, N_META - 1
                    else:
                        k0 = int(pos[kb // 4, (kb % 4) * 128])
                        if k0 < 0:
                            continue
                        k1 = k0 + 127
                    d = max(0, k0 - q1, q0 - k1)
                    dmin[qi, kb] = min(dmin[qi, kb], d)
    return dmin


def _core_layout(cfg, x_seq, meta, ctype):
    D, NT, NQ, NS, NKB = cfg.D, cfg.NT, cfg.NQ, cfg.NS, cfg.NKB
    S = x_seq.shape[0]
    L = S + N_META
    ntile = S // 512
    tmap, pos = _slot_positions(cfg, ntile, ctype)
    xt = np.zeros((NS * 512, D), np.float32)
    for sl in range(NT):
        if tmap[sl] >= 0:
            t = tmap[sl]
            xt[sl * 512:(sl + 1) * 512] = x_seq[512 * t:512 * (t + 1)]
    xt[NT * 512:NT * 512 + N_META] = meta

    def row_of(p):
        return meta[p] if p < N_META else x_seq[p - N_META]
    halo = np.zeros((NS, 2, 64, D), np.float32)
    bandm = np.zeros((NS, 128, 4, 4, 128), np.float32)
    bandh = np.zeros((NS, 64, 2, 4, 128), np.float32)
    for sl in range(NS):
        for s in range(4):
            pm = pos[sl, s * 128:(s + 1) * 128]
            ph = -np.ones(32, np.int64)
            if pm[0] >= 0:
                nvalid = int((pm >= 0).sum())
                p0, p1 = int(pm[0]), int(pm[0]) + nvalid
                cand = list(range(p0 - 8, p0)) + list(range(p1, p1 + 8))
                for i, p in enumerate(cand):
                    if 0 <= p < L:
                        ph[i] = p
                        halo[sl, s // 2, 32 * (s % 2) + i] = row_of(p)
                bm, bh = _band_for(pm, ph, L)
                bandm[sl, :, s] = bm
                bandh[sl, 32 * (s % 2):32 * (s % 2) + 32, s // 2] = bh
    tabs = np.zeros((NQ, 128, 3, NKB), np.float32)
    tabs[:, :, 0, :] = 1.0
    for qi in range(NQ):
        if pos[qi, 0] < 0:
            continue
        pq0 = int(pos[qi, 0])
        for kb in range(NKB):
            if kb == NKB - 1:
                tabs[qi, :, 0, kb] = 1.0
                tabs[qi, :, 1, kb] = pq0
                tabs[qi, N_META:, 2, kb] = NEG
                continue
            sl, sub = kb // 4, kb % 4
            if sl == qi:
                continue
            pk0 = int(pos[sl, sub * 128])
            if pk0 < 0:
                tabs[qi, :, 2, kb] = NEG
                continue
            dlt = pq0 - pk0
            tabs[qi, :, 0, kb] = 1.0 if dlt > 0 else -1.0
            tabs[qi, :, 1, kb] = abs(dlt)
    valid_q = [tmap[i] for i in range(NQ)]
    return dict(xt=xt, halo=halo.reshape(NS * 128, D),
                bandm=bandm.reshape(NS * 128, 2048).astype(NPBF),
                bandh=bandh.reshape(NS * 64, 1024).astype(NPBF),
                tabs=tabs.reshape(NQ * 128, 3 * NKB)), valid_q


def _shared_inputs(cfg, inp):
    D, KC = cfg.D, cfg.KC
    f = np.float32
    kk = np.arange(128)[:, None]
    base0 = (np.arange(512)[None, :] - kk).astype(f)
    w0 = np.abs(np.arange(896)[None, :] - 384 - kk).astype(f)
    gl = [inp["mixer_norm_g"][0], inp["ffn_norm_g"][0], inp["mixer_norm_g"][1], inp["ffn_norm_g"][1]]
    gcols = np.concatenate([np.asarray(g, f).reshape(KC, 128).T for g in gl], axis=1)
    lamb = np.concatenate([np.asarray(inp[k], f).reshape(1, 128) for k in ("lambda_q1", "lambda_k1", "lambda_q2", "lambda_k2")], axis=1)
    return dict(
        base0=np.ascontiguousarray(base0), w0=np.ascontiguousarray(w0),
        ident=np.eye(128, dtype=f).astype(NPBF), ones=np.ones((128, 128), f).astype(NPBF),
        gcols=np.ascontiguousarray(gcols),
        gfin=np.ascontiguousarray(np.broadcast_to(np.asarray(inp["final_norm_g"], f).reshape(1, D), (128, D))),
        pscb=np.ascontiguousarray(np.broadcast_to(np.asarray(inp["pool_scale"], f).reshape(1, D), (128, D))),
        subg=np.ascontiguousarray(np.asarray(inp["subln_g"], f).reshape(2, 128).T),
        lamb=np.ascontiguousarray(np.broadcast_to(lamb, (128, 512))),
        pool_w=np.ascontiguousarray(np.asarray(inp["pool_w"], f).reshape(4 * cfg.GD, cfg.GD)),
        w_qkv=np.ascontiguousarray(np.asarray(inp["w_qkv"], f).reshape(D, 3 * D)),
        w_o=np.ascontiguousarray(np.asarray(inp["w_o"], f).reshape(D, D)),
        w_gate=np.ascontiguousarray(np.asarray(inp["w_gate"], f).reshape(2 * D, cfg.DFF)),
        w_up=np.ascontiguousarray(np.asarray(inp["w_up"], f).reshape(2 * D, cfg.DFF)),
        w_down=np.ascontiguousarray(np.asarray(inp["w_down"], f).reshape(2 * cfg.DFF, D)),
    )


def run(cfg, inp):
    xp = np.asarray(inp["x_prompt"], np.float32)
    xs = np.asarray(inp["x_sample"], np.float32)
    meta = np.asarray(inp["meta_tokens"], np.float32)
    shared = _shared_inputs(cfg, inp)
    seqs = [xs[0], xs[1], xp[0], xp[1]]
    in_maps, vq = [], []
    for c in range(8):
        lay, valid_q = _core_layout(cfg, seqs[c // 2], meta, c % 2)
        m = dict(shared)
        m.update(lay)
        in_maps.append(m)
        vq.append(valid_q)
    nc = build(cfg)
    res = run_bass_kernel_spmd(nc, in_maps, core_ids=list(range(8)))
    yp = np.zeros_like(xp)
    ys = np.zeros_like(xs)
    outs = [ys[0], ys[1], yp[0], yp[1]]
    for c in range(8):
        y = np.asarray(res.results[c]["y"], dtype=np.float32)
        for qi, t in enumerate(vq[c]):
            if t >= 0:
                outs[c // 2][512 * t:512 * (t + 1)] = y[512 * qi:512 * (qi + 1)]
    return yp, ys


def kernel(**inputs):
    cfg = Cfg()
    return run(cfg, inputs)
```

```python
import math
from contextlib import ExitStack
import numpy as np
import ml_dtypes
import concourse.bass as bass
import concourse.mybir as mybir
from concourse.bass_utils import run_bass_kernel_spmd

F32 = mybir.dt.float32
BF16 = mybir.dt.bfloat16
ALU = mybir.AluOpType
AF = mybir.ActivationFunctionType
AX = mybir.AxisListType
NPBF = ml_dtypes.bfloat16

N_META = 16
WINDOWS = (2, 4, 8, 16)
EPS = 1e-6
NEG = -30000.0
_SKIP_TH = 60.0


class Cfg:
    def __init__(self, D=2048, DFF=5632, NT=32, NQ=16):
        self.D, self.DFF, self.NT, self.NQ = D, DFF, NT, NQ
        self.KC = D // 128
        self.FC = DFF // 128
        self.H = D // 256
        self.GD = D // 4
        self.CG = self.GD // 128
        self.NS = NT + 1
        self.NKB = 4 * NT + 1
        self.OC = min(D, 1024)
        self.NPASS = D // self.OC
        self.NOG = self.OC // 512 if self.OC >= 512 else 1
        self.OW = min(512, self.OC)


class Buf:
    def __init__(self, name):
        self.name = name
        self.w = None
        self.r = {}


class Tracker:
    ENG = ("pe", "act", "dve", "pool", "sp")

    def __init__(self):
        self.stream = {e: [] for e in self.ENG}
        self.cnt = {e: 0 for e in self.ENG}
        self.waited = {e: {} for e in self.ENG}
        self.dma_cnt = {}
        self.semkeys = set("prog_" + e for e in ("pe", "act", "dve", "pool"))

    def _deps(self, reads, writes):
        deps = {}

        def add(tok):
            if tok is None:
                return
            k, v = tok
            if deps.get(k, 0) < v:
                deps[k] = v
        for b in reads:
            add(b.w)
        for b in writes:
            add(b.w)
            for k, v in b.r.items():
                add((k, v))
        return deps

    def _waits(self, e, deps):
        ws = []
        for k, v in deps.items():
            if e == "pe" and k == "prog_pe":
                continue
            if self.waited[e].get(k, 0) >= v:
                continue
            self.waited[e][k] = v
            ws.append((k, v))
        return ws

    def op(self, e, fn, reads=(), writes=(), inc=True):
        deps = self._deps(reads, writes)
        ws = self._waits(e, deps)
        key = "prog_" + e
        if inc:
            self.cnt[e] += 1
            tok = (key, self.cnt[e])
            incs = [(key, 1)]
        else:
            tok = (key, self.cnt[e] + 1)
            incs = []
        self.stream[e].append((ws, fn, incs))
        for b in reads:
            if b.r.get(key, 0) < tok[1]:
                b.r[key] = tok[1]
        for b in writes:
            b.w = tok
            b.r = {}
        return tok

    def dma(self, q, fn, semname, reads=(), writes=()):
        deps = self._deps(reads, writes)
        ws = self._waits(q, deps)
        self.semkeys.add(semname)
        self.dma_cnt[semname] = self.dma_cnt.get(semname, 0) + 1
        tok = (semname, 16 * self.dma_cnt[semname])
        self.stream[q].append((ws, fn, [(semname, 16)]))
        for b in reads:
            if b.r.get(semname, 0) < tok[1]:
                b.r[semname] = tok[1]
        for b in writes:
            b.w = tok
            b.r = {}
        return tok

    def barrier(self):
        allv = {"prog_" + e: self.cnt[e] for e in ("pe", "act", "dve", "pool") if self.cnt[e] > 0}
        for k, c in self.dma_cnt.items():
            allv[k] = 16 * c
        for e in self.ENG:
            ws = self._waits(e, allv)
            if ws:
                self.stream[e].append((ws, None, []))

    def replay(self, e, eng, sems):
        for ws, fn, incs in self.stream[e]:
            for k, v in ws:
                eng.wait_ge(sems[k], v)
            if fn is None:
                continue
            ins = fn(eng)
            for k, v in incs:
                ins = ins.then_inc(sems[k], v)


def build(cfg):
    D, DFF, NT, NQ = cfg.D, cfg.DFF, cfg.NT, cfg.NQ
    KC, FC, H, GD, CG, NS, NKB = cfg.KC, cfg.FC, cfg.H, cfg.GD, cfg.CG, cfg.NS, cfg.NKB
    OC, NPASS, NOG, OW = cfg.OC, cfg.NPASS, cfg.NOG, cfg.OW
    scale = 128 ** -0.5
    lam_init = 0.8 - 0.6 * math.exp(-0.3 * 1)

    nc = bass.Bass("TRN2", target_bir_lowering=False)

    def din(name, shape, dt=F32):
        return nc.dram_tensor(name, list(shape), dt, kind="ExternalInput").ap()

    def dscr(name, shape, dt):
        return nc.dram_tensor(name, list(shape), dt, kind="Internal").ap()

    xt_d = din("xt", [NS * 512, D])
    halo_d = din("halo", [NS * 128, D])
    bandm_d = din("bandm", [NS * 128, 4 * 4 * 128], BF16)
    bandh_d = din("bandh", [NS * 64, 2 * 4 * 128], BF16)
    tabs_d = din("tabs", [NQ * 128, 3 * NKB])
    base0_d = din("base0", [128, 512])
    w0_d = din("w0", [128, 896])
    ident_d = din("ident", [128, 128], BF16)
    ones_d = din("ones", [128, 128], BF16)
    gcols_d = din("gcols", [128, 4 * KC])
    gfin_d = din("gfin", [128, D])
    pscb_d = din("pscb", [128, D])
    subg_d = din("subg", [128, 2])
    lamb_d = din("lamb", [128, 4 * 128])
    poolw_d = din("pool_w", [4 * GD, GD])
    wqkv_d = din("w_qkv", [D, 3 * D])
    wo_d = din("w_o", [D, D])
    wg_d = din("w_gate", [2 * D, DFF])
    wu_d = din("w_up", [2 * D, DFF])
    wd_d = din("w_down", [2 * DFF, D])
    y_d = nc.dram_tensor("y", [NQ * 512, D], F32, kind="ExternalOutput").ap()

    wqkv_b = dscr("wqkv_b", [D, 3 * D], BF16)
    wo_b = dscr("wo_b", [D, D], BF16)
    wg_b = dscr("wg_b", [2 * D, DFF], BF16)
    wu_b = dscr("wu_b", [2 * D, DFF], BF16)
    wd_b = dscr("wd_b", [2 * DFF, D], BF16)
    kt_s = dscr("kt_s", [H * 2 * 128, NS * 512], BF16)
    v_s = dscr("v_s", [NS * 512, D], BF16)
    qt_s = dscr("qt_s", [H * 2 * 128, NQ * 512], BF16)
    h1_s = dscr("h1_s", [NQ * 512, D], F32)

    tr = Tracker()
    es = ExitStack()

    def sb(name, shape, dt):
        return es.enter_context(nc.sbuf_tensor("sb_" + name, list(shape), dt))

    def ps(name):
        return es.enter_context(nc.psum_tensor(name, [128, 512], F32))

    xt = sb("xt", [128, 4, D], F32)
    hn = sb("hn", [128, 4, D], BF16)
    cT = sb("cT", [128, KC, 512], BF16)
    ARW = max(FC * 512, 22 * 1024)
    arena = sb("arena", [128, ARW], BF16)
    slabs = [sb("slab%d" % i, [128, KC, 256], BF16) for i in range(4)]
    wdc = [sb("wdc%d" % i, [128, 2, OC], BF16) for i in range(3)]
    sg = [sb("sg%d" % i, [128, 512], F32) for i in range(2)]
    PHW = 18 * 1024
    parena = sb("parena", [128, PHW], BF16)
    ident = sb("ident_sb", [128, 128], BF16)
    ones = sb("ones_sb", [128, 128], BF16)
    gcols = sb("gcols_sb", [128, 4 * KC], F32)
    subg = sb("subg_sb", [128, 2], F32)
    small = sb("small", [128, 32], F32)
    lamc = sb("lamc", [128, 8], F32)
    pb = [ps("ps%d" % i) for i in range(8)]

    def carve(base_ap_t, off, n, dt, shape=None):
        a = base_ap_t[:, off:off + n]
        if dt == F32:
            a = a.bitcast(F32)
        return a

    actT = arena[:, 0:FC * 512].rearrange("p (f t) -> p f t", f=FC)
    o = [0]

    def acarve(nbf, dt):
        a = carve(arena, o[0], nbf, dt)
        o[0] += nbf
        return a
    hl = arena[0:64, 0:4 * D].bitcast(F32).rearrange("p (a c) -> p a c", a=2)
    QT = [acarve(1024, BF16).rearrange("p (j t) -> p j t", j=2) for _ in range(2)]
    KTc = [acarve(1024, BF16).rearrange("p (j t) -> p j t", j=2) for _ in range(3)]
    Vc = [acarve(1024, BF16).rearrange("p (b e) -> p b e", b=4) for _ in range(3)]
    Sp = [acarve(1024, F32) for _ in range(2)]
    PT = [[acarve(512, BF16) for _ in range(2)] for _ in range(2)]
    Rr = [acarve(1024, F32) for _ in range(2)]
    of = acarve(2048, F32).rearrange("p (e t) -> p e t", e=2)
    t1 = acarve(1024, F32)
    sq = acarve(1024, BF16).rearrange("p (e t) -> p e t", e=2)
    rs = acarve(1024, F32)
    assert o[0] <= ARW

    po = [0]

    def pcarve(nbf, dt, parts=128):
        a = parena[0:parts, po[0]:po[0] + nbf]
        if dt == F32:
            a = a.bitcast(F32)
        po[0] += nbf
        return a
    hnh = pcarve(2 * D, BF16, 64).rearrange("p (a c) -> p a c", a=2)
    pw = pcarve(4 * CG * GD, BF16).rearrange("p (g k d) -> p g k d", g=4, k=CG)
    bandm = pcarve(4 * 4 * 128, BF16).rearrange("p (s g t) -> p s g t", s=4, g=4)
    bandh = pcarve(2 * 4 * 128, BF16, 64).rearrange("p (a g t) -> p a g t", a=2, g=4)
    ktsb = [pcarve(512, BF16) for _ in range(2)]
    vsb = [pcarve(1024, BF16).rearrange("p (s e) -> p s e", s=4) for _ in range(2)]
    assert po[0] <= PHW, po[0]
    po[0] = 0
    base0 = pcarve(1024, F32)
    w0 = pcarve(1792, F32)
    tabs = pcarve(2 * 3 * NKB, F32).rearrange("p (a k) -> p a k", a=3)
    biasrow = pcarve(2 * NKB, F32)
    sgnrow = pcarve(2 * NKB, F32)
    gfin = pcarve(2 * D, F32)
    assert po[0] <= PHW, po[0]

    B = {}

    def bf(name):
        if name not in B:
            B[name] = Buf(name)
        return B[name]

    PB = [bf("psum%d" % i) for i in range(8)]

    def mm(out, lhsT, rhs, start, stop, reads, wbuf, last):
        tr.op("pe", lambda e: e.matmul(out, lhsT, rhs, start=start, stop=stop),
              reads=reads, writes=[wbuf], inc=last)

    def tp(out, in_, reads, wbuf, last):
        tr.op("pe", lambda e: e.transpose(out, in_, ident[:, :]), reads=reads, writes=[wbuf], inc=last)

    WBK = {}
    step = 256
    conv = [("g0", wg_d, wg_b, 0, D), ("u0", wu_d, wu_b, 0, D), ("d0", wd_d, wd_b, 0, DFF), ("qkv", wqkv_d, wqkv_b, 0, D),
            ("o", wo_d, wo_b, 0, D), ("g1", wg_d, wg_b, D, 2 * D), ("u1", wu_d, wu_b, D, 2 * D), ("d1", wd_d, wd_b, DFF, 2 * DFF)]
    for key, src, dst, ra, rb in conv:
        WBK[key] = bf("wb_" + key)
        for r0 in range(ra, rb, step):
            tr.dma("pool", (lambda s_, d_, r0_: (lambda e: e.dma_start(out=d_[r0_:r0_ + step, :], in_=s_[r0_:r0_ + step, :])))(src, dst, r0),
                   "s_wc_" + key, writes=[WBK[key]])
    CB = bf("consts")
    for dst, src in ((ident[:, :], ident_d), (ones[:, :], ones_d), (gcols[:, :], gcols_d), (subg[:, :], subg_d)):
        tr.dma("sp", (lambda d_, s_: (lambda e: e.dma_start(out=d_, in_=s_[:, :])))(dst, src), "s_const", writes=[CB])
    pwtmp = arena[:, 0:2 * 4 * CG * GD].bitcast(F32).rearrange("p (g k d) -> p g k d", g=4, k=CG)
    psc_t = xt[:, 0, :]
    AR = bf("arena")
    XT = bf("xt")
    PW = bf("pw")
    tr.dma("sp", lambda e: e.dma_start(out=pwtmp, in_=poolw_d.rearrange("(g k p) d -> p g k d", g=4, k=CG)), "s_arena", writes=[AR])
    tr.dma("sp", lambda e: e.dma_start(out=psc_t, in_=pscb_d[:, :]), "s_xt", writes=[XT])
    for g in range(4):
        for k in range(CG):
            tr.op("dve", (lambda g_, k_: (lambda e: e.tensor_tensor(out=pw[:, g_, k_, :], in0=pwtmp[:, g_, k_, :], in1=psc_t[:, g_ * GD:(g_ + 1) * GD], op=ALU.mult)))(g, k),
                  reads=[AR, XT], writes=[PW])
    lamt = xt[:, 1, 0:512]
    lamp = xt[:, 2, 0:256]
    LM = bf("lamtmp")
    SM = bf("small")
    tr.dma("sp", lambda e: e.dma_start(out=lamt, in_=lamb_d[:, :]), "s_lam", writes=[LM])
    tr.op("dve", lambda e: e.tensor_tensor(out=lamp[:, 0:128], in0=lamt[:, 0:128], in1=lamt[:, 128:256], op=ALU.mult), reads=[LM], writes=[LM])
    tr.op("dve", lambda e: e.tensor_tensor(out=lamp[:, 128:256], in0=lamt[:, 256:384], in1=lamt[:, 384:512], op=ALU.mult), reads=[LM], writes=[LM])
    tr.op("dve", lambda e: e.reduce_sum(out=lamc[:, 0:1], in_=lamp[:, 0:128], axis=AX.X), reads=[LM], writes=[SM])
    tr.op("dve", lambda e: e.reduce_sum(out=lamc[:, 1:2], in_=lamp[:, 128:256], axis=AX.X), reads=[LM], writes=[SM])
    tr.op("act", lambda e: e.activation(out=lamc[:, 2:4], in_=lamc[:, 0:2], func=AF.Exp), reads=[SM], writes=[SM])
    tr.op("dve", lambda e: e.tensor_tensor(out=lamc[:, 4:5], in0=lamc[:, 2:3], in1=lamc[:, 3:4], op=ALU.subtract), reads=[SM], writes=[SM])
    tr.op("dve", lambda e: e.tensor_scalar(out=lamc[:, 5:6], in0=lamc[:, 4:5], scalar1=lam_init, scalar2=None, op0=ALU.add), reads=[SM], writes=[SM])
    lam_col = lamc[:, 5:6]
    tr.op("dve", lambda e: e.tensor_scalar(out=subg[:, :], in0=subg[:, :], scalar1=(1.0 - lam_init), scalar2=None, op0=ALU.mult), reads=[CB], writes=[CB])
    tr.barrier()

    HNS = [bf("hn%d" % i) for i in range(4)]
    CT = bf("cT")
    SL = [bf("slab%d" % i) for i in range(4)]
    WDC = [bf("wdc%d" % i) for i in range(3)]
    SG = [bf("sg%d" % i) for i in range(2)]
    slab_i = [0]
    wdc_i = [0]
    sg_i = [0]
    pbi = [0]

    def next_pb():
        i = pbi[0] % 8
        pbi[0] += 1
        return i

    def rms_stats(col0, nsub=4):
        tr.op("dve", lambda e: e.memset(small[:, col0:col0 + nsub], 0.0), writes=[SM])
        for s in range(nsub):
            tr.op("act", (lambda s_: (lambda e: e.activation(out=hn[:, s_, :], in_=xt[:, s_, :], func=AF.Square, accum_out=small[:, col0 + s_:col0 + s_ + 1])))(s),
                  reads=[XT], writes=[HNS[s], SM])
        tr.op("dve", lambda e: e.tensor_scalar(out=small[:, col0:col0 + nsub], in0=small[:, col0:col0 + nsub], scalar1=1.0 / D, scalar2=EPS, op0=ALU.mult, op1=ALU.add), reads=[SM], writes=[SM])
        tr.op("act", lambda e: e.activation(out=small[:, col0:col0 + nsub], in_=small[:, col0:col0 + nsub], func=AF.Sqrt), reads=[SM], writes=[SM])
        tr.op("dve", lambda e: e.reciprocal(out=small[:, col0:col0 + nsub], in_=small[:, col0:col0 + nsub]), reads=[SM], writes=[SM])

    def make_hn(col0):
        for s in range(4):
            if s % 2 == 0:
                tr.op("dve", (lambda s_: (lambda e: e.tensor_scalar(out=hn[:, s_, :], in0=xt[:, s_, :], scalar1=small[:, col0 + s_:col0 + s_ + 1], scalar2=None, op0=ALU.mult)))(s),
                      reads=[XT, SM], writes=[HNS[s]])
            else:
                tr.op("act", (lambda s_: (lambda e: e.activation(out=hn[:, s_, :], in_=xt[:, s_, :], func=AF.Copy, scale=small[:, col0 + s_:col0 + s_ + 1])))(s),
                      reads=[XT, SM], writes=[HNS[s]])

    def transpose_hn(gi):
        for k in range(KC):
            b = next_pb()
            pv = pb[b][:, :].bitcast(BF16)
            for s in range(4):
                tp(pv[:, s * 128:(s + 1) * 128], hn[:, s, k * 128:(k + 1) * 128], [HNS[s], CB], PB[b], last=(s == 3))
            eng = "act" if k % 2 == 0 else "dve"
            if eng == "act":
                tr.op("act", (lambda k_, pv_: (lambda e: e.activation(out=cT[:, k_, :], in_=pv_[:, 0:512], func=AF.Copy, scale=gcols[:, gi * KC + k_:gi * KC + k_ + 1])))(k, pv),
                      reads=[PB[b], CB], writes=[CT])
            else:
                tr.op("dve", (lambda k_, pv_: (lambda e: e.tensor_scalar(out=cT[:, k_, :], in0=pv_[:, 0:512], scalar1=gcols[:, gi * KC + k_:gi * KC + k_ + 1], scalar2=None, op0=ALU.mult)))(k, pv),
                      reads=[PB[b], CB], writes=[CT])

    def load_slab(src_ap, c0, wb):
        i = slab_i[0] % 4
        slab_i[0] += 1
        tr.dma("sp", (lambda i_: (lambda e: e.dma_start(out=slabs[i_][:, :, :], in_=src_ap[:, c0:c0 + 256].rearrange("(k p) f -> p k f", p=128))))(i),
               "s_slab%d" % i, reads=[wb], writes=[SL[i]])
        return i

    AT = bf("actT")

    def ffn(layer, gi, col0):
        rms_stats(col0)
        make_hn(col0)
        transpose_hn(gi)
        wg_l = wg_b[layer * D:(layer + 1) * D, :]
        wu_l = wu_b[layer * D:(layer + 1) * D, :]
        wd_l = wd_b[layer * DFF:(layer + 1) * DFF, :]
        nfg = DFF // 256
        pend = None
        WG, WU, WD = WBK["g%d" % layer], WBK["u%d" % layer], WBK["d%d" % layer]
        nxt = (load_slab(wg_l, 0, WG), load_slab(wu_l, 0, WU))
        for fg in range(nfg):
            cur = nxt
            if fg + 1 < nfg:
                nxt = (load_slab(wg_l, (fg + 1) * 256, WG), load_slab(wu_l, (fg + 1) * 256, WU))
            for fc in range(2):
                f = fg * 2 + fc
                bg, bu = next_pb(), next_pb()
                for (si, b) in ((cur[0], bg), (cur[1], bu)):
                    for k in range(KC):
                        mm(pb[b][:, :], slabs[si][:, k, fc * 128:(fc + 1) * 128], cT[:, k, :], k == 0, k == KC - 1,
                           [SL[si], CT], PB[b], last=(k == KC - 1))
                gi_ = sg_i[0] % 2
                sg_i[0] += 1
                tr.op("act", (lambda b_, g_: (lambda e: e.activation(out=sg[g_][:, :], in_=pb[b_][:, :], func=AF.Silu)))(bg, gi_),
                      reads=[PB[bg]], writes=[SG[gi_]])
                tr.op("dve", (lambda b_, g_, f_: (lambda e: e.tensor_tensor(out=actT[:, f_, :], in0=sg[g_][:, :], in1=pb[b_][:, :], op=ALU.mult)))(bu, gi_, f),
                      reads=[SG[gi_], PB[bu]], writes=[AT])
        for p_ in range(NPASS):
            banks = [[next_pb() for _ in range(NOG)] for _ in range(4)]
            nch = FC // 2
            def ld(c):
                i = wdc_i[0] % 3
                wdc_i[0] += 1
                tr.dma("sp", (lambda i_, c_, pp_: (lambda e: e.dma_start(out=wdc[i_][:, :, :], in_=wd_l[c_ * 256:(c_ + 1) * 256, pp_ * OC:(pp_ + 1) * OC].rearrange("(a p) d -> p a d", p=128))))(i, c, p_),
                       "s_wdc%d" % i, reads=[WD], writes=[WDC[i]])
                return i
            ring = [ld(c_) for c_ in range(min(2, nch))]
            for c in range(nch):
                cu = ring.pop(0)
                if c + 2 < nch:
                    ring.append(ld(c + 2))
                for a in range(2):
                    f = c * 2 + a
                    for s in range(4):
                        for og in range(NOG):
                            b = banks[s][og]
                            mm(pb[b][:, 0:OW], actT[:, f, s * 128:(s + 1) * 128], wdc[cu][:, a, og * OW:(og + 1) * OW], f == 0, f == FC - 1,
                               [AT, WDC[cu]], PB[b], last=(f == FC - 1 or (a == 1 and s == 3 and og == NOG - 1)))
            for s in range(4):
                for og in range(NOG):
                    b = banks[s][og]
                    c0 = p_ * OC + og * OW
                    tr.op("dve", (lambda s_, b_, c0_: (lambda e: e.tensor_tensor(out=xt[:, s_, c0_:c0_ + OW], in0=xt[:, s_, c0_:c0_ + OW], in1=pb[b_][:, 0:OW], op=ALU.add)))(s, b, c0),
                          reads=[XT, PB[b]], writes=[XT])

    HL = AR
    BD = bf("band")
    HH = bf("hnh")
    KSB = [bf("ktsb%d" % i) for i in range(2)]
    VSB = [bf("vsb%d" % i) for i in range(2)]
    KTS, VS, QTS, H1S = bf("kt_s"), bf("v_s"), bf("qt_s"), bf("h1_s")
    ksb_i = [0]
    vsb_i = [0]
    def load_x_band(sl):
        tr.dma("pool", lambda e: e.dma_start(out=xt[:, :, :], in_=xt_d[sl * 512:(sl + 1) * 512, :].rearrange("(s p) c -> p s c", p=128)), "s_xt", writes=[XT])
        tr.dma("pool", lambda e: e.dma_start(out=bandm, in_=bandm_d[sl * 128:(sl + 1) * 128, :].rearrange("p (s g t) -> p s g t", s=4, g=4)), "s_band", writes=[BD])
        tr.dma("pool", lambda e: e.dma_start(out=bandh, in_=bandh_d[sl * 64:(sl + 1) * 64, :].rearrange("p (a g t) -> p a g t", a=2, g=4)), "s_band", writes=[BD])

    for slot in range(NS):
        isq = slot < NQ
        if slot == 0:
            load_x_band(0)
        tr.dma("sp", (lambda sl: (lambda e: e.dma_start(out=hl, in_=halo_d[sl * 128:(sl + 1) * 128, :].rearrange("(a p) c -> p a c", p=64))))(slot), "s_arena", writes=[AT])
        rms_stats(0)
        tr.op("dve", lambda e: e.memset(small[0:64, 4:6], 0.0), writes=[SM])
        for a in range(2):
            tr.op("act", (lambda a_: (lambda e: e.activation(out=hnh[:, a_, :], in_=hl[:, a_, :], func=AF.Square, accum_out=small[0:64, 4 + a_:5 + a_])))(a),
                  reads=[AT], writes=[HH, SM])
        tr.op("dve", lambda e: e.tensor_scalar(out=small[0:64, 4:6], in0=small[0:64, 4:6], scalar1=1.0 / D, scalar2=EPS, op0=ALU.mult, op1=ALU.add), reads=[SM], writes=[SM])
        tr.op("act", lambda e: e.activation(out=small[0:64, 4:6], in_=small[0:64, 4:6], func=AF.Sqrt), reads=[SM], writes=[SM])
        tr.op("dve", lambda e: e.reciprocal(out=small[0:64, 4:6], in_=small[0:64, 4:6]), reads=[SM], writes=[SM])
        make_hn(0)
        for a in range(2):
            tr.op("dve", (lambda a_: (lambda e: e.tensor_scalar(out=hnh[:, a_, :], in0=hl[:, a_, :], scalar1=small[0:64, 4 + a_:5 + a_], scalar2=None, op0=ALU.mult)))(a),
                  reads=[AT, SM], writes=[HH])
        for k in range(KC):
            g = k // CG
            b = next_pb()
            for s in range(4):
                a, hp = s // 2, 32 * (s % 2)
                mm(pb[b][:, s * 128:(s + 1) * 128], hn[:, s, k * 128:(k + 1) * 128], bandm[:, s, g, :], True, False, [HNS[s], BD], PB[b], last=False)
                mm(pb[b][:, s * 128:(s + 1) * 128], hnh[hp:hp + 32, a, k * 128:(k + 1) * 128], bandh[hp:hp + 32, a, g, :], False, True, [HH, BD], PB[b], last=(s == 3))
            if k % 2 == 0:
                tr.op("act", (lambda k_, b_: (lambda e: e.activation(out=cT[:, k_, :], in_=pb[b_][:, :], func=AF.Copy, scale=gcols[:, k_:k_ + 1])))(k, b),
                      reads=[PB[b], CB], writes=[CT])
            else:
                tr.op("dve", (lambda k_, b_: (lambda e: e.tensor_scalar(out=cT[:, k_, :], in0=pb[b_][:, :], scalar1=gcols[:, k_:k_ + 1], scalar2=None, op0=ALU.mult)))(k, b),
                      reads=[PB[b], CB], writes=[CT])
        for s in range(4):
            for g in range(4):
                b = next_pb()
                for kk in range(CG):
                    mm(pb[b][:, 0:GD], cT[:, g * CG + kk, s * 128:(s + 1) * 128], pw[:, g, kk, :], kk == 0, kk == CG - 1, [CT, PW], PB[b], last=(kk == CG - 1))
                tr.op("dve", (lambda s_, g_, b_: (lambda e: e.tensor_tensor(out=xt[:, s_, g_ * GD:(g_ + 1) * GD], in0=xt[:, s_, g_ * GD:(g_ + 1) * GD], in1=pb[b_][:, 0:GD], op=ALU.add)))(s, g, b),
                      reads=[XT, PB[b]], writes=[XT])
        ffn(0, 1, 8)
        if isq:
            tr.dma("pool", (lambda sl: (lambda e: e.dma_start(out=h1_s[sl * 512:(sl + 1) * 512, :].rearrange("(s p) c -> p s c", p=128), in_=xt[:, :, :])))(slot),
                   "s_h1st", reads=[XT], writes=[H1S])
        rms_stats(12)
        make_hn(12)
        if slot + 1 < NS:
            load_x_band(slot + 1)
        transpose_hn(2)
        jobs = []
        for h in range(H):
            jobs.append(("k", h))
            jobs.append(("v", h))
            if isq:
                jobs.append(("q", h))

        def jcol(job):
            kind, h = job
            return {"q": 0, "k": D, "v": 2 * D}[kind] + h * 256
        nxs = load_slab(wqkv_b, jcol(jobs[0]), WBK["qkv"])
        for ji, job in enumerate(jobs):
            cs = nxs
            if ji + 1 < len(jobs):
                nxs = load_slab(wqkv_b, jcol(jobs[ji + 1]), WBK["qkv"])
            kind, h = job
            if kind in ("k", "q"):
                for j in range(2):
                    b = next_pb()
                    for k in range(KC):
                        mm(pb[b][:, :], slabs[cs][:, k, j * 128:(j + 1) * 128], cT[:, k, :], k == 0, k == KC - 1, [SL[cs], CT], PB[b], last=(k == KC - 1))
                    ki = ksb_i[0] % 2
                    ksb_i[0] += 1
                    tr.op("act", (lambda b_, ki_: (lambda e: e.activation(out=ktsb[ki_], in_=pb[b_][:, :], func=AF.Copy)))(b, ki), reads=[PB[b]], writes=[KSB[ki]])
                    dst, DB, ncol = (kt_s, KTS, NS * 512) if kind == "k" else (qt_s, QTS, NQ * 512)
                    r0 = (h * 2 + j) * 128
                    tr.dma("pool", (lambda dst_, r0_, sl, ki_: (lambda e: e.dma_start(out=dst_[r0_:r0_ + 128, sl * 512:(sl + 1) * 512], in_=ktsb[ki_])))(dst, r0, slot, ki),
                           "s_kst%d" % ki, reads=[KSB[ki]], writes=[DB])
            else:
                vi = vsb_i[0] % 2
                vsb_i[0] += 1
                for s in range(4):
                    b = next_pb()
                    for k in range(KC):
                        mm(pb[b][:, 0:256], cT[:, k, s * 128:(s + 1) * 128], slabs[cs][:, k, :], k == 0, k == KC - 1, [CT, SL[cs]], PB[b], last=(k == KC - 1))
                    tr.op("dve", (lambda b_, vi_, s_: (lambda e: e.tensor_copy(out=vsb[vi_][:, s_, :], in_=pb[b_][:, 0:256])))(b, vi, s), reads=[PB[b]], writes=[VSB[vi]])
                tr.dma("pool", (lambda sl, h_, vi_: (lambda e: e.dma_start(out=v_s[sl * 512:(sl + 1) * 512, h_ * 256:(h_ + 1) * 256].rearrange("(s p) e -> p s e", p=128), in_=vsb[vi_])))(slot, h, vi),
                       "s_vst%d" % vi, reads=[VSB[vi]], writes=[VS])
    tr.barrier()

    QC = bf("qconst")
    for dst, src in ((base0, base0_d), (w0, w0_d), (gfin, gfin_d)):
        tr.dma("sp", (lambda d_, s_: (lambda e: e.dma_start(out=d_, in_=s_[:, :])))(dst, src), "s_const", writes=[QC])
    TB = bf("tabs")
    ROW = bf("rows")
    QTB = [bf("QT%d" % i) for i in range(2)]
    KTB = [bf("KTc%d" % i) for i in range(3)]
    VCB = [bf("Vc%d" % i) for i in range(3)]
    SPB = [bf("Sp%d" % i) for i in range(2)]
    PTB = [[bf("PT%d_%d" % (j, i)) for i in range(2)] for j in range(2)]
    FIN = bf("fin")
    qt_i = [0]
    kv_i = [0]
    pt_i = [0]
    slopes = [2.0 ** (-8.0 * (h + 1) / H) for h in range(H)]
    SKIP_TH = _SKIP_TH
    dmin = _min_dist(cfg)
    for qi in range(NQ):
        tr.barrier()
        tr.dma("sp", (lambda q_: (lambda e: e.dma_start(out=xt[:, :, :], in_=h1_s[q_ * 512:(q_ + 1) * 512, :].rearrange("(s p) c -> p s c", p=128))))(qi), "s_xt", reads=[H1S], writes=[XT])
        tr.dma("sp", (lambda q_: (lambda e: e.dma_start(out=tabs, in_=tabs_d[q_ * 128:(q_ + 1) * 128, :].rearrange("p (a k) -> p a k", a=3))))(qi), "s_tabs", writes=[TB])
        def ldkv(h, c):
            i = kv_i[0] % 3
            kv_i[0] += 1
            nb = min(4, NKB - 4 * c)
            tr.dma("sp", (lambda h_, c_, i_, nb_: (lambda e: e.dma_start(out=KTc[i_][:, :, 0:nb_ * 128], in_=kt_s[h_ * 256:(h_ + 1) * 256, c_ * 512:c_ * 512 + nb_ * 128].rearrange("(j p) t -> p j t", p=128))))(h, c, i, nb),
                   "s_kt%d" % i, reads=[KTS], writes=[KTB[i]])
            tr.dma("sp", (lambda h_, c_, i_, nb_: (lambda e: e.dma_start(out=Vc[i_][:, 0:nb_, :], in_=v_s[c_ * 512:c_ * 512 + nb_ * 128, h_ * 256:(h_ + 1) * 256].rearrange("(b p) e -> p b e", p=128))))(h, c, i, nb),
                   "s_vc%d" % i, reads=[VS], writes=[VCB[i]])
            return i

        def head_setup(h):
            nchunk = (NKB + 3) // 4
            need = [kb for kb in range(NKB) if slopes[h] * dmin[qi, kb] < SKIP_TH]
            plan = []
            for c in range(nchunk):
                bis = [kb - 4 * c for kb in need if kb // 4 == c]
                if bis:
                    plan.append((c, bis))
            qb_ = qt_i[0] % 2
            qt_i[0] += 1
            tr.dma("sp", (lambda h_, q_, b_: (lambda e: e.dma_start(out=QT[b_], in_=qt_s[h_ * 256:(h_ + 1) * 256, q_ * 512:(q_ + 1) * 512].rearrange("(j p) t -> p j t", p=128))))(h, qi, qb_),
                   "s_qt%d" % qb_, reads=[QTS], writes=[QTB[qb_]])
            ring = [ldkv(h, plan[i_][0]) for i_ in range(min(2, len(plan)))]
            return dict(plan=plan, qb=qb_, ring=ring, last_need=need[-1])

        hs_next = head_setup(0)
        for h in range(H):
            fh = -slopes[h] / scale
            tr.op("dve", (lambda h_: (lambda e: e.scalar_tensor_tensor(out=biasrow, in0=tabs[:, 1, :], scalar=-slopes[h_], in1=tabs[:, 2, :], op0=ALU.mult, op1=ALU.add)))(h), reads=[TB], writes=[ROW])
            tr.op("dve", (lambda f_: (lambda e: e.tensor_scalar(out=sgnrow, in0=tabs[:, 0, :], scalar1=f_, scalar2=None, op0=ALU.mult)))(fh), reads=[TB], writes=[ROW])
            hs = hs_next
            plan, qb_, ring, last_need = hs["plan"], hs["qb"], hs["ring"], hs["last_need"]
            first = True
            pend_pv = [None]
            for pi_, (c, bis) in enumerate(plan):
                ckv = ring.pop(0)
                for bn, bi in enumerate(bis):
                    kb = 4 * c + bi
                    lastkb = (kb == last_need)
                    diag = (kb // 4 == qi) and kb < 4 * NT
                    pts = []
                    for j in range(2):
                        mm(pb[j][:, :], KTc[ckv][:, j, bi * 128:(bi + 1) * 128], QT[qb_][:, j, :], True, True, [KTB[ckv], QTB[qb_]], PB[j], last=(j == 1))
                    for j in range(2):
                        if diag:
                            o_ = kb % 4
                            tr.op("dve", (lambda j_, o__, f_: (lambda e: e.scalar_tensor_tensor(out=Sp[j_], in0=w0[:, 384 - 128 * o__:384 - 128 * o__ + 512], scalar=f_, in1=pb[j_][:, :], op0=ALU.mult, op1=ALU.add)))(j, o_, fh),
                                  reads=[QC, PB[j]], writes=[SPB[j]])
                        else:
                            tr.op("dve", (lambda j_, kb_: (lambda e: e.scalar_tensor_tensor(out=Sp[j_], in0=base0, scalar=sgnrow[:, kb_:kb_ + 1], in1=pb[j_][:, :], op0=ALU.mult, op1=ALU.add)))(j, kb),
                                  reads=[QC, ROW, PB[j]], writes=[SPB[j]])
                        pi = pt_i[0] % 2
                        tr.op("act", (lambda j_, kb_, pi_: (lambda e: e.activation(out=PT[j_][pi_], in_=Sp[j_], func=AF.Exp, bias=biasrow[:, kb_:kb_ + 1], scale=scale)))(j, kb, pi),
                              reads=[SPB[j], ROW], writes=[PTB[j][pi]])
                        pts.append(pi)
                    pt_i[0] += 1

                    def pv(pts=pts, ckv=ckv, bi=bi, first=first, lastkb=lastkb):
                        for j in range(2):
                            pi = pts[j]
                            for ec in range(2):
                                b = 2 + 2 * j + ec
                                mm(pb[b][:, :], Vc[ckv][:, bi, ec * 128:(ec + 1) * 128], PT[j][pi], first, lastkb, [VCB[ckv], PTB[j][pi]], PB[b], last=False)
                            mm(pb[6 + j][:, :], ones[:, :], PT[j][pi], first, lastkb, [CB, PTB[j][pi]], PB[6 + j], last=(j == 1))
                    if pend_pv[0] is not None:
                        pend_pv[0]()
                    pend_pv[0] = pv
                    first = False
                    if bn == 0 and pi_ + 2 < len(plan):
                        ring.append(ldkv(h, plan[pi_ + 2][0]))
            pend_pv[0]()
            pend_pv[0] = None
            if h + 1 < H:
                hs_next = head_setup(h + 1)
            for j in range(2):
                tr.op("dve", (lambda j_: (lambda e: e.reciprocal(out=Rr[j_], in_=pb[6 + j_][:, :])))(j), reads=[PB[6 + j]], writes=[FIN])
            tr.op("dve", lambda e: e.tensor_scalar(out=Rr[1], in0=Rr[1], scalar1=lam_col, scalar2=None, op0=ALU.mult), reads=[FIN, SM], writes=[FIN])
            for ec in range(2):
                tr.op("dve", (lambda ec_: (lambda e: e.tensor_tensor(out=of[:, ec_, :], in0=pb[2 + ec_][:, :], in1=Rr[0], op=ALU.mult)))(ec), reads=[PB[2 + ec], FIN], writes=[FIN])
                tr.op("dve", (lambda ec_: (lambda e: e.tensor_tensor(out=t1, in0=pb[4 + ec_][:, :], in1=Rr[1], op=ALU.mult)))(ec), reads=[PB[4 + ec], FIN], writes=[FIN])
                tr.op("dve", (lambda ec_: (lambda e: e.tensor_tensor(out=of[:, ec_, :], in0=of[:, ec_, :], in1=t1, op=ALU.subtract)))(ec), reads=[FIN], writes=[FIN])
                tr.op("act", (lambda ec_: (lambda e: e.activation(out=sq[:, ec_, :], in_=of[:, ec_, :], func=AF.Square)))(ec), reads=[FIN], writes=[FIN])
            for ec in range(2):
                mm(pb[0][:, :], ones[:, :], sq[:, ec, :], ec == 0, ec == 1, [CB, FIN], PB[0], last=(ec == 1))
            tr.op("dve", lambda e: e.tensor_scalar(out=rs, in0=pb[0][:, :], scalar1=1.0 / 256.0, scalar2=EPS, op0=ALU.mult, op1=ALU.add), reads=[PB[0]], writes=[FIN])
            tr.op("act", lambda e: e.activation(out=rs, in_=rs, func=AF.Sqrt), reads=[FIN], writes=[FIN])
            tr.op("dve", lambda e: e.reciprocal(out=rs, in_=rs), reads=[FIN], writes=[FIN])
            for ec in range(2):
                tr.op("dve", (lambda ec_, h_: (lambda e: e.scalar_tensor_tensor(out=cT[:, 2 * h_ + ec_, :], in0=of[:, ec_, :], scalar=subg[:, ec_:ec_ + 1], in1=rs, op0=ALU.mult, op1=ALU.mult)))(ec, h),
                      reads=[FIN, CB], writes=[CT])
        tr.barrier()
        for p_ in range(NPASS):
            banks = [[next_pb() for _ in range(NOG)] for _ in range(4)]
            nch = KC // 2

            def ldo(c):
                i = wdc_i[0] % 3
                wdc_i[0] += 1
                tr.dma("sp", (lambda i_, c_, pp_: (lambda e: e.dma_start(out=wdc[i_][:, :, :], in_=wo_b[c_ * 256:(c_ + 1) * 256, pp_ * OC:(pp_ + 1) * OC].rearrange("(a p) d -> p a d", p=128))))(i, c, p_),
                       "s_wdc%d" % i, reads=[WBK["o"]], writes=[WDC[i]])
                return i
            ring = [ldo(c_) for c_ in range(min(2, nch))]
            for c in range(nch):
                cu = ring.pop(0)
                if c + 2 < nch:
                    ring.append(ldo(c + 2))
                for a in range(2):
                    k = c * 2 + a
                    for s in range(4):
                        for og in range(NOG):
                            b = banks[s][og]
                            mm(pb[b][:, 0:OW], cT[:, k, s * 128:(s + 1) * 128], wdc[cu][:, a, og * OW:(og + 1) * OW], k == 0, k == KC - 1,
                               [CT, WDC[cu]], PB[b], last=(k == KC - 1 or (a == 1 and s == 3 and og == NOG - 1)))
            for s in range(4):
                for og in range(NOG):
                    b = banks[s][og]
                    c0 = p_ * OC + og * OW
                    tr.op("dve", (lambda s_, b_, c0_: (lambda e: e.tensor_tensor(out=xt[:, s_, c0_:c0_ + OW], in0=xt[:, s_, c0_:c0_ + OW], in1=pb[b_][:, 0:OW], op=ALU.add)))(s, b, c0),
                          reads=[XT, PB[b]], writes=[XT])
        ffn(1, 3, 16)
        rms_stats(20)
        for s in range(4):
            tr.op("dve", (lambda s_: (lambda e: e.scalar_tensor_tensor(out=xt[:, s_, :], in0=xt[:, s_, :], scalar=small[:, 20 + s_:21 + s_], in1=gfin, op0=ALU.mult, op1=ALU.mult)))(s),
                  reads=[XT, SM, QC], writes=[XT])
        tr.dma("pool", (lambda q_: (lambda e: e.dma_start(out=y_d[q_ * 512:(q_ + 1) * 512, :].rearrange("(s p) c -> p s c", p=128), in_=xt[:, :, :])))(qi), "s_yst", reads=[XT], writes=[bf("y")])
    tr.barrier()

    sems = {}
    for k in sorted(tr.semkeys):
        sems[k] = es.enter_context(nc.semaphore(k))
    with nc.Block() as block:
        @block.tensor
        def _(e):
            tr.replay("pe", e, sems)

        @block.scalar
        def _(e):
            tr.replay("act", e, sems)

        @block.vector
        def _(e):
            tr.replay("dve", e, sems)

        @block.gpsimd
        def _(e):
            tr.replay("pool", e, sems)

        @block.sync
        def _(e):
            tr.replay("sp", e, sems)
    es.close()
    return nc


_band_cache = {}


def _band_for(pos_main, pos_halo, L):
    key = (tuple(pos_main.tolist()), tuple(pos_halo.tolist()), L)
    p0 = int(pos_main[0])
    rel = (tuple((pos_main - p0).tolist()) if p0 >= 0 else None, tuple(np.where(pos_halo >= 0, pos_halo - p0, -999).tolist()), min(p0, 40), min(L - p0, 300) if p0 >= 0 else -1)
    if rel in _band_cache:
        return _band_cache[rel]
    bm = np.zeros((128, 4, 128), np.float32)
    bh = np.zeros((32, 4, 128), np.float32)
    where = {}
    for i, p in enumerate(pos_main):
        if p >= 0:
            where[int(p)] = ("m", i)
    for i, p in enumerate(pos_halo):
        if p >= 0 and int(p) not in where:
            where[int(p)] = ("h", i)
    for r, p in enumerate(pos_main):
        if p < 0:
            continue
        p = int(p)
        for g, w in enumerate(WINDOWS):
            lo, hi = max(p - w // 2, 0), min(p + w // 2, L)
            inv = 1.0 / float(hi - lo)
            for u in range(lo, hi):
                kind, i = where[u]
                if kind == "m":
                    bm[i, g, r] += inv
                else:
                    bh[i, g, r] += inv
            bm[r, g, r] -= 1.0
    _band_cache[rel] = (bm, bh)
    return bm, bh


def _slot_positions(cfg, ntile, ctype):
    NT, NQ, NS = cfg.NT, cfg.NQ, cfg.NS
    half = ntile // 2
    tmap = [-1] * NT
    for i in range(half):
        tmap[i] = (ctype * half + i)
        tmap[NQ + i] = ((1 - ctype) * half + i)
    pos = -np.ones((NS, 512), np.int64)
    for sl in range(NT):
        if tmap[sl] >= 0:
            pos[sl] = N_META + 512 * tmap[sl] + np.arange(512)
    pos[NT, :N_META] = np.arange(N_META)
    return tmap, pos


def _min_dist(cfg):
    NT, NQ, NKB = cfg.NT, cfg.NQ, cfg.NKB
    dmin = np.full((NQ, NKB), np.inf)
    for ntile in (NT, NT // 2):
        for ctype in (0, 1):
            _, pos = _slot_positions(cfg, ntile, ctype)
            for qi in range(NQ):
                if pos[qi, 0] < 0:
                    continue
                q0, q1 = int(pos[qi, 0]), int(pos[qi, 0]) + 511
                for kb in range(NKB):
                    if kb == NKB - 1:
                        k0, k1 = 0, N_META - 1
                    else:
                        k0 = int(pos[kb // 4, (kb % 4) * 128])
                        if k0 < 0:
                            continue
                        k1 = k0 + 127
                    d = max(0, k0 - q1, q0 - k1)
                    dmin[qi, kb] = min(dmin[qi, kb], d)
    return dmin


def _core_layout(cfg, x_seq, meta, ctype):
    D, NT, NQ, NS, NKB = cfg.D, cfg.NT, cfg.NQ, cfg.NS, cfg.NKB
    S = x_seq.shape[0]
    L = S + N_META
    ntile = S // 512
    tmap, pos = _slot_positions(cfg, ntile, ctype)
    xt = np.zeros((NS * 512, D), np.float32)
    for sl in range(NT):
        if tmap[sl] >= 0:
            t = tmap[sl]
            xt[sl * 512:(sl + 1) * 512] = x_seq[512 * t:512 * (t + 1)]
    xt[NT * 512:NT * 512 + N_META] = meta

    def row_of(p):
        return meta[p] if p < N_META else x_seq[p - N_META]
    halo = np.zeros((NS, 2, 64, D), np.float32)
    bandm = np.zeros((NS, 128, 4, 4, 128), np.float32)
    bandh = np.zeros((NS, 64, 2, 4, 128), np.float32)
    for sl in range(NS):
        for s in range(4):
            pm = pos[sl, s * 128:(s + 1) * 128]
            ph = -np.ones(32, np.int64)
            if pm[0] >= 0:
                nvalid = int((pm >= 0).sum())
                p0, p1 = int(pm[0]), int(pm[0]) + nvalid
                cand = list(range(p0 - 8, p0)) + list(range(p1, p1 + 8))
                for i, p in enumerate(cand):
                    if 0 <= p < L:
                        ph[i] = p
                        halo[sl, s // 2, 32 * (s % 2) + i] = row_of(p)
                bm, bh = _band_for(pm, ph, L)
                bandm[sl, :, s] = bm
                bandh[sl, 32 * (s % 2):32 * (s % 2) + 32, s // 2] = bh
    tabs = np.zeros((NQ, 128, 3, NKB), np.float32)
    tabs[:, :, 0, :] = 1.0
    for qi in range(NQ):
        if pos[qi, 0] < 0:
            continue
        pq0 = int(pos[qi, 0])
        for kb in range(NKB):
            if kb == NKB - 1:
                tabs[qi, :, 0, kb] = 1.0
                tabs[qi, :, 1, kb] = pq0
                tabs[qi, N_META:, 2, kb] = NEG
                continue
            sl, sub = kb // 4, kb % 4
            if sl == qi:
                continue
            pk0 = int(pos[sl, sub * 128])
            if pk0 < 0:
                tabs[qi, :, 2, kb] = NEG
                continue
            dlt = pq0 - pk0
            tabs[qi, :, 0, kb] = 1.0 if dlt > 0 else -1.0
            tabs[qi, :, 1, kb] = abs(dlt)
    valid_q = [tmap[i] for i in range(NQ)]
    return dict(xt=xt, halo=halo.reshape(NS * 128, D),
                bandm=bandm.reshape(NS * 128, 2048).astype(NPBF),
                bandh=bandh.reshape(NS * 64, 1024).astype(NPBF),
                tabs=tabs.reshape(NQ * 128, 3 * NKB)), valid_q


def _shared_inputs(cfg, inp):
    D, KC = cfg.D, cfg.KC
    f = np.float32
    kk = np.arange(128)[:, None]
    base0 = (np.arange(512)[None, :] - kk).astype(f)
    w0 = np.abs(np.arange(896)[None, :] - 384 - kk).astype(f)
    gl = [inp["mixer_norm_g"][0], inp["ffn_norm_g"][0], inp["mixer_norm_g"][1], inp["ffn_norm_g"][1]]
    gcols = np.concatenate([np.asarray(g, f).reshape(KC, 128).T for g in gl], axis=1)
    lamb = np.concatenate([np.asarray(inp[k], f).reshape(1, 128) for k in ("lambda_q1", "lambda_k1", "lambda_q2", "lambda_k2")], axis=1)
    return dict(
        base0=np.ascontiguousarray(base0), w0=np.ascontiguousarray(w0),
        ident=np.eye(128, dtype=f).astype(NPBF), ones=np.ones((128, 128), f).astype(NPBF),
        gcols=np.ascontiguousarray(gcols),
        gfin=np.ascontiguousarray(np.broadcast_to(np.asarray(inp["final_norm_g"], f).reshape(1, D), (128, D))),
        pscb=np.ascontiguousarray(np.broadcast_to(np.asarray(inp["pool_scale"], f).reshape(1, D), (128, D))),
        subg=np.ascontiguousarray(np.asarray(inp["subln_g"], f).reshape(2, 128).T),
        lamb=np.ascontiguousarray(np.broadcast_to(lamb, (128, 512))),
        pool_w=np.ascontiguousarray(np.asarray(inp["pool_w"], f).reshape(4 * cfg.GD, cfg.GD)),
        w_qkv=np.ascontiguousarray(np.asarray(inp["w_qkv"], f).reshape(D, 3 * D)),
        w_o=np.ascontiguousarray(np.asarray(inp["w_o"], f).reshape(D, D)),
        w_gate=np.ascontiguousarray(np.asarray(inp["w_gate"], f).reshape(2 * D, cfg.DFF)),
        w_up=np.ascontiguousarray(np.asarray(inp["w_up"], f).reshape(2 * D, cfg.DFF)),
        w_down=np.ascontiguousarray(np.asarray(inp["w_down"], f).reshape(2 * cfg.DFF, D)),
    )


def run(cfg, inp):
    xp = np.asarray(inp["x_prompt"], np.float32)
    xs = np.asarray(inp["x_sample"], np.float32)
    meta = np.asarray(inp["meta_tokens"], np.float32)
    shared = _shared_inputs(cfg, inp)
    seqs = [xs[0], xs[1], xp[0], xp[1]]
    in_maps, vq = [], []
    for c in range(8):
        lay, valid_q = _core_layout(cfg, seqs[c // 2], meta, c % 2)
        m = dict(shared)
        m.update(lay)
        in_maps.append(m)
        vq.append(valid_q)
    nc = build(cfg)
    res = run_bass_kernel_spmd(nc, in_maps, core_ids=list(range(8)))
    yp = np.zeros_like(xp)
    ys = np.zeros_like(xs)
    outs = [ys[0], ys[1], yp[0], yp[1]]
    for c in range(8):
        y = np.asarray(res.results[c]["y"], dtype=np.float32)
        for qi, t in enumerate(vq[c]):
            if t >= 0:
                outs[c // 2][512 * t:512 * (t + 1)] = y[512 * qi:512 * (qi + 1)]
    return yp, ys


def kernel(**inputs):
    cfg = Cfg()
    return run(cfg, inputs)
```

```python
import math
from contextlib import ExitStack
import numpy as np
import ml_dtypes
import concourse.bass as bass
import concourse.mybir as mybir
from concourse.bass_utils import run_bass_kernel_spmd

F32 = mybir.dt.float32
BF16 = mybir.dt.bfloat16
ALU = mybir.AluOpType
AF = mybir.ActivationFunctionType
AX = mybir.AxisListType
NPBF = ml_dtypes.bfloat16

N_META = 16
WINDOWS = (2, 4, 8, 16)
EPS = 1e-6
NEG = -30000.0
_SKIP_TH = 60.0


class Cfg:
    def __init__(self, D=2048, DFF=5632, NT=32, NQ=16):
        self.D, self.DFF, self.NT, self.NQ = D, DFF, NT, NQ
        self.KC = D // 128
        self.FC = DFF // 128
        self.H = D // 256
        self.GD = D // 4
        self.CG = self.GD // 128
        self.NS = NT + 1
        self.NKB = 4 * NT + 1
        self.OC = min(D, 1024)
        self.NPASS = D // self.OC
        self.NOG = self.OC // 512 if self.OC >= 512 else 1
        self.OW = min(512, self.OC)


class Buf:
    def __init__(self, name):
        self.name = name
        self.w = None
        self.r = {}


class Tracker:
    ENG = ("pe", "act", "dve", "pool", "sp")

    def __init__(self):
        self.stream = {e: [] for e in self.ENG}
        self.cnt = {e: 0 for e in self.ENG}
        self.waited = {e: {} for e in self.ENG}
        self.dma_cnt = {}
        self.semkeys = set("prog_" + e for e in ("pe", "act", "dve", "pool"))

    def _deps(self, reads, writes):
        deps = {}

        def add(tok):
            if tok is None:
                return
            k, v = tok
            if deps.get(k, 0) < v:
                deps[k] = v
        for b in reads:
            add(b.w)
        for b in writes:
            add(b.w)
            for k, v in b.r.items():
                add((k, v))
        return deps

    def _waits(self, e, deps):
        ws = []
        for k, v in deps.items():
            if e == "pe" and k == "prog_pe":
                continue
            if self.waited[e].get(k, 0) >= v:
                continue
            self.waited[e][k] = v
            ws.append((k, v))
        return ws

    def op(self, e, fn, reads=(), writes=(), inc=True):
        deps = self._deps(reads, writes)
        ws = self._waits(e, deps)
        key = "prog_" + e
        if inc:
            self.cnt[e] += 1
            tok = (key, self.cnt[e])
            incs = [(key, 1)]
        else:
            tok = (key, self.cnt[e] + 1)
            incs = []
        self.stream[e].append((ws, fn, incs))
        for b in reads:
            if b.r.get(key, 0) < tok[1]:
                b.r[key] = tok[1]
        for b in writes:
            b.w = tok
            b.r = {}
        return tok

    def dma(self, q, fn, semname, reads=(), writes=()):
        deps = self._deps(reads, writes)
        ws = self._waits(q, deps)
        self.semkeys.add(semname)
        self.dma_cnt[semname] = self.dma_cnt.get(semname, 0) + 1
        tok = (semname, 16 * self.dma_cnt[semname])
        self.stream[q].append((ws, fn, [(semname, 16)]))
        for b in reads:
            if b.r.get(semname, 0) < tok[1]:
                b.r[semname] = tok[1]
        for b in writes:
            b.w = tok
            b.r = {}
        return tok

    def barrier(self):
        allv = {"prog_" + e: self.cnt[e] for e in ("pe", "act", "dve", "pool") if self.cnt[e] > 0}
        for k, c in self.dma_cnt.items():
            allv[k] = 16 * c
        for e in self.ENG:
            ws = self._waits(e, allv)
            if ws:
                self.stream[e].append((ws, None, []))

    def replay(self, e, eng, sems):
        for ws, fn, incs in self.stream[e]:
            for k, v in ws:
                eng.wait_ge(sems[k], v)
            if fn is None:
                continue
            ins = fn(eng)
            for k, v in incs:
                ins = ins.then_inc(sems[k], v)


def build(cfg):
    D, DFF, NT, NQ = cfg.D, cfg.DFF, cfg.NT, cfg.NQ
    KC, FC, H, GD, CG, NS, NKB = cfg.KC, cfg.FC, cfg.H, cfg.GD, cfg.CG, cfg.NS, cfg.NKB
    OC, NPASS, NOG, OW = cfg.OC, cfg.NPASS, cfg.NOG, cfg.OW
    scale = 128 ** -0.5
    lam_init = 0.8 - 0.6 * math.exp(-0.3 * 1)

    nc = bass.Bass("TRN2", target_bir_lowering=False)

    def din(name, shape, dt=F32):
        return nc.dram_tensor(name, list(shape), dt, kind="ExternalInput").ap()

    def dscr(name, shape, dt):
        return nc.dram_tensor(name, list(shape), dt, kind="Internal").ap()

    xt_d = din("xt", [NS * 512, D])
    halo_d = din("halo", [NS * 128, D])
    bandm_d = din("bandm", [NS * 128, 4 * 4 * 128], BF16)
    bandh_d = din("bandh", [NS * 64, 2 * 4 * 128], BF16)
    tabs_d = din("tabs", [NQ * 128, 3 * NKB])
    base0_d = din("base0", [128, 512])
    w0_d = din("w0", [128, 896])
    ident_d = din("ident", [128, 128], BF16)
    ones_d = din("ones", [128, 128], BF16)
    gcols_d = din("gcols", [128, 4 * KC])
    gfin_d = din("gfin", [128, D])
    pscb_d = din("pscb", [128, D])
    subg_d = din("subg", [128, 2])
    lamb_d = din("lamb", [128, 4 * 128])
    poolw_d = din("pool_w", [4 * GD, GD])
    wqkv_d = din("w_qkv", [D, 3 * D])
    wo_d = din("w_o", [D, D])
    wg_d = din("w_gate", [2 * D, DFF])
    wu_d = din("w_up", [2 * D, DFF])
    wd_d = din("w_down", [2 * DFF, D])
    y_d = nc.dram_tensor("y", [NQ * 512, D], F32, kind="ExternalOutput").ap()

    wqkv_b = dscr("wqkv_b", [D, 3 * D], BF16)
    wo_b = dscr("wo_b", [D, D], BF16)
    wg_b = dscr("wg_b", [2 * D, DFF], BF16)
    wu_b = dscr("wu_b", [2 * D, DFF], BF16)
    wd_b = dscr("wd_b", [2 * DFF, D], BF16)
    kt_s = dscr("kt_s", [H * 2 * 128, NS * 512], BF16)
    v_s = dscr("v_s", [NS * 512, D], BF16)
    qt_s = dscr("qt_s", [H * 2 * 128, NQ * 512], BF16)
    h1_s = dscr("h1_s", [NQ * 512, D], F32)

    tr = Tracker()
    es = ExitStack()

    def sb(name, shape, dt):
        return es.enter_context(nc.sbuf_tensor("sb_" + name, list(shape), dt))

    def ps(name):
        return es.enter_context(nc.psum_tensor(name, [128, 512], F32))

    xt = sb("xt", [128, 4, D], F32)
    hn = sb("hn", [128, 4, D], BF16)
    cT = sb("cT", [128, KC, 512], BF16)
    ARW = max(FC * 512, 22 * 1024)
    arena = sb("arena", [128, ARW], BF16)
    slabs = [sb("slab%d" % i, [128, KC, 256], BF16) for i in range(4)]
    wdc = [sb("wdc%d" % i, [128, 2, OC], BF16) for i in range(3)]
    sg = [sb("sg%d" % i, [128, 512], F32) for i in range(2)]
    PHW = 18 * 1024
    parena = sb("parena", [128, PHW], BF16)
    ident = sb("ident_sb", [128, 128], BF16)
    ones = sb("ones_sb", [128, 128], BF16)
    gcols = sb("gcols_sb", [128, 4 * KC], F32)
    subg = sb("subg_sb", [128, 2], F32)
    small = sb("small", [128, 32], F32)
    lamc = sb("lamc", [128, 8], F32)
    pb = [ps("ps%d" % i) for i in range(8)]

    def carve(base_ap_t, off, n, dt, shape=None):
        a = base_ap_t[:, off:off + n]
        if dt == F32:
            a = a.bitcast(F32)
        return a

    actT = arena[:, 0:FC * 512].rearrange("p (f t) -> p f t", f=FC)
    o = [0]

    def acarve(nbf, dt):
        a = carve(arena, o[0], nbf, dt)
        o[0] += nbf
        return a
    hl = arena[0:64, 0:4 * D].bitcast(F32).rearrange("p (a c) -> p a c", a=2)
    QT = [acarve(1024, BF16).rearrange("p (j t) -> p j t", j=2) for _ in range(2)]
    KTc = [acarve(1024, BF16).rearrange("p (j t) -> p j t", j=2) for _ in range(3)]
    Vc = [acarve(1024, BF16).rearrange("p (b e) -> p b e", b=4) for _ in range(3)]
    Sp = [acarve(1024, F32) for _ in range(2)]
    PT = [[acarve(512, BF16) for _ in range(2)] for _ in range(2)]
    Rr = [acarve(1024, F32) for _ in range(2)]
    of = acarve(2048, F32).rearrange("p (e t) -> p e t", e=2)
    t1s = [acarve(1024, F32) for _ in range(2)]
    sq = acarve(1024, BF16).rearrange("p (e t) -> p e t", e=2)
    rs = acarve(1024, F32)
    assert o[0] <= ARW

    po = [0]

    def pcarve(nbf, dt, parts=128):
        a = parena[0:parts, po[0]:po[0] + nbf]
        if dt == F32:
            a = a.bitcast(F32)
        po[0] += nbf
        return a
    hnh = pcarve(2 * D, BF16, 64).rearrange("p (a c) -> p a c", a=2)
    pw = pcarve(4 * CG * GD, BF16).rearrange("p (g k d) -> p g k d", g=4, k=CG)
    bandm = pcarve(4 * 4 * 128, BF16).rearrange("p (s g t) -> p s g t", s=4, g=4)
    bandh = pcarve(2 * 4 * 128, BF16, 64).rearrange("p (a g t) -> p a g t", a=2, g=4)
    ktsb = [pcarve(512, BF16) for _ in range(2)]
    vsb = [pcarve(1024, BF16).rearrange("p (s e) -> p s e", s=4) for _ in range(2)]
    assert po[0] <= PHW, po[0]
    po[0] = 0
    base0 = pcarve(1024, F32)
    w0 = pcarve(1792, F32)
    tabs = pcarve(2 * 3 * NKB, F32).rearrange("p (a k) -> p a k", a=3)
    biasrow = pcarve(2 * NKB, F32)
    sgnrow = pcarve(2 * NKB, F32)
    gfin = pcarve(2 * D, F32)
    assert po[0] <= PHW, po[0]

    B = {}

    def bf(name):
        if name not in B:
            B[name] = Buf(name)
        return B[name]

    PB = [bf("psum%d" % i) for i in range(8)]

    def mm(out, lhsT, rhs, start, stop, reads, wbuf, last):
        tr.op("pe", lambda e: e.matmul(out, lhsT, rhs, start=start, stop=stop),
              reads=reads, writes=[wbuf], inc=last)

    def tp(out, in_, reads, wbuf, last):
        tr.op("pe", lambda e: e.transpose(out, in_, ident[:, :]), reads=reads, writes=[wbuf], inc=last)

    WBK = {}
    step = 256
    conv = [("g0", wg_d, wg_b, 0, D), ("u0", wu_d, wu_b, 0, D), ("d0", wd_d, wd_b, 0, DFF), ("qkv", wqkv_d, wqkv_b, 0, D),
            ("o", wo_d, wo_b, 0, D), ("g1", wg_d, wg_b, D, 2 * D), ("u1", wu_d, wu_b, D, 2 * D), ("d1", wd_d, wd_b, DFF, 2 * DFF)]
    for key, src, dst, ra, rb in conv:
        WBK[key] = bf("wb_" + key)
        for r0 in range(ra, rb, step):
            tr.dma("pool", (lambda s_, d_, r0_: (lambda e: e.dma_start(out=d_[r0_:r0_ + step, :], in_=s_[r0_:r0_ + step, :])))(src, dst, r0),
                   "s_wc_" + key, writes=[WBK[key]])
    CB = bf("consts")
    for dst, src in ((ident[:, :], ident_d), (ones[:, :], ones_d), (gcols[:, :], gcols_d), (subg[:, :], subg_d)):
        tr.dma("sp", (lambda d_, s_: (lambda e: e.dma_start(out=d_, in_=s_[:, :])))(dst, src), "s_const", writes=[CB])
    pwtmp = arena[:, 0:2 * 4 * CG * GD].bitcast(F32).rearrange("p (g k d) -> p g k d", g=4, k=CG)
    psc_t = xt[:, 0, :]
    AR = bf("arena")
    XT = bf("xt")
    PW = bf("pw")
    tr.dma("sp", lambda e: e.dma_start(out=pwtmp, in_=poolw_d.rearrange("(g k p) d -> p g k d", g=4, k=CG)), "s_arena", writes=[AR])
    tr.dma("sp", lambda e: e.dma_start(out=psc_t, in_=pscb_d[:, :]), "s_xt", writes=[XT])
    for g in range(4):
        for k in range(CG):
            tr.op("dve", (lambda g_, k_: (lambda e: e.tensor_tensor(out=pw[:, g_, k_, :], in0=pwtmp[:, g_, k_, :], in1=psc_t[:, g_ * GD:(g_ + 1) * GD], op=ALU.mult)))(g, k),
                  reads=[AR, XT], writes=[PW])
    lamt = xt[:, 1, 0:512]
    lamp = xt[:, 2, 0:256]
    LM = bf("lamtmp")
    SM = bf("small")
    tr.dma("sp", lambda e: e.dma_start(out=lamt, in_=lamb_d[:, :]), "s_lam", writes=[LM])
    tr.op("dve", lambda e: e.tensor_tensor(out=lamp[:, 0:128], in0=lamt[:, 0:128], in1=lamt[:, 128:256], op=ALU.mult), reads=[LM], writes=[LM])
    tr.op("dve", lambda e: e.tensor_tensor(out=lamp[:, 128:256], in0=lamt[:, 256:384], in1=lamt[:, 384:512], op=ALU.mult), reads=[LM], writes=[LM])
    tr.op("dve", lambda e: e.reduce_sum(out=lamc[:, 0:1], in_=lamp[:, 0:128], axis=AX.X), reads=[LM], writes=[SM])
    tr.op("dve", lambda e: e.reduce_sum(out=lamc[:, 1:2], in_=lamp[:, 128:256], axis=AX.X), reads=[LM], writes=[SM])
    tr.op("act", lambda e: e.activation(out=lamc[:, 2:4], in_=lamc[:, 0:2], func=AF.Exp), reads=[SM], writes=[SM])
    tr.op("dve", lambda e: e.tensor_tensor(out=lamc[:, 4:5], in0=lamc[:, 2:3], in1=lamc[:, 3:4], op=ALU.subtract), reads=[SM], writes=[SM])
    tr.op("dve", lambda e: e.tensor_scalar(out=lamc[:, 5:6], in0=lamc[:, 4:5], scalar1=lam_init, scalar2=None, op0=ALU.add), reads=[SM], writes=[SM])
    lam_col = lamc[:, 5:6]
    tr.op("dve", lambda e: e.memset(lamc[:, 6:7], EPS), writes=[SM])
    eps_col = lamc[:, 6:7]
    tr.op("dve", lambda e: e.tensor_scalar(out=subg[:, :], in0=subg[:, :], scalar1=(1.0 - lam_init), scalar2=None, op0=ALU.mult), reads=[CB], writes=[CB])
    tr.barrier()

    HNS = [bf("hn%d" % i) for i in range(4)]
    CT = bf("cT")
    SL = [bf("slab%d" % i) for i in range(4)]
    WDC = [bf("wdc%d" % i) for i in range(3)]
    SG = [bf("sg%d" % i) for i in range(2)]
    slab_i = [0]
    wdc_i = [0]
    sg_i = [0]
    pbi = [0]

    def next_pb():
        i = pbi[0] % 8
        pbi[0] += 1
        return i

    def rms_stats(col0, nsub=4):
        tr.op("dve", lambda e: e.memset(small[:, col0:col0 + nsub], 0.0), writes=[SM])
        for s in range(nsub):
            if s < 2:
                tr.op("act", (lambda s_: (lambda e: e.activation(out=hn[:, s_, :], in_=xt[:, s_, :], func=AF.Square, accum_out=small[:, col0 + s_:col0 + s_ + 1])))(s),
                      reads=[XT], writes=[HNS[s], SM])
            else:
                tr.op("dve", (lambda s_: (lambda e: e.scalar_tensor_tensor(out=hn[:, s_, :], in0=xt[:, s_, :], scalar=1.0, in1=xt[:, s_, :], op0=ALU.mult, op1=ALU.mult, accum_out=small[:, col0 + s_:col0 + s_ + 1])))(s),
                      reads=[XT], writes=[HNS[s], SM])
        tr.op("act", lambda e: e.activation(out=small[:, col0:col0 + nsub], in_=small[:, col0:col0 + nsub], func=AF.Sqrt, bias=eps_col, scale=1.0 / D), reads=[SM], writes=[SM])
        tr.op("dve", lambda e: e.reciprocal(out=small[:, col0:col0 + nsub], in_=small[:, col0:col0 + nsub]), reads=[SM], writes=[SM])

    def make_hn(col0):
        for s in range(4):
            if s % 2 == 0:
                tr.op("dve", (lambda s_: (lambda e: e.tensor_scalar(out=hn[:, s_, :], in0=xt[:, s_, :], scalar1=small[:, col0 + s_:col0 + s_ + 1], scalar2=None, op0=ALU.mult)))(s),
                      reads=[XT, SM], writes=[HNS[s]])
            else:
                tr.op("act", (lambda s_: (lambda e: e.activation(out=hn[:, s_, :], in_=xt[:, s_, :], func=AF.Copy, scale=small[:, col0 + s_:col0 + s_ + 1])))(s),
                      reads=[XT, SM], writes=[HNS[s]])

    def transpose_hn(gi):
        for k in range(KC):
            b = next_pb()
            pv = pb[b][:, :].bitcast(BF16)
            for s in range(4):
                tp(pv[:, s * 128:(s + 1) * 128], hn[:, s, k * 128:(k + 1) * 128], [HNS[s], CB], PB[b], last=(s == 3))
            eng = "act" if k % 2 == 0 else "dve"
            if eng == "act":
                tr.op("act", (lambda k_, pv_: (lambda e: e.activation(out=cT[:, k_, :], in_=pv_[:, 0:512], func=AF.Copy, scale=gcols[:, gi * KC + k_:gi * KC + k_ + 1])))(k, pv),
                      reads=[PB[b], CB], writes=[CT])
            else:
                tr.op("dve", (lambda k_, pv_: (lambda e: e.tensor_scalar(out=cT[:, k_, :], in0=pv_[:, 0:512], scalar1=gcols[:, gi * KC + k_:gi * KC + k_ + 1], scalar2=None, op0=ALU.mult)))(k, pv),
                      reads=[PB[b], CB], writes=[CT])

    def load_slab(src_ap, c0, wb):
        i = slab_i[0] % 4
        slab_i[0] += 1
        tr.dma("sp", (lambda i_: (lambda e: e.dma_start(out=slabs[i_][:, :, :], in_=src_ap[:, c0:c0 + 256].rearrange("(k p) f -> p k f", p=128))))(i),
               "s_slab%d" % i, reads=[wb], writes=[SL[i]])
        return i

    AT = bf("actT")

    def ffn(layer, gi, col0):
        rms_stats(col0)
        make_hn(col0)
        transpose_hn(gi)
        wg_l = wg_b[layer * D:(layer + 1) * D, :]
        wu_l = wu_b[layer * D:(layer + 1) * D, :]
        wd_l = wd_b[layer * DFF:(layer + 1) * DFF, :]
        nfg = DFF // 256
        pend = None
        WG, WU, WD = WBK["g%d" % layer], WBK["u%d" % layer], WBK["d%d" % layer]
        nxt = (load_slab(wg_l, 0, WG), load_slab(wu_l, 0, WU))
        for fg in range(nfg):
            cur = nxt
            if fg + 1 < nfg:
                nxt = (load_slab(wg_l, (fg + 1) * 256, WG), load_slab(wu_l, (fg + 1) * 256, WU))
            for fc in range(2):
                f = fg * 2 + fc
                bg, bu = next_pb(), next_pb()
                for (si, b) in ((cur[0], bg), (cur[1], bu)):
                    for k in range(KC):
                        mm(pb[b][:, :], slabs[si][:, k, fc * 128:(fc + 1) * 128], cT[:, k, :], k == 0, k == KC - 1,
                           [SL[si], CT], PB[b], last=(k == KC - 1))
                gi_ = sg_i[0] % 2
                sg_i[0] += 1
                tr.op("act", (lambda b_, g_: (lambda e: e.activation(out=sg[g_][:, :], in_=pb[b_][:, :], func=AF.Silu)))(bg, gi_),
                      reads=[PB[bg]], writes=[SG[gi_]])
                tr.op("dve", (lambda b_, g_, f_: (lambda e: e.tensor_tensor(out=actT[:, f_, :], in0=sg[g_][:, :], in1=pb[b_][:, :], op=ALU.mult)))(bu, gi_, f),
                      reads=[SG[gi_], PB[bu]], writes=[AT])
        for p_ in range(NPASS):
            banks = [[next_pb() for _ in range(NOG)] for _ in range(4)]
            nch = FC // 2
            def ld(c):
                i = wdc_i[0] % 3
                wdc_i[0] += 1
                tr.dma("sp", (lambda i_, c_, pp_: (lambda e: e.dma_start(out=wdc[i_][:, :, :], in_=wd_l[c_ * 256:(c_ + 1) * 256, pp_ * OC:(pp_ + 1) * OC].rearrange("(a p) d -> p a d", p=128))))(i, c, p_),
                       "s_wdc%d" % i, reads=[WD], writes=[WDC[i]])
                return i
            ring = [ld(c_) for c_ in range(min(2, nch))]
            for c in range(nch):
                cu = ring.pop(0)
                if c + 2 < nch:
                    ring.append(ld(c + 2))
                for a in range(2):
                    f = c * 2 + a
                    for s in range(4):
                        for og in range(NOG):
                            b = banks[s][og]
                            mm(pb[b][:, 0:OW], actT[:, f, s * 128:(s + 1) * 128], wdc[cu][:, a, og * OW:(og + 1) * OW], f == 0, f == FC - 1,
                               [AT, WDC[cu]], PB[b], last=(f == FC - 1 or (a == 1 and s == 3 and og == NOG - 1)))
            for s in range(4):
                for og in range(NOG):
                    b = banks[s][og]
                    c0 = p_ * OC + og * OW
                    tr.op("dve", (lambda s_, b_, c0_: (lambda e: e.tensor_tensor(out=xt[:, s_, c0_:c0_ + OW], in0=xt[:, s_, c0_:c0_ + OW], in1=pb[b_][:, 0:OW], op=ALU.add)))(s, b, c0),
                          reads=[XT, PB[b]], writes=[XT])

    HL = AR
    BD = bf("band")
    HH = bf("hnh")
    KSB = [bf("ktsb%d" % i) for i in range(2)]
    VSB = [bf("vsb%d" % i) for i in range(2)]
    KTS, VS, QTS, H1S = bf("kt_s"), bf("v_s"), bf("qt_s"), bf("h1_s")
    ksb_i = [0]
    vsb_i = [0]
    def load_x_band(sl):
        tr.dma("pool", lambda e: e.dma_start(out=xt[:, :, :], in_=xt_d[sl * 512:(sl + 1) * 512, :].rearrange("(s p) c -> p s c", p=128)), "s_xt", writes=[XT])
        tr.dma("pool", lambda e: e.dma_start(out=bandm, in_=bandm_d[sl * 128:(sl + 1) * 128, :].rearrange("p (s g t) -> p s g t", s=4, g=4)), "s_band", writes=[BD])
        tr.dma("pool", lambda e: e.dma_start(out=bandh, in_=bandh_d[sl * 64:(sl + 1) * 64, :].rearrange("p (a g t) -> p a g t", a=2, g=4)), "s_band", writes=[BD])

    for slot in range(NS):
        isq = slot < NQ
        if slot == 0:
            load_x_band(0)
        tr.dma("sp", (lambda sl: (lambda e: e.dma_start(out=hl, in_=halo_d[sl * 128:(sl + 1) * 128, :].rearrange("(a p) c -> p a c", p=64))))(slot), "s_arena", writes=[AT])
        rms_stats(0)
        tr.op("dve", lambda e: e.memset(small[0:64, 4:6], 0.0), writes=[SM])
        for a in range(2):
            tr.op("act", (lambda a_: (lambda e: e.activation(out=hnh[:, a_, :], in_=hl[:, a_, :], func=AF.Square, accum_out=small[0:64, 4 + a_:5 + a_])))(a),
                  reads=[AT], writes=[HH, SM])
        tr.op("act", lambda e: e.activation(out=small[0:64, 4:6], in_=small[0:64, 4:6], func=AF.Sqrt, bias=lamc[0:64, 6:7], scale=1.0 / D), reads=[SM], writes=[SM])
        tr.op("dve", lambda e: e.reciprocal(out=small[0:64, 4:6], in_=small[0:64, 4:6]), reads=[SM], writes=[SM])
        make_hn(0)
        for a in range(2):
            tr.op("dve", (lambda a_: (lambda e: e.tensor_scalar(out=hnh[:, a_, :], in0=hl[:, a_, :], scalar1=small[0:64, 4 + a_:5 + a_], scalar2=None, op0=ALU.mult)))(a),
                  reads=[AT, SM], writes=[HH])
        for k in range(KC):
            g = k // CG
            b = next_pb()
            for s in range(4):
                a, hp = s // 2, 32 * (s % 2)
                mm(pb[b][:, s * 128:(s + 1) * 128], hn[:, s, k * 128:(k + 1) * 128], bandm[:, s, g, :], True, False, [HNS[s], BD], PB[b], last=False)
                mm(pb[b][:, s * 128:(s + 1) * 128], hnh[hp:hp + 32, a, k * 128:(k + 1) * 128], bandh[hp:hp + 32, a, g, :], False, True, [HH, BD], PB[b], last=(s == 3))
            if k % 2 == 0:
                tr.op("act", (lambda k_, b_: (lambda e: e.activation(out=cT[:, k_, :], in_=pb[b_][:, :], func=AF.Copy, scale=gcols[:, k_:k_ + 1])))(k, b),
                      reads=[PB[b], CB], writes=[CT])
            else:
                tr.op("dve", (lambda k_, b_: (lambda e: e.tensor_scalar(out=cT[:, k_, :], in0=pb[b_][:, :], scalar1=gcols[:, k_:k_ + 1], scalar2=None, op0=ALU.mult)))(k, b),
                      reads=[PB[b], CB], writes=[CT])
        for s in range(4):
            for g in range(4):
                b = next_pb()
                for kk in range(CG):
                    mm(pb[b][:, 0:GD], cT[:, g * CG + kk, s * 128:(s + 1) * 128], pw[:, g, kk, :], kk == 0, kk == CG - 1, [CT, PW], PB[b], last=(kk == CG - 1))
                tr.op("dve", (lambda s_, g_, b_: (lambda e: e.tensor_tensor(out=xt[:, s_, g_ * GD:(g_ + 1) * GD], in0=xt[:, s_, g_ * GD:(g_ + 1) * GD], in1=pb[b_][:, 0:GD], op=ALU.add)))(s, g, b),
                      reads=[XT, PB[b]], writes=[XT])
        ffn(0, 1, 8)
        if isq:
            tr.dma("pool", (lambda sl: (lambda e: e.dma_start(out=h1_s[sl * 512:(sl + 1) * 512, :].rearrange("(s p) c -> p s c", p=128), in_=xt[:, :, :])))(slot),
                   "s_h1st", reads=[XT], writes=[H1S])
        rms_stats(12)
        make_hn(12)
        if slot + 1 < NS:
            load_x_band(slot + 1)
        transpose_hn(2)
        jobs = []
        for h in range(H):
            jobs.append(("k", h))
            jobs.append(("v", h))
            if isq:
                jobs.append(("q", h))

        def jcol(job):
            kind, h = job
            return {"q": 0, "k": D, "v": 2 * D}[kind] + h * 256
        nxs = load_slab(wqkv_b, jcol(jobs[0]), WBK["qkv"])
        for ji, job in enumerate(jobs):
            cs = nxs
            if ji + 1 < len(jobs):
                nxs = load_slab(wqkv_b, jcol(jobs[ji + 1]), WBK["qkv"])
            kind, h = job
            if kind in ("k", "q"):
                for j in range(2):
                    b = next_pb()
                    for k in range(KC):
                        mm(pb[b][:, :], slabs[cs][:, k, j * 128:(j + 1) * 128], cT[:, k, :], k == 0, k == KC - 1, [SL[cs], CT], PB[b], last=(k == KC - 1))
                    ki = ksb_i[0] % 2
                    ksb_i[0] += 1
                    tr.op("act", (lambda b_, ki_: (lambda e: e.activation(out=ktsb[ki_], in_=pb[b_][:, :], func=AF.Copy)))(b, ki), reads=[PB[b]], writes=[KSB[ki]])
                    dst, DB, ncol = (kt_s, KTS, NS * 512) if kind == "k" else (qt_s, QTS, NQ * 512)
                    r0 = (h * 2 + j) * 128
                    tr.dma("pool", (lambda dst_, r0_, sl, ki_: (lambda e: e.dma_start(out=dst_[r0_:r0_ + 128, sl * 512:(sl + 1) * 512], in_=ktsb[ki_])))(dst, r0, slot, ki),
                           "s_kst%d" % ki, reads=[KSB[ki]], writes=[DB])
            else:
                vi = vsb_i[0] % 2
                vsb_i[0] += 1
                for s in range(4):
                    b = next_pb()
                    for k in range(KC):
                        mm(pb[b][:, 0:256], cT[:, k, s * 128:(s + 1) * 128], slabs[cs][:, k, :], k == 0, k == KC - 1, [CT, SL[cs]], PB[b], last=(k == KC - 1))
                    tr.op("dve", (lambda b_, vi_, s_: (lambda e: e.tensor_copy(out=vsb[vi_][:, s_, :], in_=pb[b_][:, 0:256])))(b, vi, s), reads=[PB[b]], writes=[VSB[vi]])
                tr.dma("pool", (lambda sl, h_, vi_: (lambda e: e.dma_start(out=v_s[sl * 512:(sl + 1) * 512, h_ * 256:(h_ + 1) * 256].rearrange("(s p) e -> p s e", p=128), in_=vsb[vi_])))(slot, h, vi),
                       "s_vst%d" % vi, reads=[VSB[vi]], writes=[VS])
    tr.barrier()

    QC = bf("qconst")
    for dst, src in ((base0, base0_d), (w0, w0_d), (gfin, gfin_d)):
        tr.dma("sp", (lambda d_, s_: (lambda e: e.dma_start(out=d_, in_=s_[:, :])))(dst, src), "s_const", writes=[QC])
    TB = bf("tabs")
    ROW = bf("rows")
    QTB = [bf("QT%d" % i) for i in range(2)]
    KTB = [bf("KTc%d" % i) for i in range(3)]
    VCB = [bf("Vc%d" % i) for i in range(3)]
    SPB = [bf("Sp%d" % i) for i in range(2)]
    PTB = [[bf("PT%d_%d" % (j, i)) for i in range(2)] for j in range(2)]
    RB = [bf("R%d" % i) for i in range(2)]
    T1B = [bf("t1_%d" % i) for i in range(2)]
    OFB = [bf("of%d" % i) for i in range(2)]
    SQB = [bf("sq%d" % i) for i in range(2)]
    RSB = bf("rs")
    qt_i = [0]
    kv_i = [0]
    pt_i = [0]
    slopes = [2.0 ** (-8.0 * (h + 1) / H) for h in range(H)]
    SKIP_TH = _SKIP_TH
    dmin = _min_dist(cfg)
    for qi in range(NQ):
        tr.barrier()
        tr.dma("sp", (lambda q_: (lambda e: e.dma_start(out=xt[:, :, :], in_=h1_s[q_ * 512:(q_ + 1) * 512, :].rearrange("(s p) c -> p s c", p=128))))(qi), "s_xt", reads=[H1S], writes=[XT])
        tr.dma("sp", (lambda q_: (lambda e: e.dma_start(out=tabs, in_=tabs_d[q_ * 128:(q_ + 1) * 128, :].rearrange("p (a k) -> p a k", a=3))))(qi), "s_tabs", writes=[TB])
        def ldkv(h, c):
            i = kv_i[0] % 3
            kv_i[0] += 1
            nb = min(4, NKB - 4 * c)
            tr.dma("sp", (lambda h_, c_, i_, nb_: (lambda e: e.dma_start(out=KTc[i_][:, :, 0:nb_ * 128], in_=kt_s[h_ * 256:(h_ + 1) * 256, c_ * 512:c_ * 512 + nb_ * 128].rearrange("(j p) t -> p j t", p=128))))(h, c, i, nb),
                   "s_kt%d" % i, reads=[KTS], writes=[KTB[i]])
            tr.dma("sp", (lambda h_, c_, i_, nb_: (lambda e: e.dma_start(out=Vc[i_][:, 0:nb_, :], in_=v_s[c_ * 512:c_ * 512 + nb_ * 128, h_ * 256:(h_ + 1) * 256].rearrange("(b p) e -> p b e", p=128))))(h, c, i, nb),
                   "s_vc%d" % i, reads=[VS], writes=[VCB[i]])
            return i

        def head_setup(h):
            nchunk = (NKB + 3) // 4
            need = [kb for kb in range(NKB) if slopes[h] * dmin[qi, kb] < SKIP_TH]
            plan = []
            for c in range(nchunk):
                bis = [kb - 4 * c for kb in need if kb // 4 == c]
                if bis:
                    plan.append((c, bis))
            qb_ = qt_i[0] % 2
            qt_i[0] += 1
            tr.dma("sp", (lambda h_, q_, b_: (lambda e: e.dma_start(out=QT[b_], in_=qt_s[h_ * 256:(h_ + 1) * 256, q_ * 512:(q_ + 1) * 512].rearrange("(j p) t -> p j t", p=128))))(h, qi, qb_),
                   "s_qt%d" % qb_, reads=[QTS], writes=[QTB[qb_]])
            ring = [ldkv(h, plan[i_][0]) for i_ in range(min(2, len(plan)))]
            return dict(plan=plan, qb=qb_, ring=ring, last_need=need[-1])

        hs_next = head_setup(0)
        for h in range(H):
            fh = -slopes[h] / scale
            tr.op("dve", (lambda h_: (lambda e: e.scalar_tensor_tensor(out=biasrow, in0=tabs[:, 1, :], scalar=-slopes[h_], in1=tabs[:, 2, :], op0=ALU.mult, op1=ALU.add)))(h), reads=[TB], writes=[ROW])
            tr.op("dve", (lambda f_: (lambda e: e.tensor_scalar(out=sgnrow, in0=tabs[:, 0, :], scalar1=f_, scalar2=None, op0=ALU.mult)))(fh), reads=[TB], writes=[ROW])
            hs = hs_next
            plan, qb_, ring, last_need = hs["plan"], hs["qb"], hs["ring"], hs["last_need"]
            first = True
            pend_pv = [None]
            for pi_, (c, bis) in enumerate(plan):
                ckv = ring.pop(0)
                for bn, bi in enumerate(bis):
                    kb = 4 * c + bi
                    lastkb = (kb == last_need)
                    diag = (kb // 4 == qi) and kb < 4 * NT
                    pts = []
                    for j in range(2):
                        mm(pb[j][:, :], KTc[ckv][:, j, bi * 128:(bi + 1) * 128], QT[qb_][:, j, :], True, True, [KTB[ckv], QTB[qb_]], PB[j], last=(j == 1))
                    for j in range(2):
                        if diag:
                            o_ = kb % 4
                            tr.op("dve", (lambda j_, o__, f_: (lambda e: e.scalar_tensor_tensor(out=Sp[j_], in0=w0[:, 384 - 128 * o__:384 - 128 * o__ + 512], scalar=f_, in1=pb[j_][:, :], op0=ALU.mult, op1=ALU.add)))(j, o_, fh),
                                  reads=[QC, PB[j]], writes=[SPB[j]])
                        else:
                            tr.op("dve", (lambda j_, kb_: (lambda e: e.scalar_tensor_tensor(out=Sp[j_], in0=base0, scalar=sgnrow[:, kb_:kb_ + 1], in1=pb[j_][:, :], op0=ALU.mult, op1=ALU.add)))(j, kb),
                                  reads=[QC, ROW, PB[j]], writes=[SPB[j]])
                        pi = pt_i[0] % 2
                        tr.op("act", (lambda j_, kb_, pi_: (lambda e: e.activation(out=PT[j_][pi_], in_=Sp[j_], func=AF.Exp, bias=biasrow[:, kb_:kb_ + 1], scale=scale)))(j, kb, pi),
                              reads=[SPB[j], ROW], writes=[PTB[j][pi]])
                        pts.append(pi)
                    pt_i[0] += 1

                    def pv(pts=pts, ckv=ckv, bi=bi, first=first, lastkb=lastkb):
                        for j in range(2):
                            pi = pts[j]
                            for ec in range(2):
                                b = 2 + 2 * j + ec
                                mm(pb[b][:, :], Vc[ckv][:, bi, ec * 128:(ec + 1) * 128], PT[j][pi], first, lastkb, [VCB[ckv], PTB[j][pi]], PB[b], last=False)
                            mm(pb[6 + j][:, :], ones[:, :], PT[j][pi], first, lastkb, [CB, PTB[j][pi]], PB[6 + j], last=(j == 1))
                    if pend_pv[0] is not None:
                        pend_pv[0]()
                    pend_pv[0] = pv
                    first = False
                    if bn == 0 and pi_ + 2 < len(plan):
                        ring.append(ldkv(h, plan[pi_ + 2][0]))
            pend_pv[0]()
            pend_pv[0] = None
            if h + 1 < H:
                hs_next = head_setup(h + 1)
            for j in range(2):
                tr.op("dve", (lambda j_: (lambda e: e.reciprocal(out=Rr[j_], in_=pb[6 + j_][:, :])))(j), reads=[PB[6 + j]], writes=[RB[j]])
            for ec in range(2):
                tr.op("dve", (lambda ec_: (lambda e: e.scalar_tensor_tensor(out=t1s[ec_], in0=pb[4 + ec_][:, :], scalar=lam_col, in1=Rr[1], op0=ALU.mult, op1=ALU.mult)))(ec), reads=[PB[4 + ec], RB[1], SM], writes=[T1B[ec]])
            for ec in range(2):
                tr.op("dve", (lambda ec_: (lambda e: e.tensor_tensor(out=of[:, ec_, :], in0=pb[2 + ec_][:, :], in1=Rr[0], op=ALU.mult)))(ec), reads=[PB[2 + ec], RB[0]], writes=[OFB[ec]])
            for ec in range(2):
                tr.op("dve", (lambda ec_: (lambda e: e.tensor_tensor(out=of[:, ec_, :], in0=of[:, ec_, :], in1=t1s[ec_], op=ALU.subtract)))(ec), reads=[T1B[ec]], writes=[OFB[ec]])
            for ec in range(2):
                tr.op("act", (lambda ec_: (lambda e: e.activation(out=sq[:, ec_, :], in_=of[:, ec_, :], func=AF.Square)))(ec), reads=[OFB[ec]], writes=[SQB[ec]])
            for ec in range(2):
                mm(pb[0][:, :], ones[:, :], sq[:, ec, :], ec == 0, ec == 1, [CB, SQB[ec]], PB[0], last=(ec == 1))
            tr.op("act", lambda e: e.activation(out=rs, in_=pb[0][:, :], func=AF.Sqrt, bias=eps_col, scale=1.0 / 256.0), reads=[PB[0], SM], writes=[RSB])
            tr.op("dve", lambda e: e.reciprocal(out=rs, in_=rs), reads=[RSB], writes=[RSB])
            for ec in range(2):
                tr.op("dve", (lambda ec_, h_: (lambda e: e.scalar_tensor_tensor(out=cT[:, 2 * h_ + ec_, :], in0=of[:, ec_, :], scalar=subg[:, ec_:ec_ + 1], in1=rs, op0=ALU.mult, op1=ALU.mult)))(ec, h),
                      reads=[OFB[ec], RSB, CB], writes=[CT])
        tr.barrier()
        for p_ in range(NPASS):
            banks = [[next_pb() for _ in range(NOG)] for _ in range(4)]
            nch = KC // 2

            def ldo(c):
                i = wdc_i[0] % 3
                wdc_i[0] += 1
                tr.dma("sp", (lambda i_, c_, pp_: (lambda e: e.dma_start(out=wdc[i_][:, :, :], in_=wo_b[c_ * 256:(c_ + 1) * 256, pp_ * OC:(pp_ + 1) * OC].rearrange("(a p) d -> p a d", p=128))))(i, c, p_),
                       "s_wdc%d" % i, reads=[WBK["o"]], writes=[WDC[i]])
                return i
            ring = [ldo(c_) for c_ in range(min(2, nch))]
            for c in range(nch):
                cu = ring.pop(0)
                if c + 2 < nch:
                    ring.append(ldo(c + 2))
                for a in range(2):
                    k = c * 2 + a
                    for s in range(4):
                        for og in range(NOG):
                            b = banks[s][og]
                            mm(pb[b][:, 0:OW], cT[:, k, s * 128:(s + 1) * 128], wdc[cu][:, a, og * OW:(og + 1) * OW], k == 0, k == KC - 1,
                               [CT, WDC[cu]], PB[b], last=(k == KC - 1 or (a == 1 and s == 3 and og == NOG - 1)))
            for s in range(4):
                for og in range(NOG):
                    b = banks[s][og]
                    c0 = p_ * OC + og * OW
                    tr.op("dve", (lambda s_, b_, c0_: (lambda e: e.tensor_tensor(out=xt[:, s_, c0_:c0_ + OW], in0=xt[:, s_, c0_:c0_ + OW], in1=pb[b_][:, 0:OW], op=ALU.add)))(s, b, c0),
                          reads=[XT, PB[b]], writes=[XT])
        ffn(1, 3, 16)
        rms_stats(20)
        for s in range(4):
            tr.op("dve", (lambda s_: (lambda e: e.scalar_tensor_tensor(out=xt[:, s_, :], in0=xt[:, s_, :], scalar=small[:, 20 + s_:21 + s_], in1=gfin, op0=ALU.mult, op1=ALU.mult)))(s),
                  reads=[XT, SM, QC], writes=[XT])
        tr.dma("pool", (lambda q_: (lambda e: e.dma_start(out=y_d[q_ * 512:(q_ + 1) * 512, :].rearrange("(s p) c -> p s c", p=128), in_=xt[:, :, :])))(qi), "s_yst", reads=[XT], writes=[bf("y")])
    tr.barrier()

    sems = {}
    for k in sorted(tr.semkeys):
        sems[k] = es.enter_context(nc.semaphore(k))
    with nc.Block() as block:
        @block.tensor
        def _(e):
            tr.replay("pe", e, sems)

        @block.scalar
        def _(e):
            tr.replay("act", e, sems)

        @block.vector
        def _(e):
            tr.replay("dve", e, sems)

        @block.gpsimd
        def _(e):
            tr.replay("pool", e, sems)

        @block.sync
        def _(e):
            tr.replay("sp", e, sems)
    es.close()
    return nc


_band_cache = {}


def _band_for(pos_main, pos_halo, L):
    key = (tuple(pos_main.tolist()), tuple(pos_halo.tolist()), L)
    p0 = int(pos_main[0])
    rel = (tuple((pos_main - p0).tolist()) if p0 >= 0 else None, tuple(np.where(pos_halo >= 0, pos_halo - p0, -999).tolist()), min(p0, 40), min(L - p0, 300) if p0 >= 0 else -1)
    if rel in _band_cache:
        return _band_cache[rel]
    bm = np.zeros((128, 4, 128), np.float32)
    bh = np.zeros((32, 4, 128), np.float32)
    where = {}
    for i, p in enumerate(pos_main):
        if p >= 0:
            where[int(p)] = ("m", i)
    for i, p in enumerate(pos_halo):
        if p >= 0 and int(p) not in where:
            where[int(p)] = ("h", i)
    for r, p in enumerate(pos_main):
        if p < 0:
            continue
        p = int(p)
        for g, w in enumerate(WINDOWS):
            lo, hi = max(p - w // 2, 0), min(p + w // 2, L)
            inv = 1.0 / float(hi - lo)
            for u in range(lo, hi):
                kind, i = where[u]
                if kind == "m":
                    bm[i, g, r] += inv
                else:
                    bh[i, g, r] += inv
            bm[r, g, r] -= 1.0
    _band_cache[rel] = (bm, bh)
    return bm, bh


def _slot_positions(cfg, ntile, ctype):
    NT, NQ, NS = cfg.NT, cfg.NQ, cfg.NS
    half = ntile // 2
    tmap = [-1] * NT
    for i in range(half):
        tmap[i] = (ctype * half + i)
        tmap[NQ + i] = ((1 - ctype) * half + i)
    pos = -np.ones((NS, 512), np.int64)
    for sl in range(NT):
        if tmap[sl] >= 0:
            pos[sl] = N_META + 512 * tmap[sl] + np.arange(512)
    pos[NT, :N_META] = np.arange(N_META)
    return tmap, pos


def _min_dist(cfg):
    NT, NQ, NKB = cfg.NT, cfg.NQ, cfg.NKB
    dmin = np.full((NQ, NKB), np.inf)
    for ntile in (NT, NT // 2):
        for ctype in (0, 1):
            _, pos = _slot_positions(cfg, ntile, ctype)
            for qi in range(NQ):
                if pos[qi, 0] < 0:
                    continue
                q0, q1 = int(pos[qi, 0]), int(pos[qi, 0]) + 511
                for kb in range(NKB):
                    if kb == NKB - 1:
                        k0, k1 = 0, N_META - 1
                    else:
                        k0 = int(pos[kb // 4, (kb % 4) * 128])
                        if k0 < 0:
                            continue
                        k1 = k0 + 127
                    d = max(0, k0 - q1, q0 - k1)
                    dmin[qi, kb] = min(dmin[qi, kb], d)
    return dmin


def _core_layout(cfg, x_seq, meta, ctype):
    D, NT, NQ, NS, NKB = cfg.D, cfg.NT, cfg.NQ, cfg.NS, cfg.NKB
    S = x_seq.shape[0]
    L = S + N_META
    ntile = S // 512
    tmap, pos = _slot_positions(cfg, ntile, ctype)
    xt = np.zeros((NS * 512, D), np.float32)
    for sl in range(NT):
        if tmap[sl] >= 0:
            t = tmap[sl]
            xt[sl * 512:(sl + 1) * 512] = x_seq[512 * t:512 * (t + 1)]
    xt[NT * 512:NT * 512 + N_META] = meta

    def row_of(p):
        return meta[p] if p < N_META else x_seq[p - N_META]
    halo = np.zeros((NS, 2, 64, D), np.float32)
    bandm = np.zeros((NS, 128, 4, 4, 128), np.float32)
    bandh = np.zeros((NS, 64, 2, 4, 128), np.float32)
    for sl in range(NS):
        for s in range(4):
            pm = pos[sl, s * 128:(s + 1) * 128]
            ph = -np.ones(32, np.int64)
            if pm[0] >= 0:
                nvalid = int((pm >= 0).sum())
                p0, p1 = int(pm[0]), int(pm[0]) + nvalid
                cand = list(range(p0 - 8, p0)) + list(range(p1, p1 + 8))
                for i, p in enumerate(cand):
                    if 0 <= p < L:
                        ph[i] = p
                        halo[sl, s // 2, 32 * (s % 2) + i] = row_of(p)
                bm, bh = _band_for(pm, ph, L)
                bandm[sl, :, s] = bm
                bandh[sl, 32 * (s % 2):32 * (s % 2) + 32, s // 2] = bh
    tabs = np.zeros((NQ, 128, 3, NKB), np.float32)
    tabs[:, :, 0, :] = 1.0
    for qi in range(NQ):
        if pos[qi, 0] < 0:
            continue
        pq0 = int(pos[qi, 0])
        for kb in range(NKB):
            if kb == NKB - 1:
                tabs[qi, :, 0, kb] = 1.0
                tabs[qi, :, 1, kb] = pq0
                tabs[qi, N_META:, 2, kb] = NEG
                continue
            sl, sub = kb // 4, kb % 4
            if sl == qi:
                continue
            pk0 = int(pos[sl, sub * 128])
            if pk0 < 0:
                tabs[qi, :, 2, kb] = NEG
                continue
            dlt = pq0 - pk0
            tabs[qi, :, 0, kb] = 1.0 if dlt > 0 else -1.0
            tabs[qi, :, 1, kb] = abs(dlt)
    valid_q = [tmap[i] for i in range(NQ)]
    return dict(xt=xt, halo=halo.reshape(NS * 128, D),
                bandm=bandm.reshape(NS * 128, 2048).astype(NPBF),
                bandh=bandh.reshape(NS * 64, 1024).astype(NPBF),
                tabs=tabs.reshape(NQ * 128, 3 * NKB)), valid_q


def _shared_inputs(cfg, inp):
    D, KC = cfg.D, cfg.KC
    f = np.float32
    kk = np.arange(128)[:, None]
    base0 = (np.arange(512)[None, :] - kk).astype(f)
    w0 = np.abs(np.arange(896)[None, :] - 384 - kk).astype(f)
    gl = [inp["mixer_norm_g"][0], inp["ffn_norm_g"][0], inp["mixer_norm_g"][1], inp["ffn_norm_g"][1]]
    gcols = np.concatenate([np.asarray(g, f).reshape(KC, 128).T for g in gl], axis=1)
    lamb = np.concatenate([np.asarray(inp[k], f).reshape(1, 128) for k in ("lambda_q1", "lambda_k1", "lambda_q2", "lambda_k2")], axis=1)
    return dict(
        base0=np.ascontiguousarray(base0), w0=np.ascontiguousarray(w0),
        ident=np.eye(128, dtype=f).astype(NPBF), ones=np.ones((128, 128), f).astype(NPBF),
        gcols=np.ascontiguousarray(gcols),
        gfin=np.ascontiguousarray(np.broadcast_to(np.asarray(inp["final_norm_g"], f).reshape(1, D), (128, D))),
        pscb=np.ascontiguousarray(np.broadcast_to(np.asarray(inp["pool_scale"], f).reshape(1, D), (128, D))),
        subg=np.ascontiguousarray(np.asarray(inp["subln_g"], f).reshape(2, 128).T),
        lamb=np.ascontiguousarray(np.broadcast_to(lamb, (128, 512))),
        pool_w=np.ascontiguousarray(np.asarray(inp["pool_w"], f).reshape(4 * cfg.GD, cfg.GD)),
        w_qkv=np.ascontiguousarray(np.asarray(inp["w_qkv"], f).reshape(D, 3 * D)),
        w_o=np.ascontiguousarray(np.asarray(inp["w_o"], f).reshape(D, D)),
        w_gate=np.ascontiguousarray(np.asarray(inp["w_gate"], f).reshape(2 * D, cfg.DFF)),
        w_up=np.ascontiguousarray(np.asarray(inp["w_up"], f).reshape(2 * D, cfg.DFF)),
        w_down=np.ascontiguousarray(np.asarray(inp["w_down"], f).reshape(2 * cfg.DFF, D)),
    )


def run(cfg, inp):
    xp = np.asarray(inp["x_prompt"], np.float32)
    xs = np.asarray(inp["x_sample"], np.float32)
    meta = np.asarray(inp["meta_tokens"], np.float32)
    shared = _shared_inputs(cfg, inp)
    seqs = [xs[0], xs[1], xp[0], xp[1]]
    in_maps, vq = [], []
    for c in range(8):
        lay, valid_q = _core_layout(cfg, seqs[c // 2], meta, c % 2)
        m = dict(shared)
        m.update(lay)
        in_maps.append(m)
        vq.append(valid_q)
    nc = build(cfg)
    res = run_bass_kernel_spmd(nc, in_maps, core_ids=list(range(8)))
    yp = np.zeros_like(xp)
    ys = np.zeros_like(xs)
    outs = [ys[0], ys[1], yp[0], yp[1]]
    for c in range(8):
        y = np.asarray(res.results[c]["y"], dtype=np.float32)
        for qi, t in enumerate(vq[c]):
            if t >= 0:
                outs[c // 2][512 * t:512 * (t + 1)] = y[512 * qi:512 * (qi + 1)]
    return yp, ys


def kernel(**inputs):
    cfg = Cfg()
    return run(cfg, inputs)
```

```python
import math
from contextlib import ExitStack
import numpy as np
import ml_dtypes
import concourse.bass as bass
import concourse.mybir as mybir
from concourse.bass_utils import run_bass_kernel_spmd

F32 = mybir.dt.float32
BF16 = mybir.dt.bfloat16
ALU = mybir.AluOpType
AF = mybir.ActivationFunctionType
AX = mybir.AxisListType
NPBF = ml_dtypes.bfloat16

N_META = 16
WINDOWS = (2, 4, 8, 16)
EPS = 1e-6
NEG = -30000.0
_SKIP_TH = 60.0


class Cfg:
    def __init__(self, D=2048, DFF=5632, NT=32, NQ=16):
        self.D, self.DFF, self.NT, self.NQ = D, DFF, NT, NQ
        self.KC = D // 128
        self.FC = DFF // 128
        self.H = D // 256
        self.GD = D // 4
        self.CG = self.GD // 128
        self.NS = NT + 1
        self.NKB = 4 * NT + 1
        self.OC = min(D, 1024)
        self.NPASS = D // self.OC
        self.NOG = self.OC // 512 if self.OC >= 512 else 1
        self.OW = min(512, self.OC)


class Buf:
    def __init__(self, name):
        self.name = name
        self.w = None
        self.r = {}


class Tracker:
    ENG = ("pe", "act", "dve", "pool", "sp")

    def __init__(self):
        self.stream = {e: [] for e in self.ENG}
        self.cnt = {e: 0 for e in self.ENG}
        self.waited = {e: {} for e in self.ENG}
        self.dma_cnt = {}
        self.semkeys = set("prog_" + e for e in ("pe", "act", "dve", "pool"))

    def _deps(self, reads, writes):
        deps = {}

        def add(tok):
            if tok is None:
                return
            k, v = tok
            if deps.get(k, 0) < v:
                deps[k] = v
        for b in reads:
            add(b.w)
        for b in writes:
            add(b.w)
            for k, v in b.r.items():
                add((k, v))
        return deps

    def _waits(self, e, deps):
        ws = []
        for k, v in deps.items():
            if e == "pe" and k == "prog_pe":
                continue
            if self.waited[e].get(k, 0) >= v:
                continue
            self.waited[e][k] = v
            ws.append((k, v))
        return ws

    def op(self, e, fn, reads=(), writes=(), inc=True):
        deps = self._deps(reads, writes)
        ws = self._waits(e, deps)
        key = "prog_" + e
        if inc:
            self.cnt[e] += 1
            tok = (key, self.cnt[e])
            incs = [(key, 1)]
        else:
            tok = (key, self.cnt[e] + 1)
            incs = []
        self.stream[e].append((ws, fn, incs))
        for b in reads:
            if b.r.get(key, 0) < tok[1]:
                b.r[key] = tok[1]
        for b in writes:
            b.w = tok
            b.r = {}
        return tok

    def dma(self, q, fn, semname, reads=(), writes=()):
        deps = self._deps(reads, writes)
        ws = self._waits(q, deps)
        self.semkeys.add(semname)
        self.dma_cnt[semname] = self.dma_cnt.get(semname, 0) + 1
        tok = (semname, 16 * self.dma_cnt[semname])
        self.stream[q].append((ws, fn, [(semname, 16)]))
        for b in reads:
            if b.r.get(semname, 0) < tok[1]:
                b.r[semname] = tok[1]
        for b in writes:
            b.w = tok
            b.r = {}
        return tok

    def barrier(self, dma=True):
        allv = {"prog_" + e: self.cnt[e] for e in ("pe", "act", "dve", "pool") if self.cnt[e] > 0}
        if dma:
            for k, c in self.dma_cnt.items():
                allv[k] = 16 * c
        for e in self.ENG:
            ws = self._waits(e, allv)
            if ws:
                self.stream[e].append((ws, None, []))

    def replay(self, e, eng, sems):
        for ws, fn, incs in self.stream[e]:
            for k, v in ws:
                eng.wait_ge(sems[k], v)
            if fn is None:
                continue
            ins = fn(eng)
            for k, v in incs:
                ins = ins.then_inc(sems[k], v)


def build(cfg):
    D, DFF, NT, NQ = cfg.D, cfg.DFF, cfg.NT, cfg.NQ
    KC, FC, H, GD, CG, NS, NKB = cfg.KC, cfg.FC, cfg.H, cfg.GD, cfg.CG, cfg.NS, cfg.NKB
    OC, NPASS, NOG, OW = cfg.OC, cfg.NPASS, cfg.NOG, cfg.OW
    scale = 128 ** -0.5
    lam_init = 0.8 - 0.6 * math.exp(-0.3 * 1)

    nc = bass.Bass("TRN2", target_bir_lowering=False)

    def din(name, shape, dt=F32):
        return nc.dram_tensor(name, list(shape), dt, kind="ExternalInput").ap()

    def dscr(name, shape, dt):
        return nc.dram_tensor(name, list(shape), dt, kind="Internal").ap()

    xt_d = din("xt", [NS * 512, D])
    halo_d = din("halo", [NS * 128, D])
    bandm_d = din("bandm", [NS * 128, 4 * 4 * 128], BF16)
    bandh_d = din("bandh", [NS * 64, 2 * 4 * 128], BF16)
    tabs_d = din("tabs", [NQ * 128, 3 * NKB])
    base0_d = din("base0", [128, 512])
    w0_d = din("w0", [128, 896])
    ident_d = din("ident", [128, 128], BF16)
    ones_d = din("ones", [128, 128], BF16)
    gcols_d = din("gcols", [128, 4 * KC])
    gfin_d = din("gfin", [128, D])
    pscb_d = din("pscb", [128, D])
    subg_d = din("subg", [128, 2])
    lamb_d = din("lamb", [128, 4 * 128])
    poolw_d = din("pool_w", [4 * GD, GD])
    wqkv_d = din("w_qkv", [D, 3 * D])
    wo_d = din("w_o", [D, D])
    wg_d = din("w_gate", [2 * D, DFF])
    wu_d = din("w_up", [2 * D, DFF])
    wd_d = din("w_down", [2 * DFF, D])
    y_d = nc.dram_tensor("y", [NQ * 512, D], F32, kind="ExternalOutput").ap()

    wqkv_b = dscr("wqkv_b", [D, 3 * D], BF16)
    wo_b = dscr("wo_b", [D, D], BF16)
    wg_b = dscr("wg_b", [2 * D, DFF], BF16)
    wu_b = dscr("wu_b", [2 * D, DFF], BF16)
    wd_b = dscr("wd_b", [2 * DFF, D], BF16)
    kt_s = dscr("kt_s", [H * 2 * 128, NS * 512], BF16)
    v_s = dscr("v_s", [NS * 512, D], BF16)
    qt_s = dscr("qt_s", [H * 2 * 128, NQ * 512], BF16)
    h1_s = dscr("h1_s", [NQ * 512, D], F32)

    tr = Tracker()
    es = ExitStack()

    def sb(name, shape, dt):
        return es.enter_context(nc.sbuf_tensor("sb_" + name, list(shape), dt))

    def ps(name):
        return es.enter_context(nc.psum_tensor(name, [128, 512], F32))

    xt = sb("xt", [128, 4, D], F32)
    hn = sb("hn", [128, 4, D], BF16)
    cT = sb("cT", [128, KC, 512], BF16)
    ARW = max(FC * 512, 22 * 1024)
    arena = sb("arena", [128, ARW], BF16)
    slabs = [sb("slab%d" % i, [128, KC, 256], BF16) for i in range(4)]
    wdc = [sb("wdc%d" % i, [128, 2, OC], BF16) for i in range(3)]
    sg = [sb("sg%d" % i, [128, 512], F32) for i in range(2)]
    PHW = 18 * 1024
    parena = sb("parena", [128, PHW], BF16)
    ident = sb("ident_sb", [128, 128], BF16)
    ones = sb("ones_sb", [128, 128], BF16)
    gcols = sb("gcols_sb", [128, 4 * KC], F32)
    subg = sb("subg_sb", [128, 2], F32)
    small = sb("small", [128, 32], F32)
    lamc = sb("lamc", [128, 8], F32)
    pb = [ps("ps%d" % i) for i in range(8)]

    def carve(base_ap_t, off, n, dt, shape=None):
        a = base_ap_t[:, off:off + n]
        if dt == F32:
            a = a.bitcast(F32)
        return a

    actT = arena[:, 0:FC * 512].rearrange("p (f t) -> p f t", f=FC)
    o = [0]

    def acarve(nbf, dt):
        a = carve(arena, o[0], nbf, dt)
        o[0] += nbf
        return a
    hl = arena[0:64, 0:4 * D].bitcast(F32).rearrange("p (a c) -> p a c", a=2)
    QT = [acarve(1024, BF16).rearrange("p (j t) -> p j t", j=2) for _ in range(2)]
    KTc = [acarve(1024, BF16).rearrange("p (j t) -> p j t", j=2) for _ in range(3)]
    Vc = [acarve(1024, BF16).rearrange("p (b e) -> p b e", b=4) for _ in range(3)]
    Sp = [acarve(1024, F32) for _ in range(2)]
    PT = [[acarve(512, BF16) for _ in range(2)] for _ in range(2)]
    Rr = [acarve(1024, F32) for _ in range(2)]
    of = acarve(2048, F32).rearrange("p (e t) -> p e t", e=2)
    t1s = [acarve(1024, F32) for _ in range(2)]
    sq = acarve(1024, BF16).rearrange("p (e t) -> p e t", e=2)
    rs = acarve(1024, F32)
    assert o[0] <= ARW

    po = [0]

    def pcarve(nbf, dt, parts=128):
        a = parena[0:parts, po[0]:po[0] + nbf]
        if dt == F32:
            a = a.bitcast(F32)
        po[0] += nbf
        return a
    hnh = pcarve(2 * D, BF16, 64).rearrange("p (a c) -> p a c", a=2)
    pw = pcarve(4 * CG * GD, BF16).rearrange("p (g k d) -> p g k d", g=4, k=CG)
    bandm = pcarve(4 * 4 * 128, BF16).rearrange("p (s g t) -> p s g t", s=4, g=4)
    bandh = pcarve(2 * 4 * 128, BF16, 64).rearrange("p (a g t) -> p a g t", a=2, g=4)
    ktsb = [pcarve(512, BF16) for _ in range(2)]
    vsb = [pcarve(1024, BF16).rearrange("p (s e) -> p s e", s=4) for _ in range(2)]
    assert po[0] <= PHW, po[0]
    po[0] = 0
    base0 = pcarve(1024, F32)
    w0 = pcarve(1792, F32)
    tabs = pcarve(2 * 3 * NKB, F32).rearrange("p (a k) -> p a k", a=3)
    biasrow = pcarve(2 * NKB, F32)
    sgnrow = pcarve(2 * NKB, F32)
    gfin = pcarve(2 * D, F32)
    assert po[0] <= PHW, po[0]

    B = {}

    def bf(name):
        if name not in B:
            B[name] = Buf(name)
        return B[name]

    PB = [bf("psum%d" % i) for i in range(8)]

    def mm(out, lhsT, rhs, start, stop, reads, wbuf, last):
        tr.op("pe", lambda e: e.matmul(out, lhsT, rhs, start=start, stop=stop),
              reads=reads, writes=[wbuf], inc=last)

    def tp(out, in_, reads, wbuf, last):
        tr.op("pe", lambda e: e.transpose(out, in_, ident[:, :]), reads=reads, writes=[wbuf], inc=last)

    WBK = {}
    step = 256
    conv = [("g0", wg_d, wg_b, 0, D), ("u0", wu_d, wu_b, 0, D), ("d0", wd_d, wd_b, 0, DFF), ("qkv", wqkv_d, wqkv_b, 0, D),
            ("o", wo_d, wo_b, 0, D), ("g1", wg_d, wg_b, D, 2 * D), ("u1", wu_d, wu_b, D, 2 * D), ("d1", wd_d, wd_b, DFF, 2 * DFF)]
    for key, src, dst, ra, rb in conv:
        WBK[key] = bf("wb_" + key)
        for r0 in range(ra, rb, step):
            tr.dma("pool", (lambda s_, d_, r0_: (lambda e: e.dma_start(out=d_[r0_:r0_ + step, :], in_=s_[r0_:r0_ + step, :])))(src, dst, r0),
                   "s_wc_" + key, writes=[WBK[key]])
    CB = bf("consts")
    for dst, src in ((ident[:, :], ident_d), (ones[:, :], ones_d), (gcols[:, :], gcols_d), (subg[:, :], subg_d)):
        tr.dma("sp", (lambda d_, s_: (lambda e: e.dma_start(out=d_, in_=s_[:, :])))(dst, src), "s_const", writes=[CB])
    pwtmp = arena[:, 0:2 * 4 * CG * GD].bitcast(F32).rearrange("p (g k d) -> p g k d", g=4, k=CG)
    psc_t = xt[:, 0, :]
    AR = bf("arena")
    XT = bf("xt")
    PW = bf("pw")
    tr.dma("sp", lambda e: e.dma_start(out=pwtmp, in_=poolw_d.rearrange("(g k p) d -> p g k d", g=4, k=CG)), "s_arena", writes=[AR])
    tr.dma("sp", lambda e: e.dma_start(out=psc_t, in_=pscb_d[:, :]), "s_xt", writes=[XT])
    for g in range(4):
        for k in range(CG):
            tr.op("dve", (lambda g_, k_: (lambda e: e.tensor_tensor(out=pw[:, g_, k_, :], in0=pwtmp[:, g_, k_, :], in1=psc_t[:, g_ * GD:(g_ + 1) * GD], op=ALU.mult)))(g, k),
                  reads=[AR, XT], writes=[PW])
    lamt = xt[:, 1, 0:512]
    lamp = xt[:, 2, 0:256]
    LM = bf("lamtmp")
    SM = bf("small")
    tr.dma("sp", lambda e: e.dma_start(out=lamt, in_=lamb_d[:, :]), "s_lam", writes=[LM])
    tr.op("dve", lambda e: e.tensor_tensor(out=lamp[:, 0:128], in0=lamt[:, 0:128], in1=lamt[:, 128:256], op=ALU.mult), reads=[LM], writes=[LM])
    tr.op("dve", lambda e: e.tensor_tensor(out=lamp[:, 128:256], in0=lamt[:, 256:384], in1=lamt[:, 384:512], op=ALU.mult), reads=[LM], writes=[LM])
    tr.op("dve", lambda e: e.reduce_sum(out=lamc[:, 0:1], in_=lamp[:, 0:128], axis=AX.X), reads=[LM], writes=[SM])
    tr.op("dve", lambda e: e.reduce_sum(out=lamc[:, 1:2], in_=lamp[:, 128:256], axis=AX.X), reads=[LM], writes=[SM])
    tr.op("act", lambda e: e.activation(out=lamc[:, 2:4], in_=lamc[:, 0:2], func=AF.Exp), reads=[SM], writes=[SM])
    tr.op("dve", lambda e: e.tensor_tensor(out=lamc[:, 4:5], in0=lamc[:, 2:3], in1=lamc[:, 3:4], op=ALU.subtract), reads=[SM], writes=[SM])
    tr.op("dve", lambda e: e.tensor_scalar(out=lamc[:, 5:6], in0=lamc[:, 4:5], scalar1=lam_init, scalar2=None, op0=ALU.add), reads=[SM], writes=[SM])
    lam_col = lamc[:, 5:6]
    tr.op("dve", lambda e: e.memset(lamc[:, 6:7], EPS), writes=[SM])
    eps_col = lamc[:, 6:7]
    tr.op("dve", lambda e: e.tensor_scalar(out=subg[:, :], in0=subg[:, :], scalar1=(1.0 - lam_init), scalar2=None, op0=ALU.mult), reads=[CB], writes=[CB])
    tr.barrier()

    HNS = [bf("hn%d" % i) for i in range(4)]
    CT = bf("cT")
    SL = [bf("slab%d" % i) for i in range(4)]
    WDC = [bf("wdc%d" % i) for i in range(3)]
    SG = [bf("sg%d" % i) for i in range(2)]
    slab_i = [0]
    wdc_i = [0]
    sg_i = [0]
    pbi = [0]

    def next_pb():
        i = pbi[0] % 8
        pbi[0] += 1
        return i

    def rms_stats(col0, nsub=4):
        tr.op("dve", lambda e: e.memset(small[:, col0:col0 + nsub], 0.0), writes=[SM])
        for s in range(nsub):
            if s < 2:
                tr.op("act", (lambda s_: (lambda e: e.activation(out=hn[:, s_, :], in_=xt[:, s_, :], func=AF.Square, accum_out=small[:, col0 + s_:col0 + s_ + 1])))(s),
                      reads=[XT], writes=[HNS[s], SM])
            else:
                tr.op("dve", (lambda s_: (lambda e: e.scalar_tensor_tensor(out=hn[:, s_, :], in0=xt[:, s_, :], scalar=1.0, in1=xt[:, s_, :], op0=ALU.mult, op1=ALU.mult, accum_out=small[:, col0 + s_:col0 + s_ + 1])))(s),
                      reads=[XT], writes=[HNS[s], SM])
        tr.op("act", lambda e: e.activation(out=small[:, col0:col0 + nsub], in_=small[:, col0:col0 + nsub], func=AF.Sqrt, bias=eps_col, scale=1.0 / D), reads=[SM], writes=[SM])
        tr.op("dve", lambda e: e.reciprocal(out=small[:, col0:col0 + nsub], in_=small[:, col0:col0 + nsub]), reads=[SM], writes=[SM])

    def make_hn(col0):
        for s in range(4):
            if s % 2 == 0:
                tr.op("dve", (lambda s_: (lambda e: e.tensor_scalar(out=hn[:, s_, :], in0=xt[:, s_, :], scalar1=small[:, col0 + s_:col0 + s_ + 1], scalar2=None, op0=ALU.mult)))(s),
                      reads=[XT, SM], writes=[HNS[s]])
            else:
                tr.op("act", (lambda s_: (lambda e: e.activation(out=hn[:, s_, :], in_=xt[:, s_, :], func=AF.Copy, scale=small[:, col0 + s_:col0 + s_ + 1])))(s),
                      reads=[XT, SM], writes=[HNS[s]])

    def transpose_hn(gi):
        for k in range(KC):
            b = next_pb()
            pv = pb[b][:, :].bitcast(BF16)
            for s in range(4):
                tp(pv[:, s * 128:(s + 1) * 128], hn[:, s, k * 128:(k + 1) * 128], [HNS[s], CB], PB[b], last=(s == 3))
            eng = "act" if k % 2 == 0 else "dve"
            if eng == "act":
                tr.op("act", (lambda k_, pv_: (lambda e: e.activation(out=cT[:, k_, :], in_=pv_[:, 0:512], func=AF.Copy, scale=gcols[:, gi * KC + k_:gi * KC + k_ + 1])))(k, pv),
                      reads=[PB[b], CB], writes=[CT])
            else:
                tr.op("dve", (lambda k_, pv_: (lambda e: e.tensor_scalar(out=cT[:, k_, :], in0=pv_[:, 0:512], scalar1=gcols[:, gi * KC + k_:gi * KC + k_ + 1], scalar2=None, op0=ALU.mult)))(k, pv),
                      reads=[PB[b], CB], writes=[CT])

    def load_slab(src_ap, c0, wb):
        i = slab_i[0] % 4
        slab_i[0] += 1
        tr.dma("sp", (lambda i_: (lambda e: e.dma_start(out=slabs[i_][:, :, :], in_=src_ap[:, c0:c0 + 256].rearrange("(k p) f -> p k f", p=128))))(i),
               "s_slab%d" % i, reads=[wb], writes=[SL[i]])
        return i

    AT = bf("actT")

    def ffn(layer, gi, col0):
        rms_stats(col0)
        make_hn(col0)
        transpose_hn(gi)
        wg_l = wg_b[layer * D:(layer + 1) * D, :]
        wu_l = wu_b[layer * D:(layer + 1) * D, :]
        wd_l = wd_b[layer * DFF:(layer + 1) * DFF, :]
        nfg = DFF // 256
        pend = None
        WG, WU, WD = WBK["g%d" % layer], WBK["u%d" % layer], WBK["d%d" % layer]
        nxt = (load_slab(wg_l, 0, WG), load_slab(wu_l, 0, WU))
        for fg in range(nfg):
            cur = nxt
            if fg + 1 < nfg:
                nxt = (load_slab(wg_l, (fg + 1) * 256, WG), load_slab(wu_l, (fg + 1) * 256, WU))
            for fc in range(2):
                f = fg * 2 + fc
                bg, bu = next_pb(), next_pb()
                for (si, b) in ((cur[0], bg), (cur[1], bu)):
                    for k in range(KC):
                        mm(pb[b][:, :], slabs[si][:, k, fc * 128:(fc + 1) * 128], cT[:, k, :], k == 0, k == KC - 1,
                           [SL[si], CT], PB[b], last=(k == KC - 1))
                gi_ = sg_i[0] % 2
                sg_i[0] += 1
                tr.op("act", (lambda b_, g_: (lambda e: e.activation(out=sg[g_][:, :], in_=pb[b_][:, :], func=AF.Silu)))(bg, gi_),
                      reads=[PB[bg]], writes=[SG[gi_]])
                tr.op("dve", (lambda b_, g_, f_: (lambda e: e.tensor_tensor(out=actT[:, f_, :], in0=sg[g_][:, :], in1=pb[b_][:, :], op=ALU.mult)))(bu, gi_, f),
                      reads=[SG[gi_], PB[bu]], writes=[AT])
        for p_ in range(NPASS):
            banks = [[next_pb() for _ in range(NOG)] for _ in range(4)]
            nch = FC // 2
            def ld(c):
                i = wdc_i[0] % 3
                wdc_i[0] += 1
                tr.dma("sp", (lambda i_, c_, pp_: (lambda e: e.dma_start(out=wdc[i_][:, :, :], in_=wd_l[c_ * 256:(c_ + 1) * 256, pp_ * OC:(pp_ + 1) * OC].rearrange("(a p) d -> p a d", p=128))))(i, c, p_),
                       "s_wdc%d" % i, reads=[WD], writes=[WDC[i]])
                return i
            ring = [ld(c_) for c_ in range(min(2, nch))]
            for c in range(nch):
                cu = ring.pop(0)
                if c + 2 < nch:
                    ring.append(ld(c + 2))
                for a in range(2):
                    f = c * 2 + a
                    for s in range(4):
                        for og in range(NOG):
                            b = banks[s][og]
                            mm(pb[b][:, 0:OW], actT[:, f, s * 128:(s + 1) * 128], wdc[cu][:, a, og * OW:(og + 1) * OW], f == 0, f == FC - 1,
                               [AT, WDC[cu]], PB[b], last=(f == FC - 1 or (a == 1 and s == 3 and og == NOG - 1)))
            for s in range(4):
                for og in range(NOG):
                    b = banks[s][og]
                    c0 = p_ * OC + og * OW
                    tr.op("dve", (lambda s_, b_, c0_: (lambda e: e.tensor_tensor(out=xt[:, s_, c0_:c0_ + OW], in0=xt[:, s_, c0_:c0_ + OW], in1=pb[b_][:, 0:OW], op=ALU.add)))(s, b, c0),
                          reads=[XT, PB[b]], writes=[XT])

    HL = AR
    BD = bf("band")
    HH = bf("hnh")
    KSB = [bf("ktsb%d" % i) for i in range(2)]
    VSB = [bf("vsb%d" % i) for i in range(2)]
    KTS, VS, QTS, H1S = bf("kt_s"), bf("v_s"), bf("qt_s"), bf("h1_s")
    ksb_i = [0]
    vsb_i = [0]
    def load_x_band(sl):
        tr.dma("pool", lambda e: e.dma_start(out=xt[:, :, :], in_=xt_d[sl * 512:(sl + 1) * 512, :].rearrange("(s p) c -> p s c", p=128)), "s_xt", writes=[XT])
        tr.dma("pool", lambda e: e.dma_start(out=bandm, in_=bandm_d[sl * 128:(sl + 1) * 128, :].rearrange("p (s g t) -> p s g t", s=4, g=4)), "s_band", writes=[BD])
        tr.dma("pool", lambda e: e.dma_start(out=bandh, in_=bandh_d[sl * 64:(sl + 1) * 64, :].rearrange("p (a g t) -> p a g t", a=2, g=4)), "s_band", writes=[BD])

    for slot in range(NS):
        isq = slot < NQ
        if slot == 0:
            load_x_band(0)
        tr.dma("sp", (lambda sl: (lambda e: e.dma_start(out=hl, in_=halo_d[sl * 128:(sl + 1) * 128, :].rearrange("(a p) c -> p a c", p=64))))(slot), "s_arena", writes=[AT])
        rms_stats(0)
        tr.op("dve", lambda e: e.memset(small[0:64, 4:6], 0.0), writes=[SM])
        for a in range(2):
            tr.op("act", (lambda a_: (lambda e: e.activation(out=hnh[:, a_, :], in_=hl[:, a_, :], func=AF.Square, accum_out=small[0:64, 4 + a_:5 + a_])))(a),
                  reads=[AT], writes=[HH, SM])
        tr.op("act", lambda e: e.activation(out=small[0:64, 4:6], in_=small[0:64, 4:6], func=AF.Sqrt, bias=lamc[0:64, 6:7], scale=1.0 / D), reads=[SM], writes=[SM])
        tr.op("dve", lambda e: e.reciprocal(out=small[0:64, 4:6], in_=small[0:64, 4:6]), reads=[SM], writes=[SM])
        make_hn(0)
        for a in range(2):
            tr.op("dve", (lambda a_: (lambda e: e.tensor_scalar(out=hnh[:, a_, :], in0=hl[:, a_, :], scalar1=small[0:64, 4 + a_:5 + a_], scalar2=None, op0=ALU.mult)))(a),
                  reads=[AT, SM], writes=[HH])
        for k in range(KC):
            g = k // CG
            b = next_pb()
            for s in range(4):
                a, hp = s // 2, 32 * (s % 2)
                mm(pb[b][:, s * 128:(s + 1) * 128], hn[:, s, k * 128:(k + 1) * 128], bandm[:, s, g, :], True, False, [HNS[s], BD], PB[b], last=False)
                mm(pb[b][:, s * 128:(s + 1) * 128], hnh[hp:hp + 32, a, k * 128:(k + 1) * 128], bandh[hp:hp + 32, a, g, :], False, True, [HH, BD], PB[b], last=(s == 3))
            if k % 2 == 0:
                tr.op("act", (lambda k_, b_: (lambda e: e.activation(out=cT[:, k_, :], in_=pb[b_][:, :], func=AF.Copy, scale=gcols[:, k_:k_ + 1])))(k, b),
                      reads=[PB[b], CB], writes=[CT])
            else:
                tr.op("dve", (lambda k_, b_: (lambda e: e.tensor_scalar(out=cT[:, k_, :], in0=pb[b_][:, :], scalar1=gcols[:, k_:k_ + 1], scalar2=None, op0=ALU.mult)))(k, b),
                      reads=[PB[b], CB], writes=[CT])
        for s in range(4):
            for g in range(4):
                b = next_pb()
                for kk in range(CG):
                    mm(pb[b][:, 0:GD], cT[:, g * CG + kk, s * 128:(s + 1) * 128], pw[:, g, kk, :], kk == 0, kk == CG - 1, [CT, PW], PB[b], last=(kk == CG - 1))
                tr.op("dve", (lambda s_, g_, b_: (lambda e: e.tensor_tensor(out=xt[:, s_, g_ * GD:(g_ + 1) * GD], in0=xt[:, s_, g_ * GD:(g_ + 1) * GD], in1=pb[b_][:, 0:GD], op=ALU.add)))(s, g, b),
                      reads=[XT, PB[b]], writes=[XT])
        ffn(0, 1, 8)
        if isq:
            tr.dma("pool", (lambda sl: (lambda e: e.dma_start(out=h1_s[sl * 512:(sl + 1) * 512, :].rearrange("(s p) c -> p s c", p=128), in_=xt[:, :, :])))(slot),
                   "s_h1st", reads=[XT], writes=[H1S])
        rms_stats(12)
        make_hn(12)
        if slot + 1 < NS:
            load_x_band(slot + 1)
        transpose_hn(2)
        jobs = []
        for h in range(H):
            jobs.append(("k", h))
            jobs.append(("v", h))
            if isq:
                jobs.append(("q", h))

        def jcol(job):
            kind, h = job
            return {"q": 0, "k": D, "v": 2 * D}[kind] + h * 256
        nxs = load_slab(wqkv_b, jcol(jobs[0]), WBK["qkv"])
        for ji, job in enumerate(jobs):
            cs = nxs
            if ji + 1 < len(jobs):
                nxs = load_slab(wqkv_b, jcol(jobs[ji + 1]), WBK["qkv"])
            kind, h = job
            if kind in ("k", "q"):
                for j in range(2):
                    b = next_pb()
                    for k in range(KC):
                        mm(pb[b][:, :], slabs[cs][:, k, j * 128:(j + 1) * 128], cT[:, k, :], k == 0, k == KC - 1, [SL[cs], CT], PB[b], last=(k == KC - 1))
                    ki = ksb_i[0] % 2
                    ksb_i[0] += 1
                    tr.op("act", (lambda b_, ki_: (lambda e: e.activation(out=ktsb[ki_], in_=pb[b_][:, :], func=AF.Copy)))(b, ki), reads=[PB[b]], writes=[KSB[ki]])
                    dst, DB, ncol = (kt_s, KTS, NS * 512) if kind == "k" else (qt_s, QTS, NQ * 512)
                    r0 = (h * 2 + j) * 128
                    tr.dma("pool", (lambda dst_, r0_, sl, ki_: (lambda e: e.dma_start(out=dst_[r0_:r0_ + 128, sl * 512:(sl + 1) * 512], in_=ktsb[ki_])))(dst, r0, slot, ki),
                           "s_kst%d" % ki, reads=[KSB[ki]], writes=[DB])
            else:
                vi = vsb_i[0] % 2
                vsb_i[0] += 1
                for s in range(4):
                    b = next_pb()
                    for k in range(KC):
                        mm(pb[b][:, 0:256], cT[:, k, s * 128:(s + 1) * 128], slabs[cs][:, k, :], k == 0, k == KC - 1, [CT, SL[cs]], PB[b], last=(k == KC - 1))
                    tr.op("dve", (lambda b_, vi_, s_: (lambda e: e.tensor_copy(out=vsb[vi_][:, s_, :], in_=pb[b_][:, 0:256])))(b, vi, s), reads=[PB[b]], writes=[VSB[vi]])
                tr.dma("pool", (lambda sl, h_, vi_: (lambda e: e.dma_start(out=v_s[sl * 512:(sl + 1) * 512, h_ * 256:(h_ + 1) * 256].rearrange("(s p) e -> p s e", p=128), in_=vsb[vi_])))(slot, h, vi),
                       "s_vst%d" % vi, reads=[VSB[vi]], writes=[VS])
    tr.barrier()

    QC = bf("qconst")
    for dst, src in ((base0, base0_d), (w0, w0_d), (gfin, gfin_d)):
        tr.dma("sp", (lambda d_, s_: (lambda e: e.dma_start(out=d_, in_=s_[:, :])))(dst, src), "s_const", writes=[QC])
    TB = bf("tabs")
    ROW = bf("rows")
    QTB = [bf("QT%d" % i) for i in range(2)]
    KTB = [bf("KTc%d" % i) for i in range(3)]
    VCB = [bf("Vc%d" % i) for i in range(3)]
    SPB = [bf("Sp%d" % i) for i in range(2)]
    PTB = [[bf("PT%d_%d" % (j, i)) for i in range(2)] for j in range(2)]
    RB = [bf("R%d" % i) for i in range(2)]
    T1B = [bf("t1_%d" % i) for i in range(2)]
    OFB = [bf("of%d" % i) for i in range(2)]
    SQB = [bf("sq%d" % i) for i in range(2)]
    RSB = bf("rs")
    qt_i = [0]
    kv_i = [0]
    pt_i = [0]
    slopes = [2.0 ** (-8.0 * (h + 1) / H) for h in range(H)]
    SKIP_TH = _SKIP_TH
    dmin = _min_dist(cfg)
    for qi in range(NQ):
        tr.barrier(dma=(qi == 0))
        tr.dma("pool", (lambda q_: (lambda e: e.dma_start(out=xt[:, :, :], in_=h1_s[q_ * 512:(q_ + 1) * 512, :].rearrange("(s p) c -> p s c", p=128))))(qi), "s_xt", reads=[H1S], writes=[XT])
        tr.dma("sp", (lambda q_: (lambda e: e.dma_start(out=tabs, in_=tabs_d[q_ * 128:(q_ + 1) * 128, :].rearrange("p (a k) -> p a k", a=3))))(qi), "s_tabs", writes=[TB])
        def ldkv(h, c):
            i = kv_i[0] % 3
            kv_i[0] += 1
            nb = min(4, NKB - 4 * c)
            tr.dma("sp", (lambda h_, c_, i_, nb_: (lambda e: e.dma_start(out=KTc[i_][:, :, 0:nb_ * 128], in_=kt_s[h_ * 256:(h_ + 1) * 256, c_ * 512:c_ * 512 + nb_ * 128].rearrange("(j p) t -> p j t", p=128))))(h, c, i, nb),
                   "s_kt%d" % i, reads=[KTS], writes=[KTB[i]])
            tr.dma("sp", (lambda h_, c_, i_, nb_: (lambda e: e.dma_start(out=Vc[i_][:, 0:nb_, :], in_=v_s[c_ * 512:c_ * 512 + nb_ * 128, h_ * 256:(h_ + 1) * 256].rearrange("(b p) e -> p b e", p=128))))(h, c, i, nb),
                   "s_vc%d" % i, reads=[VS], writes=[VCB[i]])
            return i

        def head_setup(h):
            nchunk = (NKB + 3) // 4
            need = [kb for kb in range(NKB) if slopes[h] * dmin[qi, kb] < SKIP_TH]
            plan = []
            for c in range(nchunk):
                bis = [kb - 4 * c for kb in need if kb // 4 == c]
                if bis:
                    plan.append((c, bis))
            qb_ = qt_i[0] % 2
            qt_i[0] += 1
            tr.dma("sp", (lambda h_, q_, b_: (lambda e: e.dma_start(out=QT[b_], in_=qt_s[h_ * 256:(h_ + 1) * 256, q_ * 512:(q_ + 1) * 512].rearrange("(j p) t -> p j t", p=128))))(h, qi, qb_),
                   "s_qt%d" % qb_, reads=[QTS], writes=[QTB[qb_]])
            ring = [ldkv(h, plan[i_][0]) for i_ in range(min(2, len(plan)))]
            return dict(plan=plan, qb=qb_, ring=ring, last_need=need[-1])

        hs_next = head_setup(0)
        for h in range(H):
            fh = -slopes[h] / scale
            tr.op("dve", (lambda h_: (lambda e: e.scalar_tensor_tensor(out=biasrow, in0=tabs[:, 1, :], scalar=-slopes[h_], in1=tabs[:, 2, :], op0=ALU.mult, op1=ALU.add)))(h), reads=[TB], writes=[ROW])
            tr.op("dve", (lambda f_: (lambda e: e.tensor_scalar(out=sgnrow, in0=tabs[:, 0, :], scalar1=f_, scalar2=None, op0=ALU.mult)))(fh), reads=[TB], writes=[ROW])
            hs = hs_next
            plan, qb_, ring, last_need = hs["plan"], hs["qb"], hs["ring"], hs["last_need"]
            first = True
            pend_pv = [None]
            for pi_, (c, bis) in enumerate(plan):
                ckv = ring.pop(0)
                for bn, bi in enumerate(bis):
                    kb = 4 * c + bi
                    lastkb = (kb == last_need)
                    diag = (kb // 4 == qi) and kb < 4 * NT
                    pts = []
                    for j in range(2):
                        mm(pb[j][:, :], KTc[ckv][:, j, bi * 128:(bi + 1) * 128], QT[qb_][:, j, :], True, True, [KTB[ckv], QTB[qb_]], PB[j], last=(j == 1))
                    for j in range(2):
                        if diag:
                            o_ = kb % 4
                            tr.op("dve", (lambda j_, o__, f_: (lambda e: e.scalar_tensor_tensor(out=Sp[j_], in0=w0[:, 384 - 128 * o__:384 - 128 * o__ + 512], scalar=f_, in1=pb[j_][:, :], op0=ALU.mult, op1=ALU.add)))(j, o_, fh),
                                  reads=[QC, PB[j]], writes=[SPB[j]])
                        else:
                            tr.op("dve", (lambda j_, kb_: (lambda e: e.scalar_tensor_tensor(out=Sp[j_], in0=base0, scalar=sgnrow[:, kb_:kb_ + 1], in1=pb[j_][:, :], op0=ALU.mult, op1=ALU.add)))(j, kb),
                                  reads=[QC, ROW, PB[j]], writes=[SPB[j]])
                        pi = pt_i[0] % 2
                        tr.op("act", (lambda j_, kb_, pi_: (lambda e: e.activation(out=PT[j_][pi_], in_=Sp[j_], func=AF.Exp, bias=biasrow[:, kb_:kb_ + 1], scale=scale)))(j, kb, pi),
                              reads=[SPB[j], ROW], writes=[PTB[j][pi]])
                        pts.append(pi)
                    pt_i[0] += 1

                    def pv(pts=pts, ckv=ckv, bi=bi, first=first, lastkb=lastkb):
                        for j in range(2):
                            pi = pts[j]
                            for ec in range(2):
                                b = 2 + 2 * j + ec
                                mm(pb[b][:, :], Vc[ckv][:, bi, ec * 128:(ec + 1) * 128], PT[j][pi], first, lastkb, [VCB[ckv], PTB[j][pi]], PB[b], last=False)
                            mm(pb[6 + j][:, :], ones[:, :], PT[j][pi], first, lastkb, [CB, PTB[j][pi]], PB[6 + j], last=(j == 1))
                    if pend_pv[0] is not None:
                        pend_pv[0]()
                    pend_pv[0] = pv
                    first = False
                    if bn == 0 and pi_ + 2 < len(plan):
                        ring.append(ldkv(h, plan[pi_ + 2][0]))
            pend_pv[0]()
            pend_pv[0] = None
            if h + 1 < H:
                hs_next = head_setup(h + 1)
            for j in range(2):
                tr.op("dve", (lambda j_: (lambda e: e.reciprocal(out=Rr[j_], in_=pb[6 + j_][:, :])))(j), reads=[PB[6 + j]], writes=[RB[j]])
            for ec in range(2):
                tr.op("dve", (lambda ec_: (lambda e: e.scalar_tensor_tensor(out=t1s[ec_], in0=pb[4 + ec_][:, :], scalar=lam_col, in1=Rr[1], op0=ALU.mult, op1=ALU.mult)))(ec), reads=[PB[4 + ec], RB[1], SM], writes=[T1B[ec]])
            for ec in range(2):
                tr.op("dve", (lambda ec_: (lambda e: e.tensor_tensor(out=of[:, ec_, :], in0=pb[2 + ec_][:, :], in1=Rr[0], op=ALU.mult)))(ec), reads=[PB[2 + ec], RB[0]], writes=[OFB[ec]])
            for ec in range(2):
                tr.op("dve", (lambda ec_: (lambda e: e.tensor_tensor(out=of[:, ec_, :], in0=of[:, ec_, :], in1=t1s[ec_], op=ALU.subtract)))(ec), reads=[T1B[ec]], writes=[OFB[ec]])
            for ec in range(2):
                tr.op("act", (lambda ec_: (lambda e: e.activation(out=sq[:, ec_, :], in_=of[:, ec_, :], func=AF.Square)))(ec), reads=[OFB[ec]], writes=[SQB[ec]])
            for ec in range(2):
                mm(pb[0][:, :], ones[:, :], sq[:, ec, :], ec == 0, ec == 1, [CB, SQB[ec]], PB[0], last=(ec == 1))
            tr.op("act", lambda e: e.activation(out=rs, in_=pb[0][:, :], func=AF.Sqrt, bias=eps_col, scale=1.0 / 256.0), reads=[PB[0], SM], writes=[RSB])
            tr.op("dve", lambda e: e.reciprocal(out=rs, in_=rs), reads=[RSB], writes=[RSB])
            for ec in range(2):
                tr.op("dve", (lambda ec_, h_: (lambda e: e.scalar_tensor_tensor(out=cT[:, 2 * h_ + ec_, :], in0=of[:, ec_, :], scalar=subg[:, ec_:ec_ + 1], in1=rs, op0=ALU.mult, op1=ALU.mult)))(ec, h),
                      reads=[OFB[ec], RSB, CB], writes=[CT])
        tr.barrier()
        for p_ in range(NPASS):
            banks = [[next_pb() for _ in range(NOG)] for _ in range(4)]
            nch = KC // 2

            def ldo(c):
                i = wdc_i[0] % 3
                wdc_i[0] += 1
                tr.dma("sp", (lambda i_, c_, pp_: (lambda e: e.dma_start(out=wdc[i_][:, :, :], in_=wo_b[c_ * 256:(c_ + 1) * 256, pp_ * OC:(pp_ + 1) * OC].rearrange("(a p) d -> p a d", p=128))))(i, c, p_),
                       "s_wdc%d" % i, reads=[WBK["o"]], writes=[WDC[i]])
                return i
            ring = [ldo(c_) for c_ in range(min(2, nch))]
            for c in range(nch):
                cu = ring.pop(0)
                if c + 2 < nch:
                    ring.append(ldo(c + 2))
                for a in range(2):
                    k = c * 2 + a
                    for s in range(4):
                        for og in range(NOG):
                            b = banks[s][og]
                            mm(pb[b][:, 0:OW], cT[:, k, s * 128:(s + 1) * 128], wdc[cu][:, a, og * OW:(og + 1) * OW], k == 0, k == KC - 1,
                               [CT, WDC[cu]], PB[b], last=(k == KC - 1 or (a == 1 and s == 3 and og == NOG - 1)))
            for s in range(4):
                for og in range(NOG):
                    b = banks[s][og]
                    c0 = p_ * OC + og * OW
                    tr.op("dve", (lambda s_, b_, c0_: (lambda e: e.tensor_tensor(out=xt[:, s_, c0_:c0_ + OW], in0=xt[:, s_, c0_:c0_ + OW], in1=pb[b_][:, 0:OW], op=ALU.add)))(s, b, c0),
                          reads=[XT, PB[b]], writes=[XT])
        ffn(1, 3, 16)
        rms_stats(20)
        for s in range(4):
            tr.op("dve", (lambda s_: (lambda e: e.scalar_tensor_tensor(out=xt[:, s_, :], in0=xt[:, s_, :], scalar=small[:, 20 + s_:21 + s_], in1=gfin, op0=ALU.mult, op1=ALU.mult)))(s),
                  reads=[XT, SM, QC], writes=[XT])
        tr.dma("pool", (lambda q_: (lambda e: e.dma_start(out=y_d[q_ * 512:(q_ + 1) * 512, :].rearrange("(s p) c -> p s c", p=128), in_=xt[:, :, :])))(qi), "s_yst", reads=[XT], writes=[bf("y")])
    tr.barrier()

    sems = {}
    for k in sorted(tr.semkeys):
        sems[k] = es.enter_context(nc.semaphore(k))
    with nc.Block() as block:
        @block.tensor
        def _(e):
            tr.replay("pe", e, sems)

        @block.scalar
        def _(e):
            tr.replay("act", e, sems)

        @block.vector
        def _(e):
            tr.replay("dve", e, sems)

        @block.gpsimd
        def _(e):
            tr.replay("pool", e, sems)

        @block.sync
        def _(e):
            tr.replay("sp", e, sems)
    es.close()
    return nc


_band_cache = {}


def _band_for(pos_main, pos_halo, L):
    key = (tuple(pos_main.tolist()), tuple(pos_halo.tolist()), L)
    p0 = int(pos_main[0])
    rel = (tuple((pos_main - p0).tolist()) if p0 >= 0 else None, tuple(np.where(pos_halo >= 0, pos_halo - p0, -999).tolist()), min(p0, 40), min(L - p0, 300) if p0 >= 0 else -1)
    if rel in _band_cache:
        return _band_cache[rel]
    bm = np.zeros((128, 4, 128), np.float32)
    bh = np.zeros((32, 4, 128), np.float32)
    where = {}
    for i, p in enumerate(pos_main):
        if p >= 0:
            where[int(p)] = ("m", i)
    for i, p in enumerate(pos_halo):
        if p >= 0 and int(p) not in where:
            where[int(p)] = ("h", i)
    for r, p in enumerate(pos_main):
        if p < 0:
            continue
        p = int(p)
        for g, w in enumerate(WINDOWS):
            lo, hi = max(p - w // 2, 0), min(p + w // 2, L)
            inv = 1.0 / float(hi - lo)
            for u in range(lo, hi):
                kind, i = where[u]
                if kind == "m":
                    bm[i, g, r] += inv
                else:
                    bh[i, g, r] += inv
            bm[r, g, r] -= 1.0
    _band_cache[rel] = (bm, bh)
    return bm, bh


def _slot_positions(cfg, ntile, ctype):
    NT, NQ, NS = cfg.NT, cfg.NQ, cfg.NS
    half = ntile // 2
    tmap = [-1] * NT
    for i in range(half):
        tmap[i] = (ctype * half + i)
        tmap[NQ + i] = ((1 - ctype) * half + i)
    pos = -np.ones((NS, 512), np.int64)
    for sl in range(NT):
        if tmap[sl] >= 0:
            pos[sl] = N_META + 512 * tmap[sl] + np.arange(512)
    pos[NT, :N_META] = np.arange(N_META)
    return tmap, pos


def _min_dist(cfg):
    NT, NQ, NKB = cfg.NT, cfg.NQ, cfg.NKB
    dmin = np.full((NQ, NKB), np.inf)
    for ntile in (NT, NT // 2):
        for ctype in (0, 1):
            _, pos = _slot_positions(cfg, ntile, ctype)
            for qi in range(NQ):
                if pos[qi, 0] < 0:
                    continue
                q0, q1 = int(pos[qi, 0]), int(pos[qi, 0]) + 511
                for kb in range(NKB):
                    if kb == NKB - 1:
                        k0, k1 = 0, N_META - 1
                    else:
                        k0 = int(pos[kb // 4, (kb % 4) * 128])
                        if k0 < 0:
                            continue
                        k1 = k0 + 127
                    d = max(0, k0 - q1, q0 - k1)
                    dmin[qi, kb] = min(dmin[qi, kb], d)
    return dmin


def _core_layout(cfg, x_seq, meta, ctype):
    D, NT, NQ, NS, NKB = cfg.D, cfg.NT, cfg.NQ, cfg.NS, cfg.NKB
    S = x_seq.shape[0]
    L = S + N_META
    ntile = S // 512
    tmap, pos = _slot_positions(cfg, ntile, ctype)
    xt = np.zeros((NS * 512, D), np.float32)
    for sl in range(NT):
        if tmap[sl] >= 0:
            t = tmap[sl]
            xt[sl * 512:(sl + 1) * 512] = x_seq[512 * t:512 * (t + 1)]
    xt[NT * 512:NT * 512 + N_META] = meta

    def row_of(p):
        return meta[p] if p < N_META else x_seq[p - N_META]
    halo = np.zeros((NS, 2, 64, D), np.float32)
    bandm = np.zeros((NS, 128, 4, 4, 128), np.float32)
    bandh = np.zeros((NS, 64, 2, 4, 128), np.float32)
    for sl in range(NS):
        for s in range(4):
            pm = pos[sl, s * 128:(s + 1) * 128]
            ph = -np.ones(32, np.int64)
            if pm[0] >= 0:
                nvalid = int((pm >= 0).sum())
                p0, p1 = int(pm[0]), int(pm[0]) + nvalid
                cand = list(range(p0 - 8, p0)) + list(range(p1, p1 + 8))
                for i, p in enumerate(cand):
                    if 0 <= p < L:
                        ph[i] = p
                        halo[sl, s // 2, 32 * (s % 2) + i] = row_of(p)
                bm, bh = _band_for(pm, ph, L)
                bandm[sl, :, s] = bm
                bandh[sl, 32 * (s % 2):32 * (s % 2) + 32, s // 2] = bh
    tabs = np.zeros((NQ, 128, 3, NKB), np.float32)
    tabs[:, :, 0, :] = 1.0
    for qi in range(NQ):
        if pos[qi, 0] < 0:
            continue
        pq0 = int(pos[qi, 0])
        for kb in range(NKB):
            if kb == NKB - 1:
                tabs[qi, :, 0, kb] = 1.0
                tabs[qi, :, 1, kb] = pq0
                tabs[qi, N_META:, 2, kb] = NEG
                continue
            sl, sub = kb // 4, kb % 4
            if sl == qi:
                continue
            pk0 = int(pos[sl, sub * 128])
            if pk0 < 0:
                tabs[qi, :, 2, kb] = NEG
                continue
            dlt = pq0 - pk0
            tabs[qi, :, 0, kb] = 1.0 if dlt > 0 else -1.0
            tabs[qi, :, 1, kb] = abs(dlt)
    valid_q = [tmap[i] for i in range(NQ)]
    return dict(xt=xt, halo=halo.reshape(NS * 128, D),
                bandm=bandm.reshape(NS * 128, 2048).astype(NPBF),
                bandh=bandh.reshape(NS * 64, 1024).astype(NPBF),
                tabs=tabs.reshape(NQ * 128, 3 * NKB)), valid_q


def _shared_inputs(cfg, inp):
    D, KC = cfg.D, cfg.KC
    f = np.float32
    kk = np.arange(128)[:, None]
    base0 = (np.arange(512)[None, :] - kk).astype(f)
    w0 = np.abs(np.arange(896)[None, :] - 384 - kk).astype(f)
    gl = [inp["mixer_norm_g"][0], inp["ffn_norm_g"][0], inp["mixer_norm_g"][1], inp["ffn_norm_g"][1]]
    gcols = np.concatenate([np.asarray(g, f).reshape(KC, 128).T for g in gl], axis=1)
    lamb = np.concatenate([np.asarray(inp[k], f).reshape(1, 128) for k in ("lambda_q1", "lambda_k1", "lambda_q2", "lambda_k2")], axis=1)
    return dict(
        base0=np.ascontiguousarray(base0), w0=np.ascontiguousarray(w0),
        ident=np.eye(128, dtype=f).astype(NPBF), ones=np.ones((128, 128), f).astype(NPBF),
        gcols=np.ascontiguousarray(gcols),
        gfin=np.ascontiguousarray(np.broadcast_to(np.asarray(inp["final_norm_g"], f).reshape(1, D), (128, D))),
        pscb=np.ascontiguousarray(np.broadcast_to(np.asarray(inp["pool_scale"], f).reshape(1, D), (128, D))),
        subg=np.ascontiguousarray(np.asarray(inp["subln_g"], f).reshape(2, 128).T),
        lamb=np.ascontiguousarray(np.broadcast_to(lamb, (128, 512))),
        pool_w=np.ascontiguousarray(np.asarray(inp["pool_w"], f).reshape(4 * cfg.GD, cfg.GD)),
        w_qkv=np.ascontiguousarray(np.asarray(inp["w_qkv"], f).reshape(D, 3 * D)),
        w_o=np.ascontiguousarray(np.asarray(inp["w_o"], f).reshape(D, D)),
        w_gate=np.ascontiguousarray(np.asarray(inp["w_gate"], f).reshape(2 * D, cfg.DFF)),
        w_up=np.ascontiguousarray(np.asarray(inp["w_up"], f).reshape(2 * D, cfg.DFF)),
        w_down=np.ascontiguousarray(np.asarray(inp["w_down"], f).reshape(2 * cfg.DFF, D)),
    )


def run(cfg, inp):
    xp = np.asarray(inp["x_prompt"], np.float32)
    xs = np.asarray(inp["x_sample"], np.float32)
    meta = np.asarray(inp["meta_tokens"], np.float32)
    shared = _shared_inputs(cfg, inp)
    seqs = [xs[0], xs[1], xp[0], xp[1]]
    in_maps, vq = [], []
    for c in range(8):
        lay, valid_q = _core_layout(cfg, seqs[c // 2], meta, c % 2)
        m = dict(shared)
        m.update(lay)
        in_maps.append(m)
        vq.append(valid_q)
    nc = build(cfg)
    res = run_bass_kernel_spmd(nc, in_maps, core_ids=list(range(8)))
    yp = np.zeros_like(xp)
    ys = np.zeros_like(xs)
    outs = [ys[0], ys[1], yp[0], yp[1]]
    for c in range(8):
        y = np.asarray(res.results[c]["y"], dtype=np.float32)
        for qi, t in enumerate(vq[c]):
            if t >= 0:
                outs[c // 2][512 * t:512 * (t + 1)] = y[512 * qi:512 * (qi + 1)]
    return yp, ys


def kernel(**inputs):
    cfg = Cfg()
    return run(cfg, inputs)
```

```python
import math
from contextlib import ExitStack
import numpy as np
import ml_dtypes
import concourse.bass as bass
import concourse.mybir as mybir
from concourse.bass_utils import run_bass_kernel_spmd

F32 = mybir.dt.float32
BF16 = mybir.dt.bfloat16
ALU = mybir.AluOpType
AF = mybir.ActivationFunctionType
AX = mybir.AxisListType
NPBF = ml_dtypes.bfloat16

N_META = 16
WINDOWS = (2, 4, 8, 16)
EPS = 1e-6
NEG = -30000.0
_SKIP_TH = 50.0


class Cfg:
    def __init__(self, D=2048, DFF=5632, NT=32, NQ=16):
        self.D, self.DFF, self.NT, self.NQ = D, DFF, NT, NQ
        self.KC = D // 128
        self.FC = DFF // 128
        self.H = D // 256
        self.GD = D // 4
        self.CG = self.GD // 128
        self.NS = NT + 1
        self.NKB = 4 * NT + 1
        self.OC = min(D, 1024)
        self.NPASS = D // self.OC
        self.NOG = self.OC // 512 if self.OC >= 512 else 1
        self.OW = min(512, self.OC)


class Buf:
    def __init__(self, name):
        self.name = name
        self.w = None
        self.r = {}


class Tracker:
    ENG = ("pe", "act", "dve", "pool", "sp")

    def __init__(self):
        self.stream = {e: [] for e in self.ENG}
        self.cnt = {e: 0 for e in self.ENG}
        self.waited = {e: {} for e in self.ENG}
        self.dma_cnt = {}
        self.semkeys = set("prog_" + e for e in ("pe", "act", "dve", "pool"))

    def _deps(self, reads, writes):
        deps = {}

        def add(tok):
            if tok is None:
                return
            k, v = tok
            if deps.get(k, 0) < v:
                deps[k] = v
        for b in reads:
            add(b.w)
        for b in writes:
            add(b.w)
            for k, v in b.r.items():
                add((k, v))
        return deps

    def _waits(self, e, deps):
        ws = []
        for k, v in deps.items():
            if e == "pe" and k == "prog_pe":
                continue
            if self.waited[e].get(k, 0) >= v:
                continue
            self.waited[e][k] = v
            ws.append((k, v))
        return ws

    def op(self, e, fn, reads=(), writes=(), inc=True):
        deps = self._deps(reads, writes)
        ws = self._waits(e, deps)
        key = "prog_" + e
        if inc:
            self.cnt[e] += 1
            tok = (key, self.cnt[e])
            incs = [(key, 1)]
        else:
            tok = (key, self.cnt[e] + 1)
            incs = []
        self.stream[e].append((ws, fn, incs))
        for b in reads:
            if b.r.get(key, 0) < tok[1]:
                b.r[key] = tok[1]
        for b in writes:
            b.w = tok
            b.r = {}
        return tok

    def dma(self, q, fn, semname, reads=(), writes=()):
        deps = self._deps(reads, writes)
        ws = self._waits(q, deps)
        self.semkeys.add(semname)
        self.dma_cnt[semname] = self.dma_cnt.get(semname, 0) + 1
        tok = (semname, 16 * self.dma_cnt[semname])
        self.stream[q].append((ws, fn, [(semname, 16)]))
        for b in reads:
            if b.r.get(semname, 0) < tok[1]:
                b.r[semname] = tok[1]
        for b in writes:
            b.w = tok
            b.r = {}
        return tok

    def barrier(self, dma=True):
        allv = {"prog_" + e: self.cnt[e] for e in ("pe", "act", "dve", "pool") if self.cnt[e] > 0}
        if dma:
            for k, c in self.dma_cnt.items():
                allv[k] = 16 * c
        for e in self.ENG:
            ws = self._waits(e, allv)
            if ws:
                self.stream[e].append((ws, None, []))

    def replay(self, e, eng, sems):
        for ws, fn, incs in self.stream[e]:
            for k, v in ws:
                eng.wait_ge(sems[k], v)
            if fn is None:
                continue
            ins = fn(eng)
            for k, v in incs:
                ins = ins.then_inc(sems[k], v)


def build(cfg):
    D, DFF, NT, NQ = cfg.D, cfg.DFF, cfg.NT, cfg.NQ
    KC, FC, H, GD, CG, NS, NKB = cfg.KC, cfg.FC, cfg.H, cfg.GD, cfg.CG, cfg.NS, cfg.NKB
    OC, NPASS, NOG, OW = cfg.OC, cfg.NPASS, cfg.NOG, cfg.OW
    scale = 128 ** -0.5
    lam_init = 0.8 - 0.6 * math.exp(-0.3 * 1)

    nc = bass.Bass("TRN2", target_bir_lowering=False)

    def din(name, shape, dt=F32):
        return nc.dram_tensor(name, list(shape), dt, kind="ExternalInput").ap()

    def dscr(name, shape, dt):
        return nc.dram_tensor(name, list(shape), dt, kind="Internal").ap()

    xt_d = din("xt", [NS * 512, D])
    halo_d = din("halo", [NS * 128, D])
    bandm_d = din("bandm", [NS * 128, 4 * 4 * 128], BF16)
    bandh_d = din("bandh", [NS * 64, 2 * 4 * 128], BF16)
    tabs_d = din("tabs", [NQ * 128, 3 * NKB])
    base0_d = din("base0", [128, 512])
    w0_d = din("w0", [128, 896])
    ident_d = din("ident", [128, 128], BF16)
    ones_d = din("ones", [128, 128], BF16)
    gcols_d = din("gcols", [128, 4 * KC])
    gfin_d = din("gfin", [128, D])
    pscb_d = din("pscb", [128, D])
    subg_d = din("subg", [128, 2])
    lamb_d = din("lamb", [128, 4 * 128])
    poolw_d = din("pool_w", [4 * GD, GD])
    wqkv_d = din("w_qkv", [D, 3 * D])
    wo_d = din("w_o", [D, D])
    wg_d = din("w_gate", [2 * D, DFF])
    wu_d = din("w_up", [2 * D, DFF])
    wd_d = din("w_down", [2 * DFF, D])
    y_d = nc.dram_tensor("y", [NQ * 512, D], F32, kind="ExternalOutput").ap()

    wqkv_b = dscr("wqkv_b", [D, 3 * D], BF16)
    wo_b = dscr("wo_b", [D, D], BF16)
    wg_b = dscr("wg_b", [2 * D, DFF], BF16)
    wu_b = dscr("wu_b", [2 * D, DFF], BF16)
    wd_b = dscr("wd_b", [2 * DFF, D], BF16)
    kt_s = dscr("kt_s", [H * 2 * 128, NS * 512], BF16)
    v_s = dscr("v_s", [NS * 512, D], BF16)
    qt_s = dscr("qt_s", [H * 2 * 128, NQ * 512], BF16)
    h1_s = dscr("h1_s", [NQ * 512, D], F32)

    tr = Tracker()
    es = ExitStack()

    def sb(name, shape, dt):
        return es.enter_context(nc.sbuf_tensor("sb_" + name, list(shape), dt))

    def ps(name):
        return es.enter_context(nc.psum_tensor(name, [128, 512], F32))

    xt = sb("xt", [128, 4, D], F32)
    hn = sb("hn", [128, 4, D], BF16)
    cT = sb("cT", [128, KC, 512], BF16)
    ARW = max(FC * 512, 22 * 1024)
    arena = sb("arena", [128, ARW], BF16)
    slabs = [sb("slab%d" % i, [128, KC, 256], BF16) for i in range(4)]
    wdc = [sb("wdc%d" % i, [128, 2, OC], BF16) for i in range(3)]
    sg = [sb("sg%d" % i, [128, 512], F32) for i in range(2)]
    PHW = 18 * 1024
    parena = sb("parena", [128, PHW], BF16)
    ident = sb("ident_sb", [128, 128], BF16)
    ones = sb("ones_sb", [128, 128], BF16)
    gcols = sb("gcols_sb", [128, 4 * KC], F32)
    subg = sb("subg_sb", [128, 2], F32)
    small = sb("small", [128, 32], F32)
    lamc = sb("lamc", [128, 8], F32)
    pb = [ps("ps%d" % i) for i in range(8)]

    def carve(base_ap_t, off, n, dt, shape=None):
        a = base_ap_t[:, off:off + n]
        if dt == F32:
            a = a.bitcast(F32)
        return a

    actT = arena[:, 0:FC * 512].rearrange("p (f t) -> p f t", f=FC)
    o = [0]

    def acarve(nbf, dt):
        a = carve(arena, o[0], nbf, dt)
        o[0] += nbf
        return a
    hl = arena[0:64, 0:4 * D].bitcast(F32).rearrange("p (a c) -> p a c", a=2)
    QT = [acarve(1024, BF16).rearrange("p (j t) -> p j t", j=2) for _ in range(2)]
    KTc = [acarve(1024, BF16).rearrange("p (j t) -> p j t", j=2) for _ in range(3)]
    Vc = [acarve(1024, BF16).rearrange("p (b e) -> p b e", b=4) for _ in range(3)]
    Sp = [acarve(1024, F32) for _ in range(2)]
    PT = [[acarve(512, BF16) for _ in range(2)] for _ in range(2)]
    Rr = [acarve(1024, F32) for _ in range(2)]
    of = acarve(2048, F32).rearrange("p (e t) -> p e t", e=2)
    t1s = [acarve(1024, F32) for _ in range(2)]
    sq = acarve(1024, BF16).rearrange("p (e t) -> p e t", e=2)
    rs = acarve(1024, F32)
    assert o[0] <= ARW

    po = [0]

    def pcarve(nbf, dt, parts=128):
        a = parena[0:parts, po[0]:po[0] + nbf]
        if dt == F32:
            a = a.bitcast(F32)
        po[0] += nbf
        return a
    hnh = pcarve(2 * D, BF16, 64).rearrange("p (a c) -> p a c", a=2)
    pw = pcarve(4 * CG * GD, BF16).rearrange("p (g k d) -> p g k d", g=4, k=CG)
    bandm = pcarve(4 * 4 * 128, BF16).rearrange("p (s g t) -> p s g t", s=4, g=4)
    bandh = pcarve(2 * 4 * 128, BF16, 64).rearrange("p (a g t) -> p a g t", a=2, g=4)
    ktsb = [pcarve(512, BF16) for _ in range(2)]
    vsb = [pcarve(1024, BF16).rearrange("p (s e) -> p s e", s=4) for _ in range(2)]
    assert po[0] <= PHW, po[0]
    po[0] = 0
    base0 = pcarve(1024, F32)
    w0 = pcarve(1792, F32)
    tabs = pcarve(2 * 3 * NKB, F32).rearrange("p (a k) -> p a k", a=3)
    biasrow = pcarve(2 * NKB, F32)
    sgnrow = pcarve(2 * NKB, F32)
    gfin = pcarve(2 * D, F32)
    assert po[0] <= PHW, po[0]

    B = {}

    def bf(name):
        if name not in B:
            B[name] = Buf(name)
        return B[name]

    PB = [bf("psum%d" % i) for i in range(8)]

    def mm(out, lhsT, rhs, start, stop, reads, wbuf, last):
        tr.op("pe", lambda e: e.matmul(out, lhsT, rhs, start=start, stop=stop),
              reads=reads, writes=[wbuf], inc=last)

    def tp(out, in_, reads, wbuf, last):
        tr.op("pe", lambda e: e.transpose(out, in_, ident[:, :]), reads=reads, writes=[wbuf], inc=last)

    WBK = {}
    step = 256
    conv = [("g0", wg_d, wg_b, 0, D), ("u0", wu_d, wu_b, 0, D), ("d0", wd_d, wd_b, 0, DFF), ("qkv", wqkv_d, wqkv_b, 0, D),
            ("o", wo_d, wo_b, 0, D), ("g1", wg_d, wg_b, D, 2 * D), ("u1", wu_d, wu_b, D, 2 * D), ("d1", wd_d, wd_b, DFF, 2 * DFF)]
    for key, src, dst, ra, rb in conv:
        WBK[key] = bf("wb_" + key)
        for r0 in range(ra, rb, step):
            tr.dma("pool", (lambda s_, d_, r0_: (lambda e: e.dma_start(out=d_[r0_:r0_ + step, :], in_=s_[r0_:r0_ + step, :])))(src, dst, r0),
                   "s_wc_" + key, writes=[WBK[key]])
    CB = bf("consts")
    for dst, src in ((ident[:, :], ident_d), (ones[:, :], ones_d), (gcols[:, :], gcols_d), (subg[:, :], subg_d)):
        tr.dma("sp", (lambda d_, s_: (lambda e: e.dma_start(out=d_, in_=s_[:, :])))(dst, src), "s_const", writes=[CB])
    pwtmp = arena[:, 0:2 * 4 * CG * GD].bitcast(F32).rearrange("p (g k d) -> p g k d", g=4, k=CG)
    psc_t = xt[:, 0, :]
    AR = bf("arena")
    XT = bf("xt")
    PW = bf("pw")
    tr.dma("sp", lambda e: e.dma_start(out=pwtmp, in_=poolw_d.rearrange("(g k p) d -> p g k d", g=4, k=CG)), "s_arena", writes=[AR])
    tr.dma("sp", lambda e: e.dma_start(out=psc_t, in_=pscb_d[:, :]), "s_xt", writes=[XT])
    for g in range(4):
        for k in range(CG):
            tr.op("dve", (lambda g_, k_: (lambda e: e.tensor_tensor(out=pw[:, g_, k_, :], in0=pwtmp[:, g_, k_, :], in1=psc_t[:, g_ * GD:(g_ + 1) * GD], op=ALU.mult)))(g, k),
                  reads=[AR, XT], writes=[PW])
    lamt = xt[:, 1, 0:512]
    lamp = xt[:, 2, 0:256]
    LM = bf("lamtmp")
    SM = bf("small")
    tr.dma("sp", lambda e: e.dma_start(out=lamt, in_=lamb_d[:, :]), "s_lam", writes=[LM])
    tr.op("dve", lambda e: e.tensor_tensor(out=lamp[:, 0:128], in0=lamt[:, 0:128], in1=lamt[:, 128:256], op=ALU.mult), reads=[LM], writes=[LM])
    tr.op("dve", lambda e: e.tensor_tensor(out=lamp[:, 128:256], in0=lamt[:, 256:384], in1=lamt[:, 384:512], op=ALU.mult), reads=[LM], writes=[LM])
    tr.op("dve", lambda e: e.reduce_sum(out=lamc[:, 0:1], in_=lamp[:, 0:128], axis=AX.X), reads=[LM], writes=[SM])
    tr.op("dve", lambda e: e.reduce_sum(out=lamc[:, 1:2], in_=lamp[:, 128:256], axis=AX.X), reads=[LM], writes=[SM])
    tr.op("act", lambda e: e.activation(out=lamc[:, 2:4], in_=lamc[:, 0:2], func=AF.Exp), reads=[SM], writes=[SM])
    tr.op("dve", lambda e: e.tensor_tensor(out=lamc[:, 4:5], in0=lamc[:, 2:3], in1=lamc[:, 3:4], op=ALU.subtract), reads=[SM], writes=[SM])
    tr.op("dve", lambda e: e.tensor_scalar(out=lamc[:, 5:6], in0=lamc[:, 4:5], scalar1=lam_init, scalar2=None, op0=ALU.add), reads=[SM], writes=[SM])
    lam_col = lamc[:, 5:6]
    tr.op("dve", lambda e: e.memset(lamc[:, 6:7], EPS), writes=[SM])
    eps_col = lamc[:, 6:7]
    tr.op("dve", lambda e: e.tensor_scalar(out=subg[:, :], in0=subg[:, :], scalar1=(1.0 - lam_init), scalar2=None, op0=ALU.mult), reads=[CB], writes=[CB])
    tr.barrier()

    HNS = [bf("hn%d" % i) for i in range(4)]
    CT = bf("cT")
    SL = [bf("slab%d" % i) for i in range(4)]
    WDC = [bf("wdc%d" % i) for i in range(3)]
    SG = [bf("sg%d" % i) for i in range(2)]
    slab_i = [0]
    wdc_i = [0]
    sg_i = [0]
    pbi = [0]

    def next_pb():
        i = pbi[0] % 8
        pbi[0] += 1
        return i

    def rms_stats(col0, nsub=4):
        tr.op("dve", lambda e: e.memset(small[:, col0:col0 + nsub], 0.0), writes=[SM])
        for s in range(nsub):
            if s < 2:
                tr.op("act", (lambda s_: (lambda e: e.activation(out=hn[:, s_, :], in_=xt[:, s_, :], func=AF.Square, accum_out=small[:, col0 + s_:col0 + s_ + 1])))(s),
                      reads=[XT], writes=[HNS[s], SM])
            else:
                tr.op("dve", (lambda s_: (lambda e: e.scalar_tensor_tensor(out=hn[:, s_, :], in0=xt[:, s_, :], scalar=1.0, in1=xt[:, s_, :], op0=ALU.mult, op1=ALU.mult, accum_out=small[:, col0 + s_:col0 + s_ + 1])))(s),
                      reads=[XT], writes=[HNS[s], SM])
        tr.op("act", lambda e: e.activation(out=small[:, col0:col0 + nsub], in_=small[:, col0:col0 + nsub], func=AF.Sqrt, bias=eps_col, scale=1.0 / D), reads=[SM], writes=[SM])
        tr.op("dve", lambda e: e.reciprocal(out=small[:, col0:col0 + nsub], in_=small[:, col0:col0 + nsub]), reads=[SM], writes=[SM])

    def make_hn(col0):
        for s in range(4):
            if s % 2 == 0:
                tr.op("dve", (lambda s_: (lambda e: e.tensor_scalar(out=hn[:, s_, :], in0=xt[:, s_, :], scalar1=small[:, col0 + s_:col0 + s_ + 1], scalar2=None, op0=ALU.mult)))(s),
                      reads=[XT, SM], writes=[HNS[s]])
            else:
                tr.op("act", (lambda s_: (lambda e: e.activation(out=hn[:, s_, :], in_=xt[:, s_, :], func=AF.Copy, scale=small[:, col0 + s_:col0 + s_ + 1])))(s),
                      reads=[XT, SM], writes=[HNS[s]])

    def transpose_hn(gi):
        for k in range(KC):
            b = next_pb()
            pv = pb[b][:, :].bitcast(BF16)
            for s in range(4):
                tp(pv[:, s * 128:(s + 1) * 128], hn[:, s, k * 128:(k + 1) * 128], [HNS[s], CB], PB[b], last=(s == 3))
            eng = "act" if k % 2 == 0 else "dve"
            if eng == "act":
                tr.op("act", (lambda k_, pv_: (lambda e: e.activation(out=cT[:, k_, :], in_=pv_[:, 0:512], func=AF.Copy, scale=gcols[:, gi * KC + k_:gi * KC + k_ + 1])))(k, pv),
                      reads=[PB[b], CB], writes=[CT])
            else:
                tr.op("dve", (lambda k_, pv_: (lambda e: e.tensor_scalar(out=cT[:, k_, :], in0=pv_[:, 0:512], scalar1=gcols[:, gi * KC + k_:gi * KC + k_ + 1], scalar2=None, op0=ALU.mult)))(k, pv),
                      reads=[PB[b], CB], writes=[CT])

    def load_slab(src_ap, c0, wb):
        i = slab_i[0] % 4
        slab_i[0] += 1
        tr.dma("sp", (lambda i_: (lambda e: e.dma_start(out=slabs[i_][:, :, :], in_=src_ap[:, c0:c0 + 256].rearrange("(k p) f -> p k f", p=128))))(i),
               "s_slab%d" % i, reads=[wb], writes=[SL[i]])
        return i

    AT = bf("actT")

    def ffn(layer, gi, col0):
        rms_stats(col0)
        make_hn(col0)
        transpose_hn(gi)
        wg_l = wg_b[layer * D:(layer + 1) * D, :]
        wu_l = wu_b[layer * D:(layer + 1) * D, :]
        wd_l = wd_b[layer * DFF:(layer + 1) * DFF, :]
        nfg = DFF // 256
        pend = None
        WG, WU, WD = WBK["g%d" % layer], WBK["u%d" % layer], WBK["d%d" % layer]
        nxt = (load_slab(wg_l, 0, WG), load_slab(wu_l, 0, WU))
        for fg in range(nfg):
            cur = nxt
            if fg + 1 < nfg:
                nxt = (load_slab(wg_l, (fg + 1) * 256, WG), load_slab(wu_l, (fg + 1) * 256, WU))
            for fc in range(2):
                f = fg * 2 + fc
                bg, bu = next_pb(), next_pb()
                for (si, b) in ((cur[0], bg), (cur[1], bu)):
                    for k in range(KC):
                        mm(pb[b][:, :], slabs[si][:, k, fc * 128:(fc + 1) * 128], cT[:, k, :], k == 0, k == KC - 1,
                           [SL[si], CT], PB[b], last=(k == KC - 1))
                gi_ = sg_i[0] % 2
                sg_i[0] += 1
                tr.op("act", (lambda b_, g_: (lambda e: e.activation(out=sg[g_][:, :], in_=pb[b_][:, :], func=AF.Silu)))(bg, gi_),
                      reads=[PB[bg]], writes=[SG[gi_]])
                tr.op("dve", (lambda b_, g_, f_: (lambda e: e.tensor_tensor(out=actT[:, f_, :], in0=sg[g_][:, :], in1=pb[b_][:, :], op=ALU.mult)))(bu, gi_, f),
                      reads=[SG[gi_], PB[bu]], writes=[AT])
        for p_ in range(NPASS):
            banks = [[next_pb() for _ in range(NOG)] for _ in range(4)]
            nch = FC // 2
            def ld(c):
                i = wdc_i[0] % 3
                wdc_i[0] += 1
                tr.dma("sp", (lambda i_, c_, pp_: (lambda e: e.dma_start(out=wdc[i_][:, :, :], in_=wd_l[c_ * 256:(c_ + 1) * 256, pp_ * OC:(pp_ + 1) * OC].rearrange("(a p) d -> p a d", p=128))))(i, c, p_),
                       "s_wdc%d" % i, reads=[WD], writes=[WDC[i]])
                return i
            ring = [ld(c_) for c_ in range(min(2, nch))]
            for c in range(nch):
                cu = ring.pop(0)
                if c + 2 < nch:
                    ring.append(ld(c + 2))
                for a in range(2):
                    f = c * 2 + a
                    for s in range(4):
                        for og in range(NOG):
                            b = banks[s][og]
                            mm(pb[b][:, 0:OW], actT[:, f, s * 128:(s + 1) * 128], wdc[cu][:, a, og * OW:(og + 1) * OW], f == 0, f == FC - 1,
                               [AT, WDC[cu]], PB[b], last=(f == FC - 1 or (a == 1 and s == 3 and og == NOG - 1)))
            for s in range(4):
                for og in range(NOG):
                    b = banks[s][og]
                    c0 = p_ * OC + og * OW
                    tr.op("dve", (lambda s_, b_, c0_: (lambda e: e.tensor_tensor(out=xt[:, s_, c0_:c0_ + OW], in0=xt[:, s_, c0_:c0_ + OW], in1=pb[b_][:, 0:OW], op=ALU.add)))(s, b, c0),
                          reads=[XT, PB[b]], writes=[XT])

    HL = AR
    BD = bf("band")
    HH = bf("hnh")
    KSB = [bf("ktsb%d" % i) for i in range(2)]
    VSB = [bf("vsb%d" % i) for i in range(2)]
    KTS, VS, QTS, H1S = bf("kt_s"), bf("v_s"), bf("qt_s"), bf("h1_s")
    ksb_i = [0]
    vsb_i = [0]
    def load_x_band(sl):
        tr.dma("pool", lambda e: e.dma_start(out=xt[:, :, :], in_=xt_d[sl * 512:(sl + 1) * 512, :].rearrange("(s p) c -> p s c", p=128)), "s_xt", writes=[XT])
        tr.dma("pool", lambda e: e.dma_start(out=bandm, in_=bandm_d[sl * 128:(sl + 1) * 128, :].rearrange("p (s g t) -> p s g t", s=4, g=4)), "s_band", writes=[BD])
        tr.dma("pool", lambda e: e.dma_start(out=bandh, in_=bandh_d[sl * 64:(sl + 1) * 64, :].rearrange("p (a g t) -> p a g t", a=2, g=4)), "s_band", writes=[BD])

    for slot in range(NS):
        isq = slot < NQ
        if slot == 0:
            load_x_band(0)
        tr.dma("sp", (lambda sl: (lambda e: e.dma_start(out=hl, in_=halo_d[sl * 128:(sl + 1) * 128, :].rearrange("(a p) c -> p a c", p=64))))(slot), "s_arena", writes=[AT])
        rms_stats(0)
        tr.op("dve", lambda e: e.memset(small[0:64, 4:6], 0.0), writes=[SM])
        for a in range(2):
            tr.op("act", (lambda a_: (lambda e: e.activation(out=hnh[:, a_, :], in_=hl[:, a_, :], func=AF.Square, accum_out=small[0:64, 4 + a_:5 + a_])))(a),
                  reads=[AT], writes=[HH, SM])
        tr.op("act", lambda e: e.activation(out=small[0:64, 4:6], in_=small[0:64, 4:6], func=AF.Sqrt, bias=lamc[0:64, 6:7], scale=1.0 / D), reads=[SM], writes=[SM])
        tr.op("dve", lambda e: e.reciprocal(out=small[0:64, 4:6], in_=small[0:64, 4:6]), reads=[SM], writes=[SM])
        make_hn(0)
        for a in range(2):
            tr.op("dve", (lambda a_: (lambda e: e.tensor_scalar(out=hnh[:, a_, :], in0=hl[:, a_, :], scalar1=small[0:64, 4 + a_:5 + a_], scalar2=None, op0=ALU.mult)))(a),
                  reads=[AT, SM], writes=[HH])
        for k in range(KC):
            g = k // CG
            b = next_pb()
            for s in range(4):
                a, hp = s // 2, 32 * (s % 2)
                mm(pb[b][:, s * 128:(s + 1) * 128], hn[:, s, k * 128:(k + 1) * 128], bandm[:, s, g, :], True, False, [HNS[s], BD], PB[b], last=False)
                mm(pb[b][:, s * 128:(s + 1) * 128], hnh[hp:hp + 32, a, k * 128:(k + 1) * 128], bandh[hp:hp + 32, a, g, :], False, True, [HH, BD], PB[b], last=(s == 3))
            if k % 2 == 0:
                tr.op("act", (lambda k_, b_: (lambda e: e.activation(out=cT[:, k_, :], in_=pb[b_][:, :], func=AF.Copy, scale=gcols[:, k_:k_ + 1])))(k, b),
                      reads=[PB[b], CB], writes=[CT])
            else:
                tr.op("dve", (lambda k_, b_: (lambda e: e.tensor_scalar(out=cT[:, k_, :], in0=pb[b_][:, :], scalar1=gcols[:, k_:k_ + 1], scalar2=None, op0=ALU.mult)))(k, b),
                      reads=[PB[b], CB], writes=[CT])
        for s in range(4):
            for g in range(4):
                b = next_pb()
                for kk in range(CG):
                    mm(pb[b][:, 0:GD], cT[:, g * CG + kk, s * 128:(s + 1) * 128], pw[:, g, kk, :], kk == 0, kk == CG - 1, [CT, PW], PB[b], last=(kk == CG - 1))
                tr.op("dve", (lambda s_, g_, b_: (lambda e: e.tensor_tensor(out=xt[:, s_, g_ * GD:(g_ + 1) * GD], in0=xt[:, s_, g_ * GD:(g_ + 1) * GD], in1=pb[b_][:, 0:GD], op=ALU.add)))(s, g, b),
                      reads=[XT, PB[b]], writes=[XT])
        ffn(0, 1, 8)
        if isq:
            tr.dma("pool", (lambda sl: (lambda e: e.dma_start(out=h1_s[sl * 512:(sl + 1) * 512, :].rearrange("(s p) c -> p s c", p=128), in_=xt[:, :, :])))(slot),
                   "s_h1st", reads=[XT], writes=[H1S])
        rms_stats(12)
        make_hn(12)
        if slot + 1 < NS:
            load_x_band(slot + 1)
        transpose_hn(2)
        jobs = []
        for h in range(H):
            jobs.append(("k", h))
            jobs.append(("v", h))
            if isq:
                jobs.append(("q", h))

        def jcol(job):
            kind, h = job
            return {"q": 0, "k": D, "v": 2 * D}[kind] + h * 256
        nxs = load_slab(wqkv_b, jcol(jobs[0]), WBK["qkv"])
        for ji, job in enumerate(jobs):
            cs = nxs
            if ji + 1 < len(jobs):
                nxs = load_slab(wqkv_b, jcol(jobs[ji + 1]), WBK["qkv"])
            kind, h = job
            if kind in ("k", "q"):
                for j in range(2):
                    b = next_pb()
                    for k in range(KC):
                        mm(pb[b][:, :], slabs[cs][:, k, j * 128:(j + 1) * 128], cT[:, k, :], k == 0, k == KC - 1, [SL[cs], CT], PB[b], last=(k == KC - 1))
                    ki = ksb_i[0] % 2
                    ksb_i[0] += 1
                    tr.op("act", (lambda b_, ki_: (lambda e: e.activation(out=ktsb[ki_], in_=pb[b_][:, :], func=AF.Copy)))(b, ki), reads=[PB[b]], writes=[KSB[ki]])
                    dst, DB, ncol = (kt_s, KTS, NS * 512) if kind == "k" else (qt_s, QTS, NQ * 512)
                    r0 = (h * 2 + j) * 128
                    tr.dma("pool", (lambda dst_, r0_, sl, ki_: (lambda e: e.dma_start(out=dst_[r0_:r0_ + 128, sl * 512:(sl + 1) * 512], in_=ktsb[ki_])))(dst, r0, slot, ki),
                           "s_kst%d" % ki, reads=[KSB[ki]], writes=[DB])
            else:
                vi = vsb_i[0] % 2
                vsb_i[0] += 1
                for s in range(4):
                    b = next_pb()
                    for k in range(KC):
                        mm(pb[b][:, 0:256], cT[:, k, s * 128:(s + 1) * 128], slabs[cs][:, k, :], k == 0, k == KC - 1, [CT, SL[cs]], PB[b], last=(k == KC - 1))
                    tr.op("dve", (lambda b_, vi_, s_: (lambda e: e.tensor_copy(out=vsb[vi_][:, s_, :], in_=pb[b_][:, 0:256])))(b, vi, s), reads=[PB[b]], writes=[VSB[vi]])
                tr.dma("pool", (lambda sl, h_, vi_: (lambda e: e.dma_start(out=v_s[sl * 512:(sl + 1) * 512, h_ * 256:(h_ + 1) * 256].rearrange("(s p) e -> p s e", p=128), in_=vsb[vi_])))(slot, h, vi),
                       "s_vst%d" % vi, reads=[VSB[vi]], writes=[VS])
    tr.barrier()

    QC = bf("qconst")
    for dst, src in ((base0, base0_d), (w0, w0_d), (gfin, gfin_d)):
        tr.dma("sp", (lambda d_, s_: (lambda e: e.dma_start(out=d_, in_=s_[:, :])))(dst, src), "s_const", writes=[QC])
    TB = bf("tabs")
    ROW = bf("rows")
    QTB = [bf("QT%d" % i) for i in range(2)]
    KTB = [bf("KTc%d" % i) for i in range(3)]
    VCB = [bf("Vc%d" % i) for i in range(3)]
    SPB = [bf("Sp%d" % i) for i in range(2)]
    PTB = [[bf("PT%d_%d" % (j, i)) for i in range(2)] for j in range(2)]
    RB = [bf("R%d" % i) for i in range(2)]
    T1B = [bf("t1_%d" % i) for i in range(2)]
    OFB = [bf("of%d" % i) for i in range(2)]
    SQB = [bf("sq%d" % i) for i in range(2)]
    RSB = bf("rs")
    qt_i = [0]
    kv_i = [0]
    pt_i = [0]
    slopes = [2.0 ** (-8.0 * (h + 1) / H) for h in range(H)]
    SKIP_TH = _SKIP_TH
    dmin = _min_dist(cfg)
    for qi in range(NQ):
        tr.barrier(dma=(qi == 0))
        tr.dma("pool", (lambda q_: (lambda e: e.dma_start(out=xt[:, :, :], in_=h1_s[q_ * 512:(q_ + 1) * 512, :].rearrange("(s p) c -> p s c", p=128))))(qi), "s_xt", reads=[H1S], writes=[XT])
        tr.dma("sp", (lambda q_: (lambda e: e.dma_start(out=tabs, in_=tabs_d[q_ * 128:(q_ + 1) * 128, :].rearrange("p (a k) -> p a k", a=3))))(qi), "s_tabs", writes=[TB])
        def ldkv(h, c):
            i = kv_i[0] % 3
            kv_i[0] += 1
            nb = min(4, NKB - 4 * c)
            tr.dma("sp", (lambda h_, c_, i_, nb_: (lambda e: e.dma_start(out=KTc[i_][:, :, 0:nb_ * 128], in_=kt_s[h_ * 256:(h_ + 1) * 256, c_ * 512:c_ * 512 + nb_ * 128].rearrange("(j p) t -> p j t", p=128))))(h, c, i, nb),
                   "s_kt%d" % i, reads=[KTS], writes=[KTB[i]])
            tr.dma("sp", (lambda h_, c_, i_, nb_: (lambda e: e.dma_start(out=Vc[i_][:, 0:nb_, :], in_=v_s[c_ * 512:c_ * 512 + nb_ * 128, h_ * 256:(h_ + 1) * 256].rearrange("(b p) e -> p b e", p=128))))(h, c, i, nb),
                   "s_vc%d" % i, reads=[VS], writes=[VCB[i]])
            return i

        def head_setup(h):
            nchunk = (NKB + 3) // 4
            need = [kb for kb in range(NKB) if slopes[h] * dmin[qi, kb] < SKIP_TH]
            plan = []
            for c in range(nchunk):
                bis = [kb - 4 * c for kb in need if kb // 4 == c]
                if bis:
                    plan.append((c, bis))
            qb_ = qt_i[0] % 2
            qt_i[0] += 1
            tr.dma("sp", (lambda h_, q_, b_: (lambda e: e.dma_start(out=QT[b_], in_=qt_s[h_ * 256:(h_ + 1) * 256, q_ * 512:(q_ + 1) * 512].rearrange("(j p) t -> p j t", p=128))))(h, qi, qb_),
                   "s_qt%d" % qb_, reads=[QTS], writes=[QTB[qb_]])
            ring = [ldkv(h, plan[i_][0]) for i_ in range(min(2, len(plan)))]
            return dict(plan=plan, qb=qb_, ring=ring, last_need=need[-1])

        hs_next = head_setup(0)
        for h in range(H):
            fh = -slopes[h] / scale
            tr.op("dve", (lambda h_: (lambda e: e.scalar_tensor_tensor(out=biasrow, in0=tabs[:, 1, :], scalar=-slopes[h_], in1=tabs[:, 2, :], op0=ALU.mult, op1=ALU.add)))(h), reads=[TB], writes=[ROW])
            tr.op("dve", (lambda f_: (lambda e: e.tensor_scalar(out=sgnrow, in0=tabs[:, 0, :], scalar1=f_, scalar2=None, op0=ALU.mult)))(fh), reads=[TB], writes=[ROW])
            hs = hs_next
            plan, qb_, ring, last_need = hs["plan"], hs["qb"], hs["ring"], hs["last_need"]
            first = True
            pend_pv = [None]
            for pi_, (c, bis) in enumerate(plan):
                ckv = ring.pop(0)
                for bn, bi in enumerate(bis):
                    kb = 4 * c + bi
                    lastkb = (kb == last_need)
                    diag = (kb // 4 == qi) and kb < 4 * NT
                    pts = []
                    for j in range(2):
                        mm(pb[j][:, :], KTc[ckv][:, j, bi * 128:(bi + 1) * 128], QT[qb_][:, j, :], True, True, [KTB[ckv], QTB[qb_]], PB[j], last=(j == 1))
                    for j in range(2):
                        if diag:
                            o_ = kb % 4
                            tr.op("dve", (lambda j_, o__, f_: (lambda e: e.scalar_tensor_tensor(out=Sp[j_], in0=w0[:, 384 - 128 * o__:384 - 128 * o__ + 512], scalar=f_, in1=pb[j_][:, :], op0=ALU.mult, op1=ALU.add)))(j, o_, fh),
                                  reads=[QC, PB[j]], writes=[SPB[j]])
                        else:
                            tr.op("dve", (lambda j_, kb_: (lambda e: e.scalar_tensor_tensor(out=Sp[j_], in0=base0, scalar=sgnrow[:, kb_:kb_ + 1], in1=pb[j_][:, :], op0=ALU.mult, op1=ALU.add)))(j, kb),
                                  reads=[QC, ROW, PB[j]], writes=[SPB[j]])
                        pi = pt_i[0] % 2
                        tr.op("act", (lambda j_, kb_, pi_: (lambda e: e.activation(out=PT[j_][pi_], in_=Sp[j_], func=AF.Exp, bias=biasrow[:, kb_:kb_ + 1], scale=scale)))(j, kb, pi),
                              reads=[SPB[j], ROW], writes=[PTB[j][pi]])
                        pts.append(pi)
                    pt_i[0] += 1

                    def pv(pts=pts, ckv=ckv, bi=bi, first=first, lastkb=lastkb):
                        for j in range(2):
                            pi = pts[j]
                            for ec in range(2):
                                b = 2 + 2 * j + ec
                                mm(pb[b][:, :], Vc[ckv][:, bi, ec * 128:(ec + 1) * 128], PT[j][pi], first, lastkb, [VCB[ckv], PTB[j][pi]], PB[b], last=False)
                            mm(pb[6 + j][:, :], ones[:, :], PT[j][pi], first, lastkb, [CB, PTB[j][pi]], PB[6 + j], last=(j == 1))
                    if pend_pv[0] is not None:
                        pend_pv[0]()
                    pend_pv[0] = pv
                    first = False
                    if bn == 0 and pi_ + 2 < len(plan):
                        ring.append(ldkv(h, plan[pi_ + 2][0]))
            pend_pv[0]()
            pend_pv[0] = None
            if h + 1 < H:
                hs_next = head_setup(h + 1)
            for j in range(2):
                tr.op("dve", (lambda j_: (lambda e: e.reciprocal(out=Rr[j_], in_=pb[6 + j_][:, :])))(j), reads=[PB[6 + j]], writes=[RB[j]])
            for ec in range(2):
                tr.op("dve", (lambda ec_: (lambda e: e.scalar_tensor_tensor(out=t1s[ec_], in0=pb[4 + ec_][:, :], scalar=lam_col, in1=Rr[1], op0=ALU.mult, op1=ALU.mult)))(ec), reads=[PB[4 + ec], RB[1], SM], writes=[T1B[ec]])
            for ec in range(2):
                tr.op("dve", (lambda ec_: (lambda e: e.tensor_tensor(out=of[:, ec_, :], in0=pb[2 + ec_][:, :], in1=Rr[0], op=ALU.mult)))(ec), reads=[PB[2 + ec], RB[0]], writes=[OFB[ec]])
            for ec in range(2):
                tr.op("dve", (lambda ec_: (lambda e: e.tensor_tensor(out=of[:, ec_, :], in0=of[:, ec_, :], in1=t1s[ec_], op=ALU.subtract)))(ec), reads=[T1B[ec]], writes=[OFB[ec]])
            for ec in range(2):
                tr.op("act", (lambda ec_: (lambda e: e.activation(out=sq[:, ec_, :], in_=of[:, ec_, :], func=AF.Square)))(ec), reads=[OFB[ec]], writes=[SQB[ec]])
            for ec in range(2):
                mm(pb[0][:, :], ones[:, :], sq[:, ec, :], ec == 0, ec == 1, [CB, SQB[ec]], PB[0], last=(ec == 1))
            tr.op("act", lambda e: e.activation(out=rs, in_=pb[0][:, :], func=AF.Sqrt, bias=eps_col, scale=1.0 / 256.0), reads=[PB[0], SM], writes=[RSB])
            tr.op("dve", lambda e: e.reciprocal(out=rs, in_=rs), reads=[RSB], writes=[RSB])
            for ec in range(2):
                tr.op("dve", (lambda ec_, h_: (lambda e: e.scalar_tensor_tensor(out=cT[:, 2 * h_ + ec_, :], in0=of[:, ec_, :], scalar=subg[:, ec_:ec_ + 1], in1=rs, op0=ALU.mult, op1=ALU.mult)))(ec, h),
                      reads=[OFB[ec], RSB, CB], writes=[CT])
        tr.barrier()
        for p_ in range(NPASS):
            banks = [[next_pb() for _ in range(NOG)] for _ in range(4)]
            nch = KC // 2

            def ldo(c):
                i = wdc_i[0] % 3
                wdc_i[0] += 1
                tr.dma("sp", (lambda i_, c_, pp_: (lambda e: e.dma_start(out=wdc[i_][:, :, :], in_=wo_b[c_ * 256:(c_ + 1) * 256, pp_ * OC:(pp_ + 1) * OC].rearrange("(a p) d -> p a d", p=128))))(i, c, p_),
                       "s_wdc%d" % i, reads=[WBK["o"]], writes=[WDC[i]])
                return i
            ring = [ldo(c_) for c_ in range(min(2, nch))]
            for c in range(nch):
                cu = ring.pop(0)
                if c + 2 < nch:
                    ring.append(ldo(c + 2))
                for a in range(2):
                    k = c * 2 + a
                    for s in range(4):
                        for og in range(NOG):
                            b = banks[s][og]
                            mm(pb[b][:, 0:OW], cT[:, k, s * 128:(s + 1) * 128], wdc[cu][:, a, og * OW:(og + 1) * OW], k == 0, k == KC - 1,
                               [CT, WDC[cu]], PB[b], last=(k == KC - 1 or (a == 1 and s == 3 and og == NOG - 1)))
            for s in range(4):
                for og in range(NOG):
                    b = banks[s][og]
                    c0 = p_ * OC + og * OW
                    tr.op("dve", (lambda s_, b_, c0_: (lambda e: e.tensor_tensor(out=xt[:, s_, c0_:c0_ + OW], in0=xt[:, s_, c0_:c0_ + OW], in1=pb[b_][:, 0:OW], op=ALU.add)))(s, b, c0),
                          reads=[XT, PB[b]], writes=[XT])
        ffn(1, 3, 16)
        rms_stats(20)
        for s in range(4):
            tr.op("dve", (lambda s_: (lambda e: e.scalar_tensor_tensor(out=xt[:, s_, :], in0=xt[:, s_, :], scalar=small[:, 20 + s_:21 + s_], in1=gfin, op0=ALU.mult, op1=ALU.mult)))(s),
                  reads=[XT, SM, QC], writes=[XT])
        tr.dma("pool", (lambda q_: (lambda e: e.dma_start(out=y_d[q_ * 512:(q_ + 1) * 512, :].rearrange("(s p) c -> p s c", p=128), in_=xt[:, :, :])))(qi), "s_yst", reads=[XT], writes=[bf("y")])
    tr.barrier()

    sems = {}
    for k in sorted(tr.semkeys):
        sems[k] = es.enter_context(nc.semaphore(k))
    with nc.Block() as block:
        @block.tensor
        def _(e):
            tr.replay("pe", e, sems)

        @block.scalar
        def _(e):
            tr.replay("act", e, sems)

        @block.vector
        def _(e):
            tr.replay("dve", e, sems)

        @block.gpsimd
        def _(e):
            tr.replay("pool", e, sems)

        @block.sync
        def _(e):
            tr.replay("sp", e, sems)
    es.close()
    return nc


_band_cache = {}


def _band_for(pos_main, pos_halo, L):
    key = (tuple(pos_main.tolist()), tuple(pos_halo.tolist()), L)
    p0 = int(pos_main[0])
    rel = (tuple((pos_main - p0).tolist()) if p0 >= 0 else None, tuple(np.where(pos_halo >= 0, pos_halo - p0, -999).tolist()), min(p0, 40), min(L - p0, 300) if p0 >= 0 else -1)
    if rel in _band_cache:
        return _band_cache[rel]
    bm = np.zeros((128, 4, 128), np.float32)
    bh = np.zeros((32, 4, 128), np.float32)
    where = {}
    for i, p in enumerate(pos_main):
        if p >= 0:
            where[int(p)] = ("m", i)
    for i, p in enumerate(pos_halo):
        if p >= 0 and int(p) not in where:
            where[int(p)] = ("h", i)
    for r, p in enumerate(pos_main):
        if p < 0:
            continue
        p = int(p)
        for g, w in enumerate(WINDOWS):
            lo, hi = max(p - w // 2, 0), min(p + w // 2, L)
            inv = 1.0 / float(hi - lo)
            for u in range(lo, hi):
                kind, i = where[u]
                if kind == "m":
                    bm[i, g, r] += inv
                else:
                    bh[i, g, r] += inv
            bm[r, g, r] -= 1.0
    _band_cache[rel] = (bm, bh)
    return bm, bh


def _slot_positions(cfg, ntile, ctype):
    NT, NQ, NS = cfg.NT, cfg.NQ, cfg.NS
    half = ntile // 2
    tmap = [-1] * NT
    for i in range(half):
        tmap[i] = (ctype * half + i)
        tmap[NQ + i] = ((1 - ctype) * half + i)
    pos = -np.ones((NS, 512), np.int64)
    for sl in range(NT):
        if tmap[sl] >= 0:
            pos[sl] = N_META + 512 * tmap[sl] + np.arange(512)
    pos[NT, :N_META] = np.arange(N_META)
    return tmap, pos


def _min_dist(cfg):
    NT, NQ, NKB = cfg.NT, cfg.NQ, cfg.NKB
    dmin = np.full((NQ, NKB), np.inf)
    for ntile in (NT, NT // 2):
        for ctype in (0, 1):
            _, pos = _slot_positions(cfg, ntile, ctype)
            for qi in range(NQ):
                if pos[qi, 0] < 0:
                    continue
                q0, q1 = int(pos[qi, 0]), int(pos[qi, 0]) + 511
                for kb in range(NKB):
                    if kb == NKB - 1:
                        k0, k1 = 0, N_META - 1
                    else:
                        k0 = int(pos[kb // 4, (kb % 4) * 128])
                        if k0 < 0:
                            continue
                        k1 = k0 + 127
                    d = max(0, k0 - q1, q0 - k1)
                    dmin[qi, kb] = min(dmin[qi, kb], d)
    return dmin


def _core_layout(cfg, x_seq, meta, ctype):
    D, NT, NQ, NS, NKB = cfg.D, cfg.NT, cfg.NQ, cfg.NS, cfg.NKB
    S = x_seq.shape[0]
    L = S + N_META
    ntile = S // 512
    tmap, pos = _slot_positions(cfg, ntile, ctype)
    xt = np.zeros((NS * 512, D), np.float32)
    for sl in range(NT):
        if tmap[sl] >= 0:
            t = tmap[sl]
            xt[sl * 512:(sl + 1) * 512] = x_seq[512 * t:512 * (t + 1)]
    xt[NT * 512:NT * 512 + N_META] = meta

    def row_of(p):
        return meta[p] if p < N_META else x_seq[p - N_META]
    halo = np.zeros((NS, 2, 64, D), np.float32)
    bandm = np.zeros((NS, 128, 4, 4, 128), np.float32)
    bandh = np.zeros((NS, 64, 2, 4, 128), np.float32)
    for sl in range(NS):
        for s in range(4):
            pm = pos[sl, s * 128:(s + 1) * 128]
            ph = -np.ones(32, np.int64)
            if pm[0] >= 0:
                nvalid = int((pm >= 0).sum())
                p0, p1 = int(pm[0]), int(pm[0]) + nvalid
                cand = list(range(p0 - 8, p0)) + list(range(p1, p1 + 8))
                for i, p in enumerate(cand):
                    if 0 <= p < L:
                        ph[i] = p
                        halo[sl, s // 2, 32 * (s % 2) + i] = row_of(p)
                bm, bh = _band_for(pm, ph, L)
                bandm[sl, :, s] = bm
                bandh[sl, 32 * (s % 2):32 * (s % 2) + 32, s // 2] = bh
    tabs = np.zeros((NQ, 128, 3, NKB), np.float32)
    tabs[:, :, 0, :] = 1.0
    for qi in range(NQ):
        if pos[qi, 0] < 0:
            continue
        pq0 = int(pos[qi, 0])
        for kb in range(NKB):
            if kb == NKB - 1:
                tabs[qi, :, 0, kb] = 1.0
                tabs[qi, :, 1, kb] = pq0
                tabs[qi, N_META:, 2, kb] = NEG
                continue
            sl, sub = kb // 4, kb % 4
            if sl == qi:
                continue
            pk0 = int(pos[sl, sub * 128])
            if pk0 < 0:
                tabs[qi, :, 2, kb] = NEG
                continue
            dlt = pq0 - pk0
            tabs[qi, :, 0, kb] = 1.0 if dlt > 0 else -1.0
            tabs[qi, :, 1, kb] = abs(dlt)
    valid_q = [tmap[i] for i in range(NQ)]
    return dict(xt=xt, halo=halo.reshape(NS * 128, D),
                bandm=bandm.reshape(NS * 128, 2048).astype(NPBF),
                bandh=bandh.reshape(NS * 64, 1024).astype(NPBF),
                tabs=tabs.reshape(NQ * 128, 3 * NKB)), valid_q


def _shared_inputs(cfg, inp):
    D, KC = cfg.D, cfg.KC
    f = np.float32
    kk = np.arange(128)[:, None]
    base0 = (np.arange(512)[None, :] - kk).astype(f)
    w0 = np.abs(np.arange(896)[None, :] - 384 - kk).astype(f)
    gl = [inp["mixer_norm_g"][0], inp["ffn_norm_g"][0], inp["mixer_norm_g"][1], inp["ffn_norm_g"][1]]
    gcols = np.concatenate([np.asarray(g, f).reshape(KC, 128).T for g in gl], axis=1)
    lamb = np.concatenate([np.asarray(inp[k], f).reshape(1, 128) for k in ("lambda_q1", "lambda_k1", "lambda_q2", "lambda_k2")], axis=1)
    return dict(
        base0=np.ascontiguousarray(base0), w0=np.ascontiguousarray(w0),
        ident=np.eye(128, dtype=f).astype(NPBF), ones=np.ones((128, 128), f).astype(NPBF),
        gcols=np.ascontiguousarray(gcols),
        gfin=np.ascontiguousarray(np.broadcast_to(np.asarray(inp["final_norm_g"], f).reshape(1, D), (128, D))),
        pscb=np.ascontiguousarray(np.broadcast_to(np.asarray(inp["pool_scale"], f).reshape(1, D), (128, D))),
        subg=np.ascontiguousarray(np.asarray(inp["subln_g"], f).reshape(2, 128).T),
        lamb=np.ascontiguousarray(np.broadcast_to(lamb, (128, 512))),
        pool_w=np.ascontiguousarray(np.asarray(inp["pool_w"], f).reshape(4 * cfg.GD, cfg.GD)),
        w_qkv=np.ascontiguousarray(np.asarray(inp["w_qkv"], f).reshape(D, 3 * D)),
        w_o=np.ascontiguousarray(np.asarray(inp["w_o"], f).reshape(D, D)),
        w_gate=np.ascontiguousarray(np.asarray(inp["w_gate"], f).reshape(2 * D, cfg.DFF)),
        w_up=np.ascontiguousarray(np.asarray(inp["w_up"], f).reshape(2 * D, cfg.DFF)),
        w_down=np.ascontiguousarray(np.asarray(inp["w_down"], f).reshape(2 * cfg.DFF, D)),
    )


def run(cfg, inp):
    xp = np.asarray(inp["x_prompt"], np.float32)
    xs = np.asarray(inp["x_sample"], np.float32)
    meta = np.asarray(inp["meta_tokens"], np.float32)
    shared = _shared_inputs(cfg, inp)
    seqs = [xs[0], xs[1], xp[0], xp[1]]
    in_maps, vq = [], []
    for c in range(8):
        lay, valid_q = _core_layout(cfg, seqs[c // 2], meta, c % 2)
        m = dict(shared)
        m.update(lay)
        in_maps.append(m)
        vq.append(valid_q)
    nc = build(cfg)
    res = run_bass_kernel_spmd(nc, in_maps, core_ids=list(range(8)))
    yp = np.zeros_like(xp)
    ys = np.zeros_like(xs)
    outs = [ys[0], ys[1], yp[0], yp[1]]
    for c in range(8):
        y = np.asarray(res.results[c]["y"], dtype=np.float32)
        for qi, t in enumerate(vq[c]):
            if t >= 0:
                outs[c // 2][512 * t:512 * (t + 1)] = y[512 * qi:512 * (qi + 1)]
    return yp, ys


def kernel(**inputs):
    cfg = Cfg()
    return run(cfg, inputs)
```
